# Optimizing a Trainium2 kernel written in Bass

```python
import jax, jax.numpy as jnp
from jax import lax
import numpy as np

D_MODEL = 1024
BATCH = 8
SEQ = 4096
DEPTH = 1

HEAD_DIM = 64
RWKV_HEADS = 8
RWKV_WIDTH = RWKV_HEADS * HEAD_DIM
ATT_Q_HEADS = 8
ATT_KV_HEADS = 2
ATT_GROUP = ATT_Q_HEADS // ATT_KV_HEADS
ATT_WIDTH = ATT_Q_HEADS * HEAD_DIM
KV_WIDTH = ATT_KV_HEADS * HEAD_DIM
WINDOW = 128
BLOCK = 128
DECAY_LORA = 32
ICLR_LORA = 32
GATE_LORA = 96
D_FF = 2816
N_BRANCH = 2
RMS_EPS = 1e-6
GN_EPS = 64e-5

RWKV_COLS = 3 * RWKV_WIDTH + DECAY_LORA + ICLR_LORA + GATE_LORA
ATT_COLS = ATT_WIDTH + 2 * KV_WIDTH
GATE_COLS = N_BRANCH * D_MODEL
IN_COLS = RWKV_COLS + ATT_COLS + GATE_COLS

kernel_name = "rwkv7_swa_sink_macaron_hybrid"


def rms_norm(x, g, eps=RMS_EPS):
    xf = x.astype(jnp.float32)
    y = xf * lax.rsqrt(jnp.mean(xf * xf, axis=-1, keepdims=True) + eps)
    return (y * g.astype(jnp.float32)).astype(x.dtype)


def swiglu(h, w_gate, w_up, w_down):
    return (jax.nn.silu(h @ w_gate) * (h @ w_up)) @ w_down


def token_shift(p):
    return jnp.pad(p, ((0, 0), (1, 0), (0, 0)))[:, :-1]


def wkv7_scan(r, w, k, v, a, b):
    Bsz, T, H, N = r.shape
    xs = tuple(jnp.moveaxis(t, 1, 0) for t in (r, w, k, v, a, b))

    def step(S, inp):
        r_t, w_t, k_t, v_t, a_t, b_t = inp
        sa = jnp.einsum('bhij,bhj->bhi', S, a_t)
        S = S * w_t[:, :, None, :] + sa[..., None] * b_t[:, :, None, :] + v_t[..., None] * k_t[:, :, None, :]
        y_t = jnp.einsum('bhij,bhj->bhi', S, r_t)
        return S, y_t

    S0 = jnp.zeros((Bsz, H, N, N), jnp.float32)
    _, ys = lax.scan(step, S0, xs)
    return jnp.moveaxis(ys, 0, 1)


def rwkv7_branch(p, mu, w0, w_lora_up, a0, a_lora_up, g_lora_up, k_k, k_a, r_k, ln_w, ln_b):
    Bsz, T, _ = p.shape
    f32 = jnp.float32
    p = p + (token_shift(p) - p) * mu
    r, k, v, xw, xa, xg = jnp.split(
        p, [RWKV_WIDTH, 2 * RWKV_WIDTH, 3 * RWKV_WIDTH,
            3 * RWKV_WIDTH + DECAY_LORA, 3 * RWKV_WIDTH + DECAY_LORA + ICLR_LORA], axis=-1)
    w_log = -jax.nn.softplus(-(w0 + jnp.tanh(xw) @ w_lora_up)) - 0.5
    decay = jnp.exp(-jnp.exp(w_log.astype(f32)))
    a = jax.nn.sigmoid(a0 + xa @ a_lora_up)
    g = jax.nn.sigmoid(xg) @ g_lora_up

    def heads(t):
        return t.reshape(Bsz, T, RWKV_HEADS, HEAD_DIM).astype(f32)

    r, k, v, decay, a = heads(r), heads(k), heads(v), heads(decay), heads(a)
    k_k = k_k.reshape(RWKV_HEADS, HEAD_DIM).astype(f32)
    k_a = k_a.reshape(RWKV_HEADS, HEAD_DIM).astype(f32)
    kk = k * k_k
    kk = kk / jnp.maximum(jnp.sqrt(jnp.sum(kk * kk, axis=-1, keepdims=True)), 1e-12)
    k = k * (1.0 + (a - 1.0) * k_a)
    y = wkv7_scan(r, decay, k, v, -kk, kk * a)
    mean = jnp.mean(y, axis=-1, keepdims=True)
    var = jnp.mean(jnp.square(y - mean), axis=-1, keepdims=True)
    y = (y - mean) * lax.rsqrt(var + GN_EPS)
    y = y * ln_w.reshape(RWKV_HEADS, HEAD_DIM).astype(f32) + ln_b.reshape(RWKV_HEADS, HEAD_DIM).astype(f32)
    bonus = jnp.sum(r * k * r_k.astype(f32), axis=-1, keepdims=True) * v
    y = (y + bonus).reshape(Bsz, T, RWKV_WIDTH) * g.astype(f32)
    return y.astype(p.dtype)


def sliding_window_attention(q, k, v, sinks):
    Bsz, T = q.shape[:2]
    nb = T // BLOCK
    f32 = jnp.float32
    qb = q.reshape(Bsz, nb, BLOCK, ATT_KV_HEADS, ATT_GROUP, HEAD_DIM)

    def with_prev(t):
        tb = t.reshape(Bsz, nb, BLOCK, ATT_KV_HEADS, HEAD_DIM)
        prev = jnp.pad(tb, ((0, 0), (1, 0), (0, 0), (0, 0), (0, 0)))[:, :-1]
        return jnp.concatenate([prev, tb], axis=2)

    kc, vc = with_prev(k), with_prev(v)
    scale = HEAD_DIM ** -0.5
    s = jnp.einsum('bnqhgd,bnkhd->bnhgqk', qb, kc).astype(f32) * scale
    qi = jnp.arange(BLOCK)[:, None]
    kj = jnp.arange(2 * BLOCK)[None, :]
    band = (kj <= qi + BLOCK) & (kj > qi + BLOCK - WINDOW)
    valid = (jnp.arange(nb)[:, None, None] > 0) | (kj >= BLOCK)[None]
    mask = (band[None] & valid)[None, :, None, None]
    s = jnp.where(mask, s, -jnp.inf)
    sink = sinks.astype(f32).reshape(1, 1, ATT_KV_HEADS, ATT_GROUP, 1, 1)
    m = jnp.maximum(jnp.max(s, axis=-1, keepdims=True), sink)
    pexp = jnp.exp(s - m)
    denom = jnp.sum(pexp, axis=-1, keepdims=True) + jnp.exp(sink - m)
    probs = (pexp / denom).astype(v.dtype)
    o = jnp.einsum('bnhgqk,bnkhd->bnqhgd', probs, vc)
    return o.reshape(Bsz, T, ATT_WIDTH)


def attention_branch(p, q_norm, k_norm, sinks):
    Bsz, T, _ = p.shape
    q, k, v = jnp.split(p, [ATT_WIDTH, ATT_WIDTH + KV_WIDTH], axis=-1)
    q = rms_norm(q.reshape(Bsz, T, ATT_Q_HEADS, HEAD_DIM), q_norm)
    k = rms_norm(k.reshape(Bsz, T, ATT_KV_HEADS, HEAD_DIM), k_norm)
    v = v.reshape(Bsz, T, ATT_KV_HEADS, HEAD_DIM)
    return sliding_window_attention(q, k, v, sinks)


def setup_inputs(seed: int = 0) -> dict:
    key = jax.random.key(seed)
    ks = iter(jax.random.split(key, 40))
    L = DEPTH

    def nrm(shape, scale):
        return jax.random.normal(next(ks), shape, jnp.float32) * scale

    def gain(shape):
        return 1.0 + nrm(shape, 0.02)

    return {
        "x": jax.random.normal(next(ks), (BATCH, SEQ, D_MODEL), jnp.float32),
        "ffn1_norm": gain((L, D_MODEL)),
        "ffn1_w_gate": nrm((L, D_MODEL, D_FF), D_MODEL ** -0.5),
        "ffn1_w_up": nrm((L, D_MODEL, D_FF), D_MODEL ** -0.5),
        "ffn1_w_down": nrm((L, D_FF, D_MODEL), D_FF ** -0.5),
        "mix_norm": gain((L, D_MODEL)),
        "w_in": nrm((L, D_MODEL, IN_COLS), D_MODEL ** -0.5),
        "rwkv_mu": jax.random.uniform(next(ks), (L, RWKV_COLS), jnp.float32),
        "rwkv_w0": jax.random.uniform(next(ks), (L, RWKV_WIDTH), jnp.float32, -6.5, -1.5),
        "rwkv_w_lora_up": nrm((L, DECAY_LORA, RWKV_WIDTH), 0.1),
        "rwkv_a0": nrm((L, RWKV_WIDTH), 0.1),
        "rwkv_a_lora_up": nrm((L, ICLR_LORA, RWKV_WIDTH), 0.1),
        "rwkv_g_lora_up": nrm((L, GATE_LORA, RWKV_WIDTH), GATE_LORA ** -0.5),
        "rwkv_k_k": 0.85 + nrm((L, RWKV_WIDTH), 0.02),
        "rwkv_k_a": gain((L, RWKV_WIDTH)),
        "rwkv_r_k": nrm((L, RWKV_HEADS, HEAD_DIM), 0.1),
        "rwkv_ln_w": gain((L, RWKV_WIDTH)),
        "rwkv_ln_b": nrm((L, RWKV_WIDTH), 0.02),
        "attn_q_norm": gain((L, HEAD_DIM)),
        "attn_k_norm": gain((L, HEAD_DIM)),
        "attn_sinks": nrm((L, ATT_Q_HEADS), 0.5),
        "w_branch_rwkv": nrm((L, RWKV_WIDTH, D_MODEL), RWKV_WIDTH ** -0.5),
        "w_branch_attn": nrm((L, ATT_WIDTH, D_MODEL), ATT_WIDTH ** -0.5),
        "w_out": nrm((L, D_MODEL, D_MODEL), D_MODEL ** -0.5),
        "ffn2_norm": gain((L, D_MODEL)),
        "ffn2_w_gate": nrm((L, D_MODEL, D_FF), D_MODEL ** -0.5),
        "ffn2_w_up": nrm((L, D_MODEL, D_FF), D_MODEL ** -0.5),
        "ffn2_w_down": nrm((L, D_FF, D_MODEL), D_FF ** -0.5),
        "final_norm": gain((L, D_MODEL)),
    }


def reference(x, ffn1_norm, ffn1_w_gate, ffn1_w_up, ffn1_w_down, mix_norm, w_in,
              rwkv_mu, rwkv_w0, rwkv_w_lora_up, rwkv_a0, rwkv_a_lora_up, rwkv_g_lora_up,
              rwkv_k_k, rwkv_k_a, rwkv_r_k, rwkv_ln_w, rwkv_ln_b,
              attn_q_norm, attn_k_norm, attn_sinks,
              w_branch_rwkv, w_branch_attn, w_out,
              ffn2_norm, ffn2_w_gate, ffn2_w_up, ffn2_w_down, final_norm):
    for l in range(DEPTH):
        x = x + 0.5 * swiglu(rms_norm(x, ffn1_norm[l]), ffn1_w_gate[l], ffn1_w_up[l], ffn1_w_down[l])
        h = rms_norm(x, mix_norm[l])
        proj = h @ w_in[l]
        p_rwkv, p_att, p_gate = jnp.split(proj, [RWKV_COLS, RWKV_COLS + ATT_COLS], axis=-1)
        y_rwkv = rwkv7_branch(p_rwkv, rwkv_mu[l], rwkv_w0[l], rwkv_w_lora_up[l], rwkv_a0[l],
                              rwkv_a_lora_up[l], rwkv_g_lora_up[l], rwkv_k_k[l], rwkv_k_a[l],
                              rwkv_r_k[l], rwkv_ln_w[l], rwkv_ln_b[l])
        y_att = attention_branch(p_att, attn_q_norm[l], attn_k_norm[l], attn_sinks[l])
        gate_rwkv, gate_att = jnp.split(jax.nn.sigmoid(p_gate), N_BRANCH, axis=-1)
        merged = gate_rwkv * (y_rwkv @ w_branch_rwkv[l]) + gate_att * (y_att @ w_branch_attn[l])
        x = x + merged @ w_out[l]
        x = x + 0.5 * swiglu(rms_norm(x, ffn2_norm[l]), ffn2_w_gate[l], ffn2_w_up[l], ffn2_w_down[l])
        x = rms_norm(x, final_norm[l])
    return x
```

```python
import math
from contextlib import ExitStack

import numpy as np
import concourse.bass as bass
import concourse.mybir as mybir
from concourse.bass_utils import run_bass_kernel_spmd

F32 = mybir.dt.float32
BF16 = mybir.dt.bfloat16
AF = mybir.ActivationFunctionType
ALU = mybir.AluOpType
AX = mybir.AxisListType

D = 1024
DFF = 2816
NF = 22
TT = 512
RW = 512
C0 = math.exp(-0.5)
RMS_EPS = 1e-6
GN_EPS = 64e-5
RING = 4
SLOT = 2048
VW = 80


class Buf:
    __slots__ = ("name", "last_w", "readers", "sem", "dma_cnt")

    def __init__(self, name):
        self.name = name
        self.last_w = None
        self.readers = []
        self.sem = None
        self.dma_cnt = 0


class Op:
    __slots__ = ("eng", "fn", "reads", "writes", "dma", "track", "deps", "signal", "tok", "barrier", "phase")

    def __init__(self, eng, fn, reads, writes, dma, track):
        self.eng = eng
        self.fn = fn
        self.reads = reads
        self.writes = writes
        self.dma = dma
        self.track = track
        self.deps = []
        self.signal = False
        self.tok = None
        self.barrier = False


ENGS = ("pe", "act", "dve", "pool", "sp")


class Prog:
    def __init__(self):
        self.ops = []
        self.bufs = []
        self.phase = "setup"

    def buf(self, name):
        b = Buf(name)
        self.bufs.append(b)
        return b

    def add(self, eng, fn, reads=(), writes=(), dma=False, track=None):
        op = Op(eng, fn, tuple(reads), tuple(writes), dma, track)
        op.phase = self.phase
        self.ops.append(op)
        return op

    def barrier(self):
        for e in ENGS:
            op = Op(e, None, (), (), False, None)
            op.barrier = True
            op.phase = self.phase
            self.ops.append(op)

    def resolve(self):
        ops = self.ops
        last_on_eng = {e: None for e in ENGS}
        dma_ops = []
        for i, op in enumerate(ops):
            if op.barrier:
                deps = set()
                for e in ENGS:
                    if e != "sp" and last_on_eng[e] is not None and e != op.eng:
                        deps.add(last_on_eng[e])
                deps.update(dma_ops)
                op.deps = sorted(deps)
                for j in op.deps:
                    ops[j].signal = True
                continue
            deps = set()

            def need(j, kind):
                p = ops[j]
                if p.dma or op.dma:
                    if p.dma and op.dma and kind == "waw" and p.track is op.track:
                        return
                    deps.add(j)
                    return
                if p.eng == op.eng:
                    if op.eng == "pe":
                        return
                deps.add(j)

            for b in op.reads:
                if b.last_w is not None:
                    need(b.last_w, "raw")
            for b in op.writes:
                if b.last_w is not None:
                    need(b.last_w, "waw")
                lastr = {}
                for j in b.readers:
                    p = ops[j]
                    if p.dma:
                        need(j, "war")
                    else:
                        lastr[p.eng] = j
                for j in lastr.values():
                    need(j, "war")
            for b in op.reads:
                b.readers.append(i)
            for b in op.writes:
                b.last_w = i
                b.readers = []
            op.deps = sorted(deps)
            for j in op.deps:
                ops[j].signal = True
            if op.dma:
                dma_ops.append(i)
            else:
                last_on_eng[op.eng] = i

    def prepare(self, nc, stack):
        ops = self.ops
        esem = {}
        for e in ("pe", "act", "dve", "pool"):
            esem[e] = stack.enter_context(nc.semaphore("es_" + e))
        for b in self.bufs:
            b.dma_cnt = 0
        tracked = []
        for op in ops:
            if op.dma:
                t = op.track
                if t.sem is None:
                    t.sem = stack.enter_context(nc.semaphore("ds_" + t.name))
                    tracked.append(t)
        cnt = {e: 0 for e in ENGS}
        for op in ops:
            if op.barrier:
                continue
            if op.dma:
                op.track.dma_cnt += 1
                op.tok = (op.track.sem, 16 * op.track.dma_cnt)
            elif op.signal:
                cnt[op.eng] += 1
                op.tok = (esem[op.eng], cnt[op.eng])
        know = {e: {} for e in ENGS}
        clocks = {}
        nw = 0
        for i, op in enumerate(ops):
            kn = know[op.eng]
            w = {}
            for j in op.deps:
                s, v = ops[j].tok
                if kn.get(s, 0) < v and w.get(s, (0, None))[0] < v:
                    w[s] = (v, j)
            waits = []
            for s, (v, j) in w.items():
                if kn.get(s, 0) >= v:
                    continue
                waits.append((s, v))
                for s2, v2 in clocks[j].items():
                    if kn.get(s2, 0) < v2:
                        kn[s2] = v2
                kn[s] = max(kn.get(s, 0), v)
            op.deps = waits
            nw += len(waits)
            if op.tok is not None:
                c = dict(kn)
                if not op.dma:
                    c[op.tok[0]] = op.tok[1]
                else:
                    c[op.tok[0]] = max(c.get(op.tok[0], 0), op.tok[1])
                clocks[i] = c
        self.nwaits = nw
        per = {e: [] for e in ENGS}
        for op in ops:
            per[op.eng].append(op)
        self.stats = {e: len(per[e]) for e in ENGS}
        self._emit_state = (ops, esem, tracked, per)

    def emit(self, block):
        ops, esem, tracked, per = self._emit_state

        def run(name, e):
            seen = {}
            for op in per[name]:
                for s, v in op.deps:
                    if seen.get(s, 0) < v:
                        e.wait_ge(s, v)
                        seen[s] = v
                if op.barrier:
                    continue
                ins = op.fn(e)
                if op.dma:
                    ins.then_inc(op.track.sem, 16)
                elif op.signal:
                    ins.then_inc(esem[name], 1)
            if name == "sp":
                for t in tracked:
                    if seen.get(t.sem, 0) < 16 * t.dma_cnt:
                        e.wait_ge(t.sem, 16 * t.dma_cnt)

        @block.sync
        def _(e):
            run("sp", e)

        @block.tensor
        def _(e):
            run("pe", e)

        @block.scalar
        def _(e):
            run("act", e)

        @block.vector
        def _(e):
            run("dve", e)

        @block.gpsimd
        def _(e):
            run("pool", e)


W_SPECS = [
    ("x", None, F32),
    ("w_g1", [D, DFF], F32), ("w_u1", [D, DFF], F32), ("w_d1", [DFF, D], F32),
    ("w_g2", [D, DFF], F32), ("w_u2", [D, DFF], F32), ("w_d2", [DFF, D], F32),
    ("w_in", [D, 4512], F32),
    ("w_br", [512, D], F32), ("w_ba", [512, D], F32), ("w_out", [D, D], F32),
    ("g1c", [128, 8], F32), ("gmc", [128, 8], F32), ("g2c", [128, 8], F32),
    ("fgain", [1, D], F32), ("mu", [1, 1696], F32),
    ("p_kk", [1, 512], F32), ("p_ka", [1, 512], F32), ("p_rk", [1, 512], F32),
    ("p_lnw", [1, 512], F32), ("p_lnb", [1, 512], F32),
    ("p_w0", [1, 512], F32), ("p_a0", [1, 512], F32),
    ("p_wl", [32, 512], F32), ("p_al", [32, 512], F32), ("p_gl", [96, 512], F32),
    ("gqc", [128, 1], F32), ("gkc", [128, 1], F32), ("sinks", [1, 8], F32),
    ("c_ident", [128, 128], F32), ("c_tri_i", [128, 128], F32), ("c_tri_s", [128, 128], F32),
    ("c_blk", [128, 128], F32), ("c_mask4", [128, 512], F32), ("c_maskl", [128, 128], F32),
    ("c_amo", [128, 128], F32), ("c_amp", [128, 128], F32), ("c_tb", [128, 4], F32),
]


def _build(T=4096, do_ffn1=True, do_mix=True, do_ffn2=True, dbg=None, skip=(), stage=9, order=None, overlap=True, ratio=1.0):
    NT = T // TT
    nc = bass.Bass("TRN2", target_bir_lowering=False)
    P = Prog()
    dr = {}
    for name, shp, dt in W_SPECS:
        if name == "x":
            shp = [T, D]
        dr[name] = nc.dram_tensor(name, shp, dt, kind="ExternalInput").ap()
    y_out = nc.dram_tensor("y", [T, D], F32, kind="ExternalOutput").ap()
    dbg_out = None
    if dbg:
        dbg_out = nc.dram_tensor("dbg", [128, dbg], F32, kind="ExternalOutput").ap()

    def scr(name, shape):
        return nc.dram_tensor(name, shape, BF16, kind="Internal").ap()

    s_gu = [scr("s_gu%d" % i, [NF, 128, 2048]) for i in range(2)]
    s_d = [scr("s_d%d" % i, [2, NF, 128, 512]) for i in range(2)]
    s_lora = scr("s_lora", [128, 16 * 64])
    s_xg = scr("s_xg", [128, 16 * 96])
    s_q = scr("s_q", [4, 128, 1024])
    s_kd = scr("s_kd", [2, 128, 1024])
    s_gate = scr("s_gate", [8, 128, 2048])
    s_rkv = scr("s_rkv", [3, 4, 128, 2048])
    s_av = scr("s_av", [128, 1024])
    s_br = scr("s_br", [8, 128, 1024])
    s_wo = scr("s_wo", [2, 2, 128, 2048])

    with ExitStack() as st:
        def sb(name, shape, dt=F32):
            return st.enter_context(nc.sbuf_tensor(name, shape, dt))

        def sbb(name, shape, dt=F32):
            return sb(name, shape, dt), P.buf(name)

        banks = []
        for i in range(8):
            t = st.enter_context(nc.psum_tensor("pb%d" % i, [128, 512], F32))
            banks.append((t, P.buf("pb%d" % i)))
        bank_free = list(range(8))
        bank_of = {b_: i_ for i_, (_, b_) in enumerate(banks)}

        def PS():
            assert bank_free, "out of PSUM banks (too many concurrently open)"
            return banks[bank_free.pop(0)]

        def PSrel(*bufs_):
            for b_ in bufs_:
                assert bank_of[b_] not in bank_free
                bank_free.append(bank_of[b_])

        def MM(out, lhsT, rhs, start, stop, reads, writes):
            P.add("pe", lambda e: e.matmul(out, lhsT=lhsT, rhs=rhs, start=start, stop=stop), reads, writes)

        def TR(out, in_, ident, reads, writes):
            P.add("pe", lambda e: e.transpose(out=out, in_=in_, identity=ident), reads, writes)

        def ACT(out, in_, func, reads, writes, scale=None, bias=None, accum=None):
            kw = {}
            if scale is not None:
                kw["scale"] = scale
            if bias is not None:
                kw["bias"] = bias
            if accum is not None:
                kw["accum_out"] = accum
            P.add("act", lambda e: e.activation(out=out, in_=in_, func=func, **kw), reads, writes)

        def TT_(eng, out, in0, in1, op, reads, writes):
            P.add(eng, lambda e: e.tensor_tensor(out=out, in0=in0, in1=in1, op=op), reads, writes)

        def TS(eng, out, in0, s1, s2, op0, op1, reads, writes):
            if op1 is None:
                P.add(eng, lambda e: e.tensor_scalar(out=out, in0=in0, scalar1=s1, scalar2=None, op0=op0), reads, writes)
            else:
                P.add(eng, lambda e: e.tensor_scalar(out=out, in0=in0, scalar1=s1, scalar2=s2, op0=op0, op1=op1), reads, writes)

        def STT(out, in0, scalar, in1, op0, op1, reads, writes):
            P.add("dve", lambda e: e.scalar_tensor_tensor(out=out, in0=in0, scalar=scalar, in1=in1, op0=op0, op1=op1), reads, writes)

        def CP(eng, out, in_, reads, writes):
            if eng == "act":
                ACT(out, in_, AF.Copy, reads, writes)
            else:
                P.add(eng, lambda e: e.tensor_copy(out=out, in_=in_), reads, writes)

        def RED(out, in_, reads, writes):
            P.add("dve", lambda e: e.tensor_reduce(out=out, in_=in_, axis=AX.X, op=ALU.add), reads, writes)

        def DMA(eng, out, in_, reads, writes, track):
            P.add(eng, lambda e: e.dma_start(out=out, in_=in_), reads, writes, dma=True, track=track)

        def MEMSET(eng, ap, val, writes):
            P.add(eng, lambda e: e.memset(ap, val), (), writes)

        ARENA = 23808
        arena = sb("arena", [128, ARENA], BF16)
        aoff = [0]

        def carve(name, shape, dt=F32):
            n = 1
            for s_ in shape[1:]:
                n *= s_
            ne = n * (2 if dt == F32 else 1)
            assert aoff[0] + ne <= ARENA, (name, aoff[0], ne)
            ap = arena[0:shape[0], aoff[0]:aoff[0] + ne]
            aoff[0] += ne
            if dt == F32:
                ap = ap.bitcast(F32)
            if len(shape) == 3:
                ap = ap.rearrange("p (a b) -> p a b", b=shape[2])
            elif len(shape) == 4:
                ap = ap.rearrange("p (a b c) -> p a b c", b=shape[2], c=shape[3])
            return ap

        def carveb(name, shape, dt=F32):
            return carve(name, shape, dt), P.buf(name)

        cb = P.buf("consts")

        def cload(name, shape, src=None, bcast=False, temp=False):
            t = carve("c_" + name, shape, F32) if temp else sb("c_" + name, shape, F32)
            s = dr[name] if src is None else src
            if bcast:
                DMA("sp", t[:], s.partition_broadcast(shape[0]), (), (cb,), cb)
            else:
                DMA("sp", t[:], s[:, :], (), (cb,), cb)
            return t

        identf = cload("c_ident", [128, 128], temp=True)
        tri_i = cload("c_tri_i", [128, 128])
        tri_s = cload("c_tri_s", [128, 128])
        blk = cload("c_blk", [128, 128])
        mask4f = cload("c_mask4", [128, 512], temp=True)
        masklf = cload("c_maskl", [128, 128], temp=True)
        amof = cload("c_amo", [128, 128], temp=True)
        ampf = cload("c_amp", [128, 128], temp=True)
        tbias = cload("c_tb", [128, 4])
        g1c = cload("g1c", [128, 8])
        gmc = cload("gmc", [128, 8])
        g2c = cload("g2c", [128, 8])
        gqc = cload("gqc", [128, 1])
        gkc = cload("gkc", [128, 1], temp=True)
        fgain = cload("fgain", [128, D], bcast=True)
        kkb = cload("p_kk", [128, 512], bcast=True)
        kab = cload("p_ka", [128, 512], bcast=True)
        rkb = cload("p_rk", [128, 512], bcast=True)
        lnwb = cload("p_lnw", [128, 512], bcast=True)
        lnbb = cload("p_lnb", [128, 512], bcast=True)
        sinkb = cload("sinks", [128, 8], bcast=True, temp=True)
        w0f = cload("p_w0", [1, 512], temp=True)
        a0f = cload("p_a0", [1, 512], temp=True)
        wl_f = carve("wl_f", [64, 512], F32)
        DMA("sp", wl_f[32:64, :], dr["p_wl"][:, :], (), (cb,), cb)
        al_f = cload("p_al", [32, 512], temp=True)
        gl_f = cload("p_gl", [96, 512], temp=True)

        cb2 = P.buf("consts2")
        identb = sb("identb", [128, 128], BF16)
        mask4 = sb("mask4", [128, 512], BF16)
        maskl = sb("maskl", [128, 128], BF16)
        amo = sb("amo", [128, 128], BF16)
        amp = sb("amp", [128, 128], BF16)
        rhs_w = sb("rhs_w", [66, 512], BF16)
        rhs_a = sb("rhs_a", [66, 512], BF16)
        gl_t = sb("gl_t", [96, 512], BF16)
        w0hl = carve("w0hl", [2, 512], BF16)
        a0b = carve("a0b", [1, 512], BF16)
        onesc = sb("onesc", [128, 1], F32)
        nhalf = sb("nhalf", [128, 8], F32)
        sinkexp = sb("sinkexp", [128, 8], F32)
        gk8 = sb("gk8", [128, 1], F32)
        w0tmp = carve("w0tmp", [1, 1024], F32)
        CP("dve", identb[:], identf[:], (cb,), (cb2,))
        CP("dve", mask4[:], mask4f[:], (cb,), (cb2,))
        CP("dve", maskl[:], masklf[:], (cb,), (cb2,))
        CP("dve", amo[:], amof[:], (cb,), (cb2,))
        CP("dve", amp[:], ampf[:], (cb,), (cb2,))
        rwb_ = P.buf("rhs_wa")
        MEMSET("pool", rhs_w[:], 0.0, (rwb_,))
        MEMSET("pool", rhs_a[:], 0.0, (rwb_,))
        CP("dve", rhs_w[32:64, :], wl_f[32:64, :], (cb, rwb_), (rwb_,))
        CP("dve", rhs_a[0:32, :], al_f[:], (cb, rwb_), (rwb_,))
        TS("dve", gl_t[:], gl_f[:], 0.5, None, ALU.mult, None, (cb,), (cb2,))
        pass
        MEMSET("pool", onesc[:], 1.0, (cb2,))
        MEMSET("pool", nhalf[:], -0.5, (cb2,))
        eps64 = sb("eps64", [128, 1], F32)
        MEMSET("pool", eps64[:], 64 * RMS_EPS, (cb2,))
        ACT(sinkexp[:], sinkb[:], AF.Exp, (cb,), (cb2,))
        TS("dve", gk8[:], gkc[:], 8.0, None, ALU.mult, None, (cb,), (cb2,))
        w0b_ = P.buf("w0b")
        CP("dve", w0hl[0:1, :], w0f[:], (cb,), (w0b_,))
        CP("dve", w0tmp[:, 0:512], w0hl[0:1, :], (w0b_,), (w0b_,))
        TT_("dve", w0tmp[:, 512:1024], w0f[:], w0tmp[:, 0:512], ALU.subtract, (cb, w0b_), (w0b_,))
        w0lo = carve("w0lo", [1, 512], BF16)
        CP("dve", w0lo[:], w0tmp[:, 512:1024], (w0b_,), (w0b_,))
        DMA("sp", rhs_w[64:65, :], w0hl[0:1, :], (w0b_, rwb_), (cb2,), cb2)
        DMA("sp", rhs_w[65:66, :], w0lo[:], (w0b_, rwb_), (cb2,), cb2)
        a0b_ = P.buf("a0b")
        CP("dve", a0b[:], a0f[:], (cb,), (a0b_,))
        DMA("sp", rhs_a[64:65, :], a0b[:], (a0b_, rwb_), (cb2,), cb2)

        xb = [sbb("xb%d" % i, [128, 4, D], F32) for i in range(3)]
        hT, hTb = sbb("hT", [128, 8, 520], BF16)
        hTf, hTfb = sbb("hTf", [128, 8, 512], BF16)
        actT, actTb = sbb("actT", [128, 8, 512], BF16)
        sgt = [sbb("sgt%d" % i, [128, 512], F32) for i in range(2)]
        junk, junkb = sgt[0][0][:].bitcast(BF16), sgt[0][1]
        hb = [(sgt[1][0][:].bitcast(BF16), sgt[1][1])]
        ssq, ssqb = sbb("ssq", [128, 8], F32)
        rstd, rstdb = sbb("rstd", [128, 8], F32)
        ring = [sbb("ring%d" % i, [128, SLOT], BF16) for i in range(RING)]

        P.barrier()
        P.phase = "prep"
        aoff[0] = 0
        NSTF, NSTB = 4, 4
        stf = [carveb("stf%d" % i, [128, 1408], F32) for i in range(NSTF)]
        stb = [carveb("stb%d" % i, [128, 1408], BF16) for i in range(NSTB)]
        mub = carve("mub", [128, 1696], F32)
        omub = carve("omub", [128, 1696], F32)
        DMA("sp", mub[:], dr["mu"].partition_broadcast(128), (), (cb,), cb)
        TS("dve", omub[:], mub[:], -1.0, 1.0, ALU.mult, ALU.add, (cb,), (cb2,))
        pc = {"f": 0, "b": 0, "e": 0}

        def prep_piece(src, ncol, variants):
            sf, sfb = stf[pc["f"] % NSTF]
            pc["f"] += 1
            DMA("sp", sf[:, 0:ncol], src, (), (sfb,), sfb)
            for (rs, const, cs, stores) in variants:
                so, sob = stb[pc["b"] % NSTB]
                pc["b"] += 1
                if cs is not None:
                    TT_("dve", so[:, 0:ncol], sf[:, 0:ncol], cs, ALU.mult, (sfb, cb, cb2), (sob,))
                else:
                    eng = ("dve", "act")[pc["e"] % 2]
                    pc["e"] += 1
                    if eng == "act":
                        assert rs is None or const == 1.0
                        ACT(so[:, 0:ncol], sf[:, 0:ncol], AF.Copy, (sfb, cb), (sob,), scale=(const if rs is None else rs))
                    elif rs is None:
                        TS(eng, so[:, 0:ncol], sf[:, 0:ncol], const, None, ALU.mult, None, (sfb,), (sob,))
                    else:
                        TS(eng, so[:, 0:ncol], sf[:, 0:ncol], rs, const, ALU.mult, ALU.mult, (sfb, cb), (sob,))
                for (dst, src_ap) in stores(so):
                    DMA("sp", dst, src_ap, (sob,), (), sob)

        sbuf_ = {}

        def sbufof(name):
            if name not in sbuf_:
                sbuf_[name] = P.buf("scr_" + name)
            return sbuf_[name]

        def CAST(name, dst, src_):
            b_ = sbufof(name)
            ph = P.phase
            P.phase = "cast"
            P.add("pool", lambda e: e.dma_start(out=dst, in_=src_), (), (b_,), dma=True, track=b_)
            P.phase = ph

        pending = {"gen": None, "n": 0}
        CAST_NEED = {"gu0b": 16, "d0": 18, "q": 42, "kd": 42, "av": 42, "gate": 42, "br": 50, "wo": 58, "gu1": 90, "d1": 92}

        def drip(n):
            g = pending["gen"]
            if g is None:
                return
            for _ in range(n):
                try:
                    next(g)
                    pending["n"] += 1
                except StopIteration:
                    pending["gen"] = None
                    return

        def ensure_cast(sname):
            need_ = CAST_NEED.get(sname)
            if need_ is not None and pending["gen"] is not None and pending["n"] < need_:
                drip(need_ - pending["n"])

        def cast_ffn(fi, fgroups=((0, NF, ""),), with_d=True):
            gn, un, dn = (("w_g1", "w_u1", "w_d1"), ("w_g2", "w_u2", "w_d2"))[fi]
            guv = s_gu[fi].rearrange("f p (t k m) -> p f t k m", t=2, k=8)
            for (fa, fb, sfx) in fgroups:
                for t_, wn in enumerate((gn, un)):
                    for k in range(8):
                        CAST("gu%d%s" % (fi, sfx), guv[:, fa:fb, t_, k, :],
                             dr[wn][k * 128:(k + 1) * 128, fa * 128:fb * 128].rearrange("p (f m) -> p f m", m=128))
                        yield
            for half in range(2):
                if with_d:
                    CAST("d%d" % fi, s_d[fi][half].rearrange("f p n -> (f p) n"), dr[dn][:, half * 512:(half + 1) * 512])
                    yield

        def cast_mix():
            qv = s_q.rearrange("m p (k n) -> p m k n", k=8)
            kdv = s_kd.rearrange("g p (k n) -> p g k n", k=8)
            avv = s_av.rearrange("p (k n) -> p k n", k=8)
            gtv = s_gate.rearrange("m p (t k n) -> p t m k n", t=2, k=8)
            for k in range(8):
                rows = dr["w_in"][k * 128:(k + 1) * 128, :]
                CAST("q", qv[:, :, k, :], rows[:, 1696:2208].rearrange("p (m n) -> p m n", n=128))
                for g in range(2):
                    for hf in range(2):
                        CAST("kd", kdv[:, g, k, hf * 64:(hf + 1) * 64], rows[:, 2208 + g * 64:2272 + g * 64])
                CAST("av", avv[:, k, :], rows[:, 2336:2464])
                yield
                for t_ in range(2):
                    CAST("gate", gtv[:, t_, :, k, :], rows[:, 2464 + t_ * 1024:3488 + t_ * 1024].rearrange("p (m n) -> p m n", n=128))
                    yield
            brv = s_br.rearrange("m p (t c n) -> p t c m n", t=2, c=4)
            for t_, wn in enumerate(("w_br", "w_ba")):
                for c in range(4):
                    CAST("br", brv[:, t_, c, :, :], dr[wn][c * 128:(c + 1) * 128, :].rearrange("p (m n) -> p m n", n=128))
                    yield
            wov = s_wo.rearrange("h q p (k n) -> h q p k n", k=4)
            for m in range(8):
                for h in range(2):
                    CAST("wo", wov[h, m // 4, :, m % 4, :], dr["w_out"][m * 128:(m + 1) * 128, h * 512:(h + 1) * 512])
                yield

        if do_ffn1:
            for _ in cast_ffn(0, ((0, 8, "a"),), with_d=False):
                pass
        def staged_prep():
            rkvv = s_rkv.rearrange("c q p (k n) -> c q p k n", k=4)
            lorav = s_lora.rearrange("p (k m) -> p k m", m=64)
            xgv = s_xg.rearrange("p (k m) -> p k m", m=96)
            for k in range(8):
                def stores_a1(so, kc):
                    return [(rkvv[c, kc // 4, :, kc % 4, :], so[:, c * 512:(c + 1) * 512]) for c in range(2)]

                def stores_a2(so, kc):
                    return [(rkvv[2, kc // 4, :, kc % 4, :], so[:, 0:512]),
                            (lorav[:, kc, 0:32], so[:, 544:576]),
                            (lorav[:, kc, 32:64], so[:, 512:544]),
                            (xgv[:, kc, :], so[:, 576:672])]
                prep_piece(dr["w_in"][k * 128:(k + 1) * 128, 0:1024], 1024,
                           [(None, 1.0, omub[:, 0:1024], lambda so, k=k: stores_a1(so, k)),
                            (None, 1.0, mub[:, 0:1024], lambda so, k=k: stores_a1(so, 8 + k))])
                yield
                prep_piece(dr["w_in"][k * 128:(k + 1) * 128, 1024:1696], 672,
                           [(None, 1.0, omub[:, 1024:1696], lambda so, k=k: stores_a2(so, k)),
                            (None, 1.0, mub[:, 1024:1696], lambda so, k=k: stores_a2(so, 8 + k))])
                yield


        def cast2():
            if do_ffn1:
                yield from cast_ffn(0, ((8, NF, "b"),))
            if do_mix:
                yield from cast_mix()
            if do_ffn2:
                yield from cast_ffn(1)
        pending["gen"] = cast2()

        rec = []

        def wresolve(dsc):
            k = dsc[0]
            if k == "gu":
                return s_gu[dsc[1]][dsc[2]], 2048
            if k == "d":
                _, fi_, half, f0, nf = dsc
                return s_d[fi_][half, f0:f0 + nf].rearrange("f p n -> p f n"), nf * 512
            if k == "lora":
                return s_lora, 1024
            if k == "xg":
                return s_xg, 1536
            if k == "q":
                return s_q[dsc[1]], 1024
            if k == "kd":
                return s_kd[dsc[1]], 1024
            if k == "rkv":
                return s_rkv[dsc[1], dsc[2]], 2048
            if k == "av":
                return s_av, 1024
            if k == "gate":
                return s_gate[dsc[1]], 2048
            if k == "br":
                return s_br[dsc[1]], 1024
            if k == "wo":
                return s_wo[dsc[1], dsc[2]], 2048
            raise KeyError(dsc)

        sidx = {"get": 0, "issued": 0}
        staged_ready = {"v": not do_mix}

        def wget(dsc):
            i = sidx["get"]
            sidx["get"] += 1
            if order is None:
                rec.append(dsc)
                return ring[i % RING]
            assert order[i] == dsc, (i, order[i], dsc)
            while sidx["issued"] < min(len(order), i + RING):
                j = sidx["issued"]
                src_, n = wresolve(order[j])
                rt, rb_ = ring[j % RING]
                knd = order[j][0]
                if knd in ("lora", "xg", "rkv") and not staged_ready["v"]:
                    break
                sname = knd + str(order[j][1]) if knd in ("gu", "d") else knd
                if sname == "gu0":
                    sname = "gu0a" if order[j][2] < 8 else "gu0b"
                ensure_cast(sname)
                rd = (sbuf_[sname],) if sname in sbuf_ else ()
                if len(src_.shape) == 3:
                    DMA("sp", rt[:, 0:n].rearrange("p (f n) -> p f n", n=512), src_, rd, (rb_,), rb_)
                else:
                    DMA("sp", rt[:, 0:n], src_, rd, (rb_,), rb_)
                sidx["issued"] += 1
            return ring[i % RING]

        Hf, Hfb = sbb("Hf", [128, 256], F32)
        hcar, hcarb = sbb("hcar", [128, 8, 1], BF16)
        MEMSET("pool", Hf[:], 0.0, (Hfb,))
        MEMSET("pool", hcar[:], 0.0, (hcarb,))

        def rms_stats(xt, xtb):
            for s in range(4):
                ACT(junk[:], xt[:, s, :], AF.Square, (xtb,), (junkb, ssqb), accum=ssq[:, s:s + 1])
            TS("dve", ssq[:, 4:8], ssq[:, 0:4], 1.0 / D, RMS_EPS, ALU.mult, ALU.add, (ssqb,), (ssqb,))
            TT_("pool", rstd[:, 0:4], ssq[:, 4:8], nhalf[:, 0:4], ALU.pow, (ssqb, cb2), (rstdb,))
            drip(24)

        def norm_T(xt, xtb, dst, dstb, col0, gcol):
            rms_stats(xt, xtb)
            for s in range(4):
                h_, hb_ = hb[0]
                TS("dve", h_[:], xt[:, s, :], rstd[:, s:s + 1], None, ALU.mult, None, (xtb, rstdb), (hb_,))
                pt, ptb = PS()
                pv = pt[:].bitcast(BF16)
                for k in range(8):
                    TR(pv[:, k * 128:(k + 1) * 128], h_[:, k * 128:(k + 1) * 128], identb[:], (hb_, cb2), (ptb,))
                TT_("dve", dst[:, :, col0 + s * 128:col0 + (s + 1) * 128], pv.rearrange("p (k n) -> p k n", n=128),
                    gcol[:, 0:8].unsqueeze(2).to_broadcast([128, 8, 128]), ALU.mult, (ptb, cb), (dstb,))
                PSrel(ptb)
                yield

        GROUPS = ((0, 8), (8, 8), (16, 6))

        def ffn(fi_, xt, xtb):
            yield from norm_T(xt, xtb, hTf, hTfb, 0, (g1c, g2c)[fi_])
            for (g0, gn) in GROUPS:
                for fi in range(gn):
                    rt, rb_ = wget(("gu", fi_, g0 + fi))
                    gu = rt[:].rearrange("p (t k m) -> p t k m", t=2, k=8)
                    pg, pgb = PS()
                    pu, pub = PS()
                    for k in range(8):
                        MM(pg[:], gu[:, 0, k, :], hTf[:, k, :], k == 0, k == 7, (rb_, hTfb), (pgb,))
                    sg, sgb = sgt[fi % 2]
                    ACT(sg[:], pg[:], AF.Tanh, (pgb,), (sgb,), scale=0.5)
                    yield
                    for k in range(8):
                        MM(pu[:], gu[:, 1, k, :], hTf[:, k, :], k == 0, k == 7, (rb_, hTfb), (pub,))
                    STT(sg[:], sg[:], 1.0, pg[:], ALU.add, ALU.mult, (sgb, pgb), (sgb,))
                    TT_("dve", actT[:, fi, :], sg[:], pu[:], ALU.mult, (sgb, pub), (actTb,))
                    PSrel(pgb, pub)
                    yield
                for half in range(2):
                    pbs = [PS() for _ in range(4)]
                    for f0 in range(0, gn, 4):
                        nf = min(4, gn - f0)
                        rt, rb_ = wget(("d", fi_, half, g0 + f0, nf))
                        dv = rt[:].rearrange("p (f n) -> p f n", n=512)
                        for ff in range(nf):
                            f = f0 + ff
                            for s in range(4):
                                MM(pbs[s][0][:], actT[:, f, s * 128:(s + 1) * 128], dv[:, ff, :], f == 0, f == gn - 1,
                                   (actTb, rb_), (pbs[s][1],))
                            if ff % 2 == 1 or ff == nf - 1:
                                yield
                    for s in range(4):
                        xs = xt[:, s, half * 512:(half + 1) * 512]
                        STT(xs, pbs[s][0][:], 0.25, xs, ALU.mult, ALU.add, (pbs[s][1], xtb), (xtb,))
                        PSrel(pbs[s][1])
                    yield

        def tagged(gen, tag):
            while True:
                P.phase = tag
                try:
                    next(gen)
                except StopIteration:
                    return
                yield

        def run_all(gen):
            for _ in gen:
                pass

        def interleave(main_gen, side_gen, ratio=1):
            side_done = side_gen is None
            acc = 0.0
            for _ in main_gen:
                acc += ratio
                while acc >= 1.0:
                    acc -= 1.0
                    if not side_done:
                        try:
                            next(side_gen)
                        except StopIteration:
                            side_done = True
            if not side_done:
                run_all(side_gen)

        if do_mix:
            rkv = [sbb("rkv%d" % c, [128, 4, 512], BF16) for c in range(3)]
            lT, lTb = sbb("lT", [66, 512], BF16)
            gTt, gTb = sbb("gTt", [96, 512], BF16)
            qT, qTb = sbb("qT", [128, 4, 512], BF16)
            kTd, kTdb = sbb("kTd", [128, 2, 2, 640], BF16)
            vaug, vaugb = sbb("vaug", [128, 5, 2, VW], BF16)
            yrT, yrTb = sbb("yrT", [128, 4, 512], BF16)
            yaT, yaTb = sbb("yaT", [128, 4, 512], BF16)
            mgT, mgTb = actT[:, 0:8, :], actTb
            aoff[0] = 0
            tmpf = [carveb("tf%d" % i, [128, 512], F32) for i in range(9)]
            tmpf.append(tmpf[0])
            tmpb = [carveb("tb%d" % i, [128, 512], BF16) for i in range(6)]
            tmpb.append(tmpb[4])
            arT, arTb = sbb("arT", [128, 4, 2, 128], BF16)
            bkT, bkTb = sbb("bkT", [128, 4, 2, 128], BF16)
            arZ, arZb = sbb("arZ", [128, 4, 2, 2, 128], BF16)
            bZ, bZb = sbb("bZ", [128, 4, 2, 128], BF16)
            Hz, Hzb = sbb("Hz", [128, 4, 128], BF16)
            MEMSET("pool", arZ[:], 0.0, (arZb,))
            MEMSET("pool", bZ[:], 0.0, (bZb,))
            MEMSET("pool", Hz[:], 0.0, (Hzb,))
            MEMSET("pool", lT[:], 1.0, (lTb,))
            SCb = [carveb("SCb%d" % p_, [128, 2, 2, 128], BF16) for p_ in range(4)]
            SCk = [carveb("SCk%d" % p_, [128, 2, 2, 128], BF16) for p_ in range(4)]
            Pm = [[carveb("Pm%d_%d" % (a, g), [128, 4, 128], BF16) for g in range(2)] for a in range(2)]
            PTm = [[carveb("PTm%d_%d" % (a, g), [128, 4, 128], BF16) for g in range(2)] for a in range(2)]
            Qm = [[carveb("Qm%d_%d" % (a, g), [128, 4, 128], BF16) for g in range(2)] for a in range(2)]
            PTa = [sbb("PTa0", [128, 2, 512], BF16)] * 2
            yatt, yattb = sbb("yatt", [128, 512], BF16)
            gamC, gamCb = sbb("gamC", [128, 4], F32)
            sts = [sbb("sts%d" % i, [128, 16], F32) for i in range(7)]
            MEMSET("pool", vaug[:], 1.0, (vaugb,))
            MEMSET("pool", kTd[:], 0.0, (kTdb,))
            MEMSET("pool", yrT[:], 0.0, (yrTb,))
            MEMSET("pool", yaT[:], 0.0, (yaTb,))

            def v3(ap, d=64):
                return ap.rearrange("p (h d) -> p h d", d=d)

            def bc8(ap, n=8, d=64):
                return ap.unsqueeze(2).to_broadcast([128, n, d])

            def rhs_k(kc, lo=0, n=512):
                if kc < 8:
                    return hT[:, kc, 8 + lo:8 + lo + n]
                return hT[:, kc - 8, 7 + lo:7 + lo + n]

            def qk_norm(pq, pqb, gcol, dst, dstb, dsts=None):
                sq, sqb = tmpf[7]
                ACT(sq[:], pq[:], AF.Square, (pqb,), (sqb,))
                ps, psb = PS()
                MM(ps[:], blk[:], sq[:], True, True, (cb, sqb), (psb,))
                t_, tb_ = tmpf[8]
                ACT(t_[:], ps[:], AF.Ln, (psb, cb2), (tb_,), bias=eps64[:, 0:1])
                PSrel(psb)
                rs, rsb = tmpf[9]
                ACT(rs[:], t_[:], AF.Exp, (tb_,), (rsb,), scale=-0.5)
                if dsts is None:
                    dsts = [(slice(0, 128), dst)]
                for (rows, d_) in dsts:
                    STT(d_, pq[rows, :], gcol[rows, :], rs[rows, :], ALU.mult, ALU.mult, (pqb, rsb, cb, cb2), (dstb,))
                PSrel(pqb)

            def attn_block(b, gb):
                for g in range(2):
                    po, pob = PS()
                    if gb > 0:
                        pp, ppb = PS()
                    for hh in range(4):
                        h = 4 * g + hh
                        ch, hf = divmod(h, 2)
                        qs = qT[:, ch, b * 128:(b + 1) * 128]
                        MM(po[:, hh * 128:(hh + 1) * 128], kTd[:, g, hf, (1 + b) * 128:(2 + b) * 128], qs, True, True,
                           (kTdb, qTb), (pob,))
                        if gb > 0:
                            MM(pp[:, hh * 128:(hh + 1) * 128], kTd[:, g, hf, b * 128:(b + 1) * 128], qs, True, True,
                               (kTdb, qTb), (ppb,))
                    pt_, ptb_ = PTa[g]
                    ACT(pt_[:, 1, :], po[:], AF.Exp, (pob,), (ptb_,))
                    PSrel(pob)
                    P.add("pool", lambda e, ap=pt_[:, 1, :]: e.affine_select(
                        out=ap, in_=ap, pattern=[[0, 4], [1, 128]], compare_op=ALU.is_ge, fill=0.0, base=0,
                        channel_multiplier=-1), (ptb_,), (ptb_,))
                    if gb > 0:
                        ACT(pt_[:, 0, :], pp[:], AF.Exp, (ppb,), (ptb_,))
                        PSrel(ppb)
                        P.add("pool", lambda e, ap=pt_[:, 0, :]: e.affine_select(
                            out=ap, in_=ap, pattern=[[0, 4], [-1, 128]], compare_op=ALU.is_ge, fill=0.0, base=-1,
                            channel_multiplier=1), (ptb_,), (ptb_,))
                    yield
                    ppv, ppvb = PS()
                    for hh in range(4):
                        o_ = ppv[:, hh * VW:(hh + 1) * VW]
                        if gb > 0:
                            MM(o_, pt_[:, 0, hh * 128:(hh + 1) * 128], vaug[:, b, g, :], True, False, (ptb_, vaugb), (ppvb,))
                        MM(o_, pt_[:, 1, hh * 128:(hh + 1) * 128], vaug[:, 1 + b, g, :], gb == 0, True, (ptb_, vaugb), (ppvb,))
                    pv3 = ppv[:, 0:4 * VW].rearrange("p (h d) -> p h d", d=VW)
                    sg_, sgb_ = sts[6]
                    TT_("dve", sg_[:, 0:4].unsqueeze(2), pv3[:, :, 64:65], sinkexp[:, 4 * g:4 * g + 4].unsqueeze(2), ALU.add,
                        (ppvb, cb2), (sgb_,))
                    P.add("dve", lambda e, sg_=sg_: e.reciprocal(out=sg_[:, 4:8], in_=sg_[:, 0:4]), (sgb_,), (sgb_,))
                    TT_("dve", v3(yatt[:, g * 256:(g + 1) * 256]), pv3[:, :, 0:64],
                        sg_[:, 4:8].unsqueeze(2).to_broadcast([128, 4, 64]), ALU.mult, (ppvb, sgb_), (yattb,))
                    PSrel(ppvb)
                    yield
                pt, ptb = PS()
                pv = pt[:].bitcast(BF16)
                for c in range(4):
                    TR(pv[:, c * 128:(c + 1) * 128], yatt[:, c * 128:(c + 1) * 128], identb[:], (yattb, cb2), (ptb,))
                CP("act", yaT[:, :, b * 128:(b + 1) * 128], pv[:, 0:512].rearrange("p (c n) -> p c n", n=128), (ptb,), (yaTb,))
                PSrel(ptb)
                yield

            def rwkv_block(b):
                t0 = b * 128
                r_b, k0_b, v_b = rkv[0][0][:, b, :], rkv[1][0][:, b, :], rkv[2][0][:, b, :]
                rB, kB, vB = rkv[0][1], rkv[1][1], rkv[2][1]
                (f_tw, f_twb), (f_al, f_alb), (f_gam, f_gamb), (f_ig, f_igb), (f_gx, f_gxb), (f_kk, f_kkb), \
                    (f_km, f_kmb), (f_s1, f_s1b), (f_s2, f_s2b), (f_yc, f_ycb) = tmpf
                (b_kt, b_ktb), (b_bt, b_btb), (b_at, b_atb), (b_rt, b_rtb), (b_X, b_Xb), (b_U, b_Ub), (b_y, b_yb) = tmpb
                (st_a, st_ab), (st_b, st_bb), (st_c, st_cb), (st_d, st_db), (st_e, st_eb), (st_f, st_fb), _ = sts
                p1, p1b = PS()
                MM(p1[:], lT[0:66, t0:t0 + 128], rhs_w[0:66, :], True, True, (lTb, cb2), (p1b,))
                ACT(f_tw[:], p1[:], AF.Tanh, (p1b,), (f_twb,), scale=0.5)
                PSrel(p1b)
                p2, p2b = PS()
                MM(p2[:], lT[0:66, t0:t0 + 128], rhs_a[0:66, :], True, True, (lTb, cb2), (p2b,))
                ACT(f_s1[:], p2[:], AF.Tanh, (p2b,), (f_s1b,), scale=0.5)
                PSrel(p2b)
                yield
                TS("dve", f_al[:], f_s1[:], 0.5, 0.5, ALU.mult, ALU.add, (f_s1b,), (f_alb,))
                p3, p3b = PS()
                MM(p3[:], tri_i[:], f_tw[:], True, True, (cb, f_twb), (p3b,))
                p4, p4b = PS()
                MM(p4[:], tri_s[:], f_tw[:], True, True, (cb, f_twb), (p4b,))
                ACT(f_gam[:], p3[:], AF.Exp, (p3b, cb), (f_gamb,), scale=-0.5 * C0, bias=tbias[:, 0:1])
                ACT(f_ig[:], p3[:], AF.Exp, (p3b, cb), (f_igb,), scale=0.5 * C0, bias=tbias[:, 1:2])
                ACT(f_gx[:], p4[:], AF.Exp, (p4b, cb), (f_gxb,), scale=-0.5 * C0, bias=tbias[:, 2:3])
                PSrel(p3b, p4b)
                p5, p5b = PS()
                for p_ in range(4):
                    MM(p5[:, p_:p_ + 1], f_tw[:, p_ * 128:(p_ + 1) * 128], onesc[:, 0:1], True, True, (f_twb, cb2), (p5b,))
                ACT(gamC[:, 0:4], p5[:, 0:4], AF.Exp, (p5b, cb), (gamCb,), scale=-0.5 * C0, bias=tbias[:, 3:4])
                PSrel(p5b)
                yield
                TT_("dve", f_kk[:], k0_b, kkb[:], ALU.mult, (kB, cb), (f_kkb,))
                ACT(f_s1[:], f_kk[:], AF.Square, (f_kkb,), (f_s1b,))
                RED(st_a[:, 0:8], v3(f_s1[:]), (f_s1b,), (st_ab,))
                TS("dve", st_a[:, 8:16], st_a[:, 0:8], 1e-24, None, ALU.max, None, (st_ab,), (st_ab,))
                TT_("pool", st_b[:, 0:8], st_a[:, 8:16], nhalf[:, 0:8], ALU.pow, (st_ab, cb2), (st_bb,))
                drip(4)
                TT_("dve", v3(f_kk[:]), v3(f_kk[:]), bc8(st_b[:, 0:8]), ALU.mult, (f_kkb, st_bb), (f_kkb,))
                STT(f_s2[:], f_al[:], -1.0, kab[:], ALU.add, ALU.mult, (f_alb, cb), (f_s2b,))
                STT(f_km[:], f_s2[:], 1.0, k0_b, ALU.add, ALU.mult, (f_s2b, kB), (f_kmb,))
                TT_("dve", b_kt[:], f_km[:], f_ig[:], ALU.mult, (f_kmb, f_igb), (b_ktb,))
                TT_("dve", f_s2[:], f_kk[:], f_al[:], ALU.mult, (f_kkb, f_alb), (f_s2b,))
                TT_("dve", b_bt[:], f_s2[:], f_ig[:], ALU.mult, (f_s2b, f_igb), (b_btb,))
                STT(b_at[:], f_kk[:], -1.0, f_gx[:], ALU.mult, ALU.mult, (f_kkb, f_gxb), (b_atb,))
                TT_("dve", b_rt[:], r_b, f_gam[:], ALU.mult, (rB, f_gamb), (b_rtb,))
                TT_("dve", f_s1[:], r_b, f_km[:], ALU.mult, (rB, f_kmb), (f_s1b,))
                TT_("dve", f_s1[:], f_s1[:], rkb[:], ALU.mult, (f_s1b, cb), (f_s1b,))
                RED(st_c[:, 0:8], v3(f_s1[:]), (f_s1b,), (st_cb,))
                yield
                for (xa, xab, xr, xrb, dst, dstb) in ((b_at, b_atb, b_rt, b_rtb, arT, arTb), (b_bt, b_btb, b_kt, b_ktb, bkT, bkTb)):
                    pt, ptb = PS()
                    pv = pt[:].bitcast(BF16)
                    for p_ in range(4):
                        TR(pv[:, p_ * 128:(p_ + 1) * 128], xa[:, p_ * 128:(p_ + 1) * 128], identb[:], (xab, cb2), (ptb,))
                    for p_ in range(4):
                        TR(pv[:, 512 + p_ * 128:512 + (p_ + 1) * 128], xr[:, p_ * 128:(p_ + 1) * 128], identb[:], (xrb, cb2), (ptb,))
                    pv4 = pv.rearrange("d (a p t) -> d a p t", a=2, p=4)
                    CP("act", dst[:].rearrange("d p a t -> d a p t"), pv4, (ptb,), (dstb,))
                    for q_ in range(2):
                        rows = slice(q_ * 64, q_ * 64 + 64)
                        if dst is arT:
                            CP("dve", arZ[rows, :, q_, :, :].rearrange("d p a t -> d a p t"), pv4[rows], (ptb,), (arZb,))
                        else:
                            CP("dve", bZ[rows, :, q_, :], pv4[rows, 0, :, :], (ptb,), (bZb,))
                    PSrel(ptb)
                yield
                for p_ in range(4):
                    rhs = arZ[:, p_, :, :, :].rearrange("k q a t -> k (q a t)")
                    for (x_, SCx) in ((0, SCb), (1, SCk)):
                        psc, pscb = PS()
                        MM(psc[:], bkT[:, p_, x_, :], rhs, True, True, (bkTb, arZb), (pscb,))
                        TT_("dve", SCx[p_][0][:].rearrange("s q a t -> s (q a t)"), psc[:], mask4[:],
                            ALU.mult, (pscb, cb2), (SCx[p_][1],))
                        PSrel(pscb)
                    if p_ % 2 == 1:
                        yield
                for hg in range(2):
                    pn, pnb = PS()
                    for pp_ in range(2):
                        p_ = hg * 2 + pp_
                        MM(pn[:, pp_ * 256:(pp_ + 1) * 256], arT[:, p_, 0, :], bZ[:, p_, :, :].rearrange("k q s -> k (q s)"),
                           True, True, (arTb, bZb), (pnb,))
                    TT_("dve", PTm[0][hg][0][:], v3(pn[:], 128), maskl[:].unsqueeze(1).to_broadcast([128, 4, 128]), ALU.mult,
                        (pnb, cb2), (PTm[0][hg][1],))
                    PSrel(pnb)
                    for pp_ in range(2):
                        p_ = hg * 2 + pp_
                        TT_("dve", Qm[0][hg][0][:, 2 * pp_:2 * pp_ + 2, :], SCb[p_][0][:, :, 0, :],
                            identb[:].unsqueeze(1).to_broadcast([128, 2, 128]), ALU.add, (SCb[p_][1], cb2), (Qm[0][hg][1],))
                for l in range(7):
                    for hg in range(2):
                        def Pl(hh):
                            if l == 0:
                                h = hg * 4 + hh
                                return SCb[h // 2][0][:, h % 2, 0, :], SCb[h // 2][1]
                            return Pm[l % 2][hg][0][:, hh, :], Pm[l % 2][hg][1]
                        PTl, PTlb = PTm[l % 2][hg]
                        if l <= 4:
                            pP, pPb = PS()
                            for hh in range(4):
                                ap, bf = Pl(hh)
                                MM(pP[:, hh * 128:(hh + 1) * 128], PTl[:, hh, :], ap, True, True, (PTlb, bf), (pPb,))
                        if l <= 5:
                            pT_, pTb_ = PS()
                            for hh in range(4):
                                ap, bf = Pl(hh)
                                MM(pT_[:, hh * 128:(hh + 1) * 128], ap, PTl[:, hh, :], True, True, (bf, PTlb), (pTb_,))
                        if l >= 1:
                            pQ, pQb = PS()
                            Qp, Qpb = Qm[(l - 1) % 2][hg]
                            for hh in range(4):
                                MM(pQ[:, hh * 128:(hh + 1) * 128], PTl[:, hh, :], Qp[:, hh, :], True, True, (PTlb, Qpb), (pQb,))
                        if l <= 4:
                            CP("act", Pm[(l + 1) % 2][hg][0][:], v3(pP[:], 128), (pPb,), (Pm[(l + 1) % 2][hg][1],))
                            PSrel(pPb)
                        if l <= 5:
                            CP("act", PTm[(l + 1) % 2][hg][0][:], v3(pT_[:], 128), (pTb_,), (PTm[(l + 1) % 2][hg][1],))
                            PSrel(pTb_)
                        if l >= 1:
                            TT_("dve", Qm[l % 2][hg][0][:], v3(pQ[:], 128), Qp[:], ALU.add, (pQb, Qpb), (Qm[l % 2][hg][1],))
                            PSrel(pQb)
                        yield
                def hd(h):
                    return h // 2, h % 2, slice(h * 64, (h + 1) * 64)
                pX, pXb = PS()
                for p_ in range(4):
                    MM(pX[:, p_ * 128:(p_ + 1) * 128], arT[:, p_, 0, :], Hz[:, p_, :], True, False, (arTb, Hzb), (pXb,))
                    for h in (2 * p_, 2 * p_ + 1):
                        _, q_, cs = hd(h)
                        MM(pX[:, cs], SCk[p_][0][:, q_, 0, :], v_b[:, cs], False, h % 2 == 1, (SCk[p_][1], vB), (pXb,))
                CP("act", b_X[:], pX[:], (pXb,), (b_Xb,))
                PSrel(pXb)
                yield
                pU, pUb = PS()
                for h in range(8):
                    p_, q_, cs = hd(h)
                    MM(pU[:, cs], Qm[0][h // 4][0][:, h % 4, :], b_X[:, cs], True, True, (Qm[0][h // 4][1], b_Xb), (pUb,))
                CP("dve", b_U[:], pU[:], (pUb,), (b_Ub,))
                PSrel(pUb)
                yield
                pY, pYb = PS()
                for p_ in range(4):
                    MM(pY[:, p_ * 128:(p_ + 1) * 128], arT[:, p_, 1, :], Hz[:, p_, :], True, False, (arTb, Hzb), (pYb,))
                    for h in (2 * p_, 2 * p_ + 1):
                        _, q_, cs = hd(h)
                        MM(pY[:, cs], SCb[p_][0][:, q_, 1, :], b_U[:, cs], False, False, (SCb[p_][1], b_Ub), (pYb,))
                        MM(pY[:, cs], SCk[p_][0][:, q_, 1, :], v_b[:, cs], False, h % 2 == 1, (SCk[p_][1], vB), (pYb,))
                pD, pDb = PS()
                for p_ in range(4):
                    ps_ = slice(p_ * 128, (p_ + 1) * 128)
                    MM(pD[:, ps_], b_bt[:, ps_], b_U[:, ps_], True, False, (b_btb, b_Ub), (pDb,))
                    MM(pD[:, ps_], b_kt[:, ps_], v_b[:, ps_], False, True, (b_ktb, vB), (pDb,))
                for q_ in range(2):
                    rows = slice(q_ * 64, q_ * 64 + 64)
                    TT_("dve", v3(Hf[rows, :]), v3(pD[rows, :], 128)[:, :, q_ * 64:(q_ + 1) * 64], v3(Hf[rows, :]), ALU.add,
                        (pDb, Hfb), (Hfb,))
                TT_("dve", v3(Hf[:]), v3(Hf[:]), gamC[:, 0:4].unsqueeze(2).to_broadcast([128, 4, 64]), ALU.mult,
                    (Hfb, gamCb), (Hfb,))
                PSrel(pDb)
                for q_ in range(2):
                    rows = slice(q_ * 64, q_ * 64 + 64)
                    CP("act", Hz[rows, :, q_ * 64:(q_ + 1) * 64], v3(Hf[rows, :]), (Hfb,), (Hzb,))
                yield
                RED(st_d[:, 0:8], v3(pY[:]), (pYb,), (st_db,))
                TS("dve", st_d[:, 8:16], st_d[:, 0:8], -1.0 / 64, None, ALU.mult, None, (st_db,), (st_db,))
                TT_("dve", v3(f_yc[:]), v3(pY[:]), bc8(st_d[:, 8:16]), ALU.add, (pYb, st_db), (f_ycb,))
                PSrel(pYb)
                ACT(f_s1[:], f_yc[:], AF.Square, (f_ycb,), (f_s1b,))
                RED(st_e[:, 0:8], v3(f_s1[:]), (f_s1b,), (st_eb,))
                TS("dve", st_e[:, 8:16], st_e[:, 0:8], 1.0 / 64, GN_EPS, ALU.mult, ALU.add, (st_eb,), (st_eb,))
                TT_("pool", st_f[:, 0:8], st_e[:, 8:16], nhalf[:, 0:8], ALU.pow, (st_eb, cb2), (st_fb,))
                drip(4)
                TT_("dve", v3(f_yc[:]), v3(f_yc[:]), bc8(st_f[:, 0:8]), ALU.mult, (f_ycb, st_fb), (f_ycb,))
                TT_("dve", f_yc[:], f_yc[:], lnwb[:], ALU.mult, (f_ycb, cb), (f_ycb,))
                TT_("dve", f_yc[:], f_yc[:], lnbb[:], ALU.add, (f_ycb, cb), (f_ycb,))
                TT_("dve", v3(f_s1[:]), v3(v_b), bc8(st_c[:, 0:8]), ALU.mult, (vB, st_cb), (f_s1b,))
                TT_("dve", f_yc[:], f_yc[:], f_s1[:], ALU.add, (f_ycb, f_s1b), (f_ycb,))
                pG, pGb = PS()
                MM(pG[:], gTt[0:96, t0:t0 + 128], gl_t[0:96, :], True, True, (gTb, cb2), (pGb,))
                TT_("dve", b_y[:], f_yc[:], pG[:], ALU.mult, (f_ycb, pGb), (b_yb,))
                PSrel(pGb)
                yield
                pt, ptb = PS()
                pv = pt[:].bitcast(BF16)
                for c in range(4):
                    TR(pv[:, c * 128:(c + 1) * 128], b_y[:, c * 128:(c + 1) * 128], identb[:], (b_yb, cb2), (ptb,))
                CP("act", yrT[:, :, t0:t0 + 128], pv[:, 0:512].rearrange("p (c n) -> p c n", n=128), (ptb,), (yrTb,))
                PSrel(ptb)
                yield

            def mixer_pre(ti, xt, xtb):
                run_all(norm_T(xt, xtb, hT, hTb, 8, gmc))
                CP("dve", hT[:, :, 7:8], hcar[:], (hcarb,), (hTb,))
                CP("dve", hcar[:], hT[:, :, 519:520], (hTb,), (hcarb,))
                rt, rb_ = wget(("lora",))
                w = rt[:, 0:1024].rearrange("p (k m) -> p k m", m=64)
                pa, pab = PS()
                for kc in range(16):
                    MM(pa[0:64, :], w[:, kc, :], rhs_k(kc), kc == 0, kc == 15, (rb_, hTb), (pab,))
                ACT(lT[0:32, :], pa[0:32, :], AF.Copy, (pab,), (lTb,))
                ACT(lT[32:64, :], pa[32:64, :], AF.Tanh, (pab,), (lTb,))
                PSrel(pab)
                rt, rb_ = wget(("xg",))
                w = rt[:, 0:1536].rearrange("p (k m) -> p k m", m=96)
                pg, pgb = PS()
                for kc in range(16):
                    MM(pg[0:96, :], w[:, kc, :], rhs_k(kc), kc == 0, kc == 15, (rb_, hTb), (pgb,))
                ACT(tmpf[0][0][0:96, :], pg[0:96, :], AF.Tanh, (pgb,), (tmpf[0][1],), scale=0.5)
                PSrel(pgb)
                TS("dve", gTt[:], tmpf[0][0][0:96, :], 1.0, None, ALU.add, None, (tmpf[0][1],), (gTb,))
                for m in range(4):
                    rt, rb_ = wget(("q", m))
                    w = rt[:, 0:1024].rearrange("p (k m) -> p k m", m=128)
                    pq, pqb = PS()
                    for k in range(8):
                        MM(pq[:], w[:, k, :], rhs_k(k), k == 0, k == 7, (rb_, hTb), (pqb,))
                    qk_norm(pq, pqb, gqc[:, 0:1], qT[:, m, :], qTb)
                CP("pool", kTd[:, :, :, 0:128], kTd[:, :, :, 512:640], (kTdb,), (kTdb,))
                for g in range(2):
                    rt, rb_ = wget(("kd", g))
                    w = rt[:, 0:1024].rearrange("p (k m) -> p k m", m=128)
                    pq, pqb = PS()
                    for k in range(8):
                        MM(pq[:], w[:, k, :], rhs_k(k), k == 0, k == 7, (rb_, hTb), (pqb,))
                    qk_norm(pq, pqb, gk8[:, 0:1], None, kTdb,
                            dsts=[(slice(0, 64), kTd[0:64, g, 0, 128:640]), (slice(64, 128), kTd[64:128, g, 1, 128:640])])
                for c in range(3):
                    pbs = [PS() for _ in range(4)]
                    for q in range(4):
                        rt, rb_ = wget(("rkv", c, q))
                        w = rt[:].rearrange("p (k n) -> p k n", n=512)
                        for kk in range(4):
                            kc = q * 4 + kk
                            for b in range(4):
                                MM(pbs[b][0][:], rhs_k(kc, b * 128, 128), w[:, kk, :], kc == 0, kc == 15, (hTb, rb_), (pbs[b][1],))
                    for b in range(4):
                        CP("act" if b % 2 else "dve", rkv[c][0][:, b, :], pbs[b][0][:], (pbs[b][1],), (rkv[c][1],))
                        PSrel(pbs[b][1])
                rt, rb_ = wget(("av",))
                w = rt[:, 0:1024].rearrange("p (k n) -> p k n", n=128)
                pv_, pvb_ = PS()
                for b in range(4):
                    for k in range(8):
                        MM(pv_[:, b * 128:(b + 1) * 128], rhs_k(k, b * 128, 128), w[:, k, :], k == 0, k == 7, (hTb, rb_), (pvb_,))
                CP("pool", vaug[:, 0, :, :], vaug[:, 4, :, :], (vaugb,), (vaugb,))
                CP("act", vaug[:, 1:5, :, 0:64], pv_[:].rearrange("p (b g d) -> p b g d", b=4, g=2), (pvb_,), (vaugb,))
                PSrel(pvb_)

            def mixer_blocks(ti):
                for b in range(4):
                    if "attn" not in skip:
                        yield from tagged(attn_block(b, ti * 4 + b), "attn")
                    if "rwkv" not in skip:
                        yield from tagged(rwkv_block(b), "rwkv")

            def mixer_post(ti, xt, xtb):
                for m in range(8):
                    rt, rb_ = wget(("gate", m))
                    gw = rt[:].rearrange("p (t k n) -> p t k n", t=2, k=8)
                    pgr, pgrb = PS()
                    pga, pgab = PS()
                    for k in range(8):
                        MM(pgr[:], gw[:, 0, k, :], rhs_k(k), k == 0, k == 7, (rb_, hTb), (pgrb,))
                    for k in range(8):
                        MM(pga[:], gw[:, 1, k, :], rhs_k(k), k == 0, k == 7, (rb_, hTb), (pgab,))
                    rt2, rb2 = wget(("br", m))
                    bw = rt2[:, 0:1024].rearrange("p (t c n) -> p t c n", t=2, c=4)
                    pbr, pbrb = PS()
                    pba, pbab = PS()
                    for c in range(4):
                        MM(pbr[:], bw[:, 0, c, :], yrT[:, c, :], c == 0, c == 3, (rb2, yrTb), (pbrb,))
                    for c in range(4):
                        MM(pba[:], bw[:, 1, c, :], yaT[:, c, :], c == 0, c == 3, (rb2, yaTb), (pbab,))
                    (fa, fab), (fb, fbb), (fc, fcb), (fd, fdb) = tmpf[0:4]
                    ACT(fa[:], pgr[:], AF.Tanh, (pgrb,), (fab,), scale=0.5)
                    ACT(fb[:], pga[:], AF.Tanh, (pgab,), (fbb,), scale=0.5)
                    STT(fc[:], fa[:], 1.0, pbr[:], ALU.add, ALU.mult, (fab, pbrb), (fcb,))
                    STT(fd[:], fb[:], 1.0, pba[:], ALU.add, ALU.mult, (fbb, pbab), (fdb,))
                    PSrel(pgrb, pgab, pbrb, pbab)
                    TT_("dve", mgT[:, m, :], fc[:], fd[:], ALU.add, (fcb, fdb), (mgTb,))
                for h in range(2):
                    pbs = [PS() for _ in range(4)]
                    for q in range(2):
                        rt, rb_ = wget(("wo", h, q))
                        w = rt[:].rearrange("p (k n) -> p k n", n=512)
                        for mm in range(4):
                            m = q * 4 + mm
                            for s in range(4):
                                MM(pbs[s][0][:], mgT[:, m, s * 128:(s + 1) * 128], w[:, mm, :], m == 0, m == 7, (mgTb, rb_), (pbs[s][1],))
                    for s in range(4):
                        xs = xt[:, s, h * 512:(h + 1) * 512]
                        STT(xs, pbs[s][0][:], 0.5, xs, ALU.mult, ALU.add, (pbs[s][1], xtb), (xtb,))
                        PSrel(pbs[s][1])

        xv = dr["x"].rearrange("(t s p) d -> t p s d", s=4, p=128)
        yv = y_out.rearrange("(t s p) d -> t p s d", s=4, p=128)

        def finish(ti):
            xt, xtb = xb[ti % 3]
            rms_stats(xt, xtb)
            for s in range(4):
                STT(xt[:, s, :], xt[:, s, :], rstd[:, s:s + 1], fgain[:], ALU.mult, ALU.mult, (xtb, rstdb, cb), (xtb,))
            DMA("act", yv[ti], xt[:], (xtb,), (), xtb)
            yield

        def side_stream(ti):
            if ti >= 1:
                if do_ffn2:
                    yield from tagged(ffn(1, xb[(ti - 1) % 3][0], xb[(ti - 1) % 3][1]), "ffn2")
                yield from tagged(finish(ti - 1), "fin")
            if ti + 1 < NT and do_ffn1:
                yield from tagged(ffn(0, xb[(ti + 1) % 3][0], xb[(ti + 1) % 3][1]), "ffn1")

        DMA("sp", xb[0][0][:], xv[0], (), (xb[0][1],), xb[0][1])
        interleave(tagged(ffn(0, xb[0][0], xb[0][1]), "ffn1") if do_ffn1 else iter(()),
                   tagged(staged_prep(), "prep") if do_mix else None, 0.3)
        P.barrier()
        staged_ready["v"] = True
        for ti in range(NT):
            xt, xtb = xb[ti % 3]
            if ti + 1 < NT:
                xn, xnb = xb[(ti + 1) % 3]
                DMA("sp", xn[:], xv[ti + 1], (), (xnb,), xnb)
            side = side_stream(ti)
            if do_mix:
                P.phase = "mix"
                mixer_pre(ti, xt, xtb)
                if overlap:
                    interleave(mixer_blocks(ti), side, ratio)
                else:
                    run_all(mixer_blocks(ti))
                    run_all(side)
                P.phase = "branch"
                mixer_post(ti, xt, xtb)
            else:
                run_all(side)
        if do_ffn2:
            run_all(tagged(ffn(1, xb[(NT - 1) % 3][0], xb[(NT - 1) % 3][1]), "ffn2"))
        run_all(tagged(finish(NT - 1), "fin"))

        if order is None:
            return None, None, rec
        P.resolve()
        P.prepare(nc, st)
        with nc.Block() as block:
            P.emit(block)
    return nc, P, None


def build(T=4096, **kw):
    _, _, rec = _build(T, order=None, **kw)
    nc, P, _ = _build(T, order=rec, **kw)
    return nc, P


def host_consts():
    s = np.arange(128)[:, None]
    t = np.arange(128)[None, :]
    tri_i = (s <= t).astype(np.float32)
    tri_s = (s < t).astype(np.float32)
    c = {
        "c_ident": np.eye(128, dtype=np.float32),
        "c_tri_i": tri_i,
        "c_tri_s": tri_s,
        "c_blk": ((s // 64) == (t // 64)).astype(np.float32),
        "c_mask4": np.concatenate([tri_s, tri_i, tri_s, tri_i], axis=1),
        "c_maskl": (s > t).astype(np.float32),
        "c_amo": tri_i.copy(),
        "c_amp": (s > t).astype(np.float32),
    }
    tt = np.arange(128, dtype=np.float64)
    c["c_tb"] = np.stack([-0.5 * C0 * (tt + 1), 0.5 * C0 * (tt + 1), -0.5 * C0 * tt,
                          np.full(128, -0.5 * C0 * 128)], axis=1).astype(np.float32)
    return c


def make_in_map(inp, xs):
    f = lambda a: np.ascontiguousarray(np.asarray(a, dtype=np.float32))
    col = lambda g: f(np.asarray(g).reshape(8, 128).T)
    m = {
        "x": f(xs),
        "w_g1": f(inp["ffn1_w_gate"][0]), "w_u1": f(inp["ffn1_w_up"][0]), "w_d1": f(inp["ffn1_w_down"][0]),
        "w_g2": f(inp["ffn2_w_gate"][0]), "w_u2": f(inp["ffn2_w_up"][0]), "w_d2": f(inp["ffn2_w_down"][0]),
        "w_in": f(inp["w_in"][0]),
        "w_br": f(inp["w_branch_rwkv"][0]), "w_ba": f(inp["w_branch_attn"][0]), "w_out": f(inp["w_out"][0]),
        "g1c": col(inp["ffn1_norm"][0]), "gmc": col(inp["mix_norm"][0]), "g2c": col(inp["ffn2_norm"][0]),
        "fgain": f(np.asarray(inp["final_norm"][0]).reshape(1, D)),
        "mu": f(np.asarray(inp["rwkv_mu"][0]).reshape(1, 1696)),
        "p_kk": f(np.asarray(inp["rwkv_k_k"][0]).reshape(1, 512)),
        "p_ka": f(np.asarray(inp["rwkv_k_a"][0]).reshape(1, 512)),
        "p_rk": f(np.asarray(inp["rwkv_r_k"][0]).reshape(1, 512)),
        "p_lnw": f(np.asarray(inp["rwkv_ln_w"][0]).reshape(1, 512)),
        "p_lnb": f(np.asarray(inp["rwkv_ln_b"][0]).reshape(1, 512)),
        "p_w0": f(np.asarray(inp["rwkv_w0"][0]).reshape(1, 512)),
        "p_a0": f(np.asarray(inp["rwkv_a0"][0]).reshape(1, 512)),
        "p_wl": f(inp["rwkv_w_lora_up"][0]), "p_al": f(inp["rwkv_a_lora_up"][0]), "p_gl": f(inp["rwkv_g_lora_up"][0]),
        "gqc": f(np.tile(np.asarray(inp["attn_q_norm"][0]).reshape(64), 2).reshape(128, 1)),
        "gkc": f(np.tile(np.asarray(inp["attn_k_norm"][0]).reshape(64), 2).reshape(128, 1)),
        "sinks": f(np.asarray(inp["attn_sinks"][0]).reshape(1, 8)),
    }
    m.update(host_consts())
    return m


_CACHE = {}


def kernel(**inputs):
    x = np.asarray(inputs["x"], dtype=np.float32)
    B, T, _ = x.shape
    key = (T,)
    if key not in _CACHE:
        _CACHE[key] = build(T)[0]
    nc = _CACHE[key]
    in_maps = [make_in_map(inputs, x[b]) for b in range(B)]
    res = run_bass_kernel_spmd(nc, in_maps, core_ids=list(range(B)))
    return np.stack([np.asarray(r["y"], dtype=np.float32) for r in res.results], axis=0)
```

```python
import math
from contextlib import ExitStack

import numpy as np
import concourse.bass as bass
import concourse.mybir as mybir
from concourse.bass_utils import run_bass_kernel_spmd

F32 = mybir.dt.float32
BF16 = mybir.dt.bfloat16
AF = mybir.ActivationFunctionType
ALU = mybir.AluOpType
AX = mybir.AxisListType

D = 1024
DFF = 2816
NF = 22
TT = 512
RW = 512
C0 = math.exp(-0.5)
RMS_EPS = 1e-6
GN_EPS = 64e-5
RING = 4
SLOT = 2048
VW = 80


class Buf:
    __slots__ = ("name", "last_w", "readers", "sem", "dma_cnt")

    def __init__(self, name):
        self.name = name
        self.last_w = None
        self.readers = []
        self.sem = None
        self.dma_cnt = 0


class Op:
    __slots__ = ("eng", "fn", "reads", "writes", "dma", "track", "deps", "signal", "tok", "barrier", "phase")

    def __init__(self, eng, fn, reads, writes, dma, track):
        self.eng = eng
        self.fn = fn
        self.reads = reads
        self.writes = writes
        self.dma = dma
        self.track = track
        self.deps = []
        self.signal = False
        self.tok = None
        self.barrier = False


ENGS = ("pe", "act", "dve", "pool", "sp")


class Prog:
    def __init__(self):
        self.ops = []
        self.bufs = []
        self.phase = "setup"

    def buf(self, name):
        b = Buf(name)
        self.bufs.append(b)
        return b

    def add(self, eng, fn, reads=(), writes=(), dma=False, track=None):
        op = Op(eng, fn, tuple(reads), tuple(writes), dma, track)
        op.phase = self.phase
        self.ops.append(op)
        return op

    def barrier(self):
        for e in ENGS:
            op = Op(e, None, (), (), False, None)
            op.barrier = True
            op.phase = self.phase
            self.ops.append(op)

    def resolve(self):
        ops = self.ops
        last_on_eng = {e: None for e in ENGS}
        dma_ops = []
        for i, op in enumerate(ops):
            if op.barrier:
                deps = set()
                for e in ENGS:
                    if e != "sp" and last_on_eng[e] is not None and e != op.eng:
                        deps.add(last_on_eng[e])
                deps.update(dma_ops)
                op.deps = sorted(deps)
                for j in op.deps:
                    ops[j].signal = True
                continue
            deps = set()

            def need(j, kind):
                p = ops[j]
                if p.dma or op.dma:
                    if p.dma and op.dma and kind == "waw" and p.track is op.track:
                        return
                    deps.add(j)
                    return
                if p.eng == op.eng:
                    if op.eng == "pe":
                        return
                deps.add(j)

            for b in op.reads:
                if b.last_w is not None:
                    need(b.last_w, "raw")
            for b in op.writes:
                if b.last_w is not None:
                    need(b.last_w, "waw")
                lastr = {}
                for j in b.readers:
                    p = ops[j]
                    if p.dma:
                        need(j, "war")
                    else:
                        lastr[p.eng] = j
                for j in lastr.values():
                    need(j, "war")
            for b in op.reads:
                b.readers.append(i)
            for b in op.writes:
                b.last_w = i
                b.readers = []
            op.deps = sorted(deps)
            for j in op.deps:
                ops[j].signal = True
            if op.dma:
                dma_ops.append(i)
            else:
                last_on_eng[op.eng] = i

    def prepare(self, nc, stack):
        ops = self.ops
        esem = {}
        for e in ("pe", "act", "dve", "pool"):
            esem[e] = stack.enter_context(nc.semaphore("es_" + e))
        for b in self.bufs:
            b.dma_cnt = 0
        tracked = []
        for op in ops:
            if op.dma:
                t = op.track
                if t.sem is None:
                    t.sem = stack.enter_context(nc.semaphore("ds_" + t.name))
                    tracked.append(t)
        cnt = {e: 0 for e in ENGS}
        for op in ops:
            if op.barrier:
                continue
            if op.dma:
                op.track.dma_cnt += 1
                op.tok = (op.track.sem, 16 * op.track.dma_cnt)
            elif op.signal:
                cnt[op.eng] += 1
                op.tok = (esem[op.eng], cnt[op.eng])
        know = {e: {} for e in ENGS}
        clocks = {}
        nw = 0
        for i, op in enumerate(ops):
            kn = know[op.eng]
            w = {}
            for j in op.deps:
                s, v = ops[j].tok
                if kn.get(s, 0) < v and w.get(s, (0, None))[0] < v:
                    w[s] = (v, j)
            waits = []
            for s, (v, j) in w.items():
                if kn.get(s, 0) >= v:
                    continue
                waits.append((s, v))
                for s2, v2 in clocks[j].items():
                    if kn.get(s2, 0) < v2:
                        kn[s2] = v2
                kn[s] = max(kn.get(s, 0), v)
            op.deps = waits
            nw += len(waits)
            if op.tok is not None:
                c = dict(kn)
                if not op.dma:
                    c[op.tok[0]] = op.tok[1]
                else:
                    c[op.tok[0]] = max(c.get(op.tok[0], 0), op.tok[1])
                clocks[i] = c
        self.nwaits = nw
        per = {e: [] for e in ENGS}
        for op in ops:
            per[op.eng].append(op)
        self.stats = {e: len(per[e]) for e in ENGS}
        self._emit_state = (ops, esem, tracked, per)

    def emit(self, block):
        ops, esem, tracked, per = self._emit_state

        def run(name, e):
            seen = {}
            for op in per[name]:
                for s, v in op.deps:
                    if seen.get(s, 0) < v:
                        e.wait_ge(s, v)
                        seen[s] = v
                if op.barrier:
                    continue
                ins = op.fn(e)
                if op.dma:
                    ins.then_inc(op.track.sem, 16)
                elif op.signal:
                    ins.then_inc(esem[name], 1)
            if name == "sp":
                for t in tracked:
                    if seen.get(t.sem, 0) < 16 * t.dma_cnt:
                        e.wait_ge(t.sem, 16 * t.dma_cnt)

        @block.sync
        def _(e):
            run("sp", e)

        @block.tensor
        def _(e):
            run("pe", e)

        @block.scalar
        def _(e):
            run("act", e)

        @block.vector
        def _(e):
            run("dve", e)

        @block.gpsimd
        def _(e):
            run("pool", e)


W_SPECS = [
    ("x", None, F32),
    ("w_g1", [D, DFF], F32), ("w_u1", [D, DFF], F32), ("w_d1", [DFF, D], F32),
    ("w_g2", [D, DFF], F32), ("w_u2", [D, DFF], F32), ("w_d2", [DFF, D], F32),
    ("w_in", [D, 4512], F32),
    ("w_br", [512, D], F32), ("w_ba", [512, D], F32), ("w_out", [D, D], F32),
    ("g1c", [128, 8], F32), ("gmc", [128, 8], F32), ("g2c", [128, 8], F32),
    ("fgain", [1, D], F32), ("mu", [1, 1696], F32),
    ("p_kk", [1, 512], F32), ("p_ka", [1, 512], F32), ("p_rk", [1, 512], F32),
    ("p_lnw", [1, 512], F32), ("p_lnb", [1, 512], F32),
    ("p_w0", [1, 512], F32), ("p_a0", [1, 512], F32),
    ("p_wl", [32, 512], F32), ("p_al", [32, 512], F32), ("p_gl", [96, 512], F32),
    ("gqc", [128, 1], F32), ("gkc", [128, 1], F32), ("sinks", [1, 8], F32),
    ("c_ident", [128, 128], F32), ("c_tri_i", [128, 128], F32), ("c_tri_s", [128, 128], F32),
    ("c_blk", [128, 128], F32), ("c_mask4", [128, 512], F32), ("c_maskl", [128, 128], F32),
    ("c_amo", [128, 128], F32), ("c_amp", [128, 128], F32), ("c_tb", [128, 4], F32),
]


def _build(T=4096, do_ffn1=True, do_mix=True, do_ffn2=True, dbg=None, skip=(), stage=9, order=None, overlap=True, ratio=1.0):
    NT = T // TT
    nc = bass.Bass("TRN2", target_bir_lowering=False)
    P = Prog()
    dr = {}
    for name, shp, dt in W_SPECS:
        if name == "x":
            shp = [T, D]
        dr[name] = nc.dram_tensor(name, shp, dt, kind="ExternalInput").ap()
    y_out = nc.dram_tensor("y", [T, D], F32, kind="ExternalOutput").ap()
    dbg_out = None
    if dbg:
        dbg_out = nc.dram_tensor("dbg", [128, dbg], F32, kind="ExternalOutput").ap()

    def scr(name, shape):
        return nc.dram_tensor(name, shape, BF16, kind="Internal").ap()

    s_gu = [scr("s_gu%d" % i, [NF, 128, 2048]) for i in range(2)]
    s_d = [scr("s_d%d" % i, [2, NF, 128, 512]) for i in range(2)]
    s_lora = scr("s_lora", [128, 16 * 64])
    s_xg = scr("s_xg", [128, 16 * 96])
    s_q = scr("s_q", [4, 128, 1024])
    s_kd = scr("s_kd", [2, 128, 1024])
    s_gate = scr("s_gate", [8, 128, 2048])
    s_rkv = scr("s_rkv", [3, 4, 128, 2048])
    s_av = scr("s_av", [128, 1024])
    s_br = scr("s_br", [8, 128, 1024])
    s_wo = scr("s_wo", [2, 2, 128, 2048])

    with ExitStack() as st:
        def sb(name, shape, dt=F32):
            return st.enter_context(nc.sbuf_tensor(name, shape, dt))

        def sbb(name, shape, dt=F32):
            return sb(name, shape, dt), P.buf(name)

        banks = []
        for i in range(8):
            t = st.enter_context(nc.psum_tensor("pb%d" % i, [128, 512], F32))
            banks.append((t, P.buf("pb%d" % i)))
        bank_free = list(range(8))
        bank_of = {b_: i_ for i_, (_, b_) in enumerate(banks)}

        def PS():
            assert bank_free, "out of PSUM banks (too many concurrently open)"
            return banks[bank_free.pop(0)]

        def PSrel(*bufs_):
            for b_ in bufs_:
                assert bank_of[b_] not in bank_free
                bank_free.append(bank_of[b_])

        def MM(out, lhsT, rhs, start, stop, reads, writes):
            P.add("pe", lambda e: e.matmul(out, lhsT=lhsT, rhs=rhs, start=start, stop=stop), reads, writes)

        def TR(out, in_, ident, reads, writes):
            P.add("pe", lambda e: e.transpose(out=out, in_=in_, identity=ident), reads, writes)

        def ACT(out, in_, func, reads, writes, scale=None, bias=None, accum=None):
            kw = {}
            if scale is not None:
                kw["scale"] = scale
            if bias is not None:
                kw["bias"] = bias
            if accum is not None:
                kw["accum_out"] = accum
            P.add("act", lambda e: e.activation(out=out, in_=in_, func=func, **kw), reads, writes)

        def TT_(eng, out, in0, in1, op, reads, writes):
            P.add(eng, lambda e: e.tensor_tensor(out=out, in0=in0, in1=in1, op=op), reads, writes)

        def TS(eng, out, in0, s1, s2, op0, op1, reads, writes):
            if op1 is None:
                P.add(eng, lambda e: e.tensor_scalar(out=out, in0=in0, scalar1=s1, scalar2=None, op0=op0), reads, writes)
            else:
                P.add(eng, lambda e: e.tensor_scalar(out=out, in0=in0, scalar1=s1, scalar2=s2, op0=op0, op1=op1), reads, writes)

        def STT(out, in0, scalar, in1, op0, op1, reads, writes):
            P.add("dve", lambda e: e.scalar_tensor_tensor(out=out, in0=in0, scalar=scalar, in1=in1, op0=op0, op1=op1), reads, writes)

        def CP(eng, out, in_, reads, writes):
            if eng == "act":
                ACT(out, in_, AF.Copy, reads, writes)
            else:
                P.add(eng, lambda e: e.tensor_copy(out=out, in_=in_), reads, writes)

        def RED(out, in_, reads, writes):
            P.add("dve", lambda e: e.tensor_reduce(out=out, in_=in_, axis=AX.X, op=ALU.add), reads, writes)

        def DMA(eng, out, in_, reads, writes, track):
            P.add(eng, lambda e: e.dma_start(out=out, in_=in_), reads, writes, dma=True, track=track)

        def MEMSET(eng, ap, val, writes):
            P.add(eng, lambda e: e.memset(ap, val), (), writes)

        ARENA = 23808
        arena = sb("arena", [128, ARENA], BF16)
        aoff = [0]

        def carve(name, shape, dt=F32):
            n = 1
            for s_ in shape[1:]:
                n *= s_
            ne = n * (2 if dt == F32 else 1)
            assert aoff[0] + ne <= ARENA, (name, aoff[0], ne)
            ap = arena[0:shape[0], aoff[0]:aoff[0] + ne]
            aoff[0] += ne
            if dt == F32:
                ap = ap.bitcast(F32)
            if len(shape) == 3:
                ap = ap.rearrange("p (a b) -> p a b", b=shape[2])
            elif len(shape) == 4:
                ap = ap.rearrange("p (a b c) -> p a b c", b=shape[2], c=shape[3])
            return ap

        def carveb(name, shape, dt=F32):
            return carve(name, shape, dt), P.buf(name)

        cb = P.buf("consts")

        def cload(name, shape, src=None, bcast=False, temp=False):
            t = carve("c_" + name, shape, F32) if temp else sb("c_" + name, shape, F32)
            s = dr[name] if src is None else src
            if bcast:
                DMA("sp", t[:], s.partition_broadcast(shape[0]), (), (cb,), cb)
            else:
                DMA("sp", t[:], s[:, :], (), (cb,), cb)
            return t

        identf = cload("c_ident", [128, 128], temp=True)
        tri_i = cload("c_tri_i", [128, 128])
        tri_s = cload("c_tri_s", [128, 128])
        blk = cload("c_blk", [128, 128])
        mask4f = cload("c_mask4", [128, 512], temp=True)
        masklf = cload("c_maskl", [128, 128], temp=True)
        amof = cload("c_amo", [128, 128], temp=True)
        ampf = cload("c_amp", [128, 128], temp=True)
        tbias = cload("c_tb", [128, 4])
        g1c = cload("g1c", [128, 8])
        gmc = cload("gmc", [128, 8])
        g2c = cload("g2c", [128, 8])
        gqc = cload("gqc", [128, 1])
        gkc = cload("gkc", [128, 1], temp=True)
        fgain = cload("fgain", [128, D], bcast=True)
        kkb = cload("p_kk", [128, 512], bcast=True)
        kab = cload("p_ka", [128, 512], bcast=True)
        rkb = cload("p_rk", [128, 512], bcast=True)
        lnwb = cload("p_lnw", [128, 512], bcast=True)
        lnbb = cload("p_lnb", [128, 512], bcast=True)
        sinkb = cload("sinks", [128, 8], bcast=True, temp=True)
        w0f = cload("p_w0", [1, 512], temp=True)
        a0f = cload("p_a0", [1, 512], temp=True)
        wl_f = carve("wl_f", [64, 512], F32)
        DMA("sp", wl_f[32:64, :], dr["p_wl"][:, :], (), (cb,), cb)
        al_f = cload("p_al", [32, 512], temp=True)
        gl_f = cload("p_gl", [96, 512], temp=True)

        cb2 = P.buf("consts2")
        identb = sb("identb", [128, 128], BF16)
        mask4 = sb("mask4", [128, 512], BF16)
        maskl = sb("maskl", [128, 128], BF16)
        amo = sb("amo", [128, 128], BF16)
        amp = sb("amp", [128, 128], BF16)
        rhs_w = sb("rhs_w", [66, 512], BF16)
        rhs_a = sb("rhs_a", [66, 512], BF16)
        gl_t = sb("gl_t", [96, 512], BF16)
        w0hl = carve("w0hl", [2, 512], BF16)
        a0b = carve("a0b", [1, 512], BF16)
        onesc = sb("onesc", [128, 1], F32)
        nhalf = sb("nhalf", [128, 8], F32)
        sinkexp = sb("sinkexp", [128, 8], F32)
        gk8 = sb("gk8", [128, 1], F32)
        w0tmp = carve("w0tmp", [1, 1024], F32)
        CP("dve", identb[:], identf[:], (cb,), (cb2,))
        CP("dve", mask4[:], mask4f[:], (cb,), (cb2,))
        CP("dve", maskl[:], masklf[:], (cb,), (cb2,))
        CP("dve", amo[:], amof[:], (cb,), (cb2,))
        CP("dve", amp[:], ampf[:], (cb,), (cb2,))
        rwb_ = P.buf("rhs_wa")
        MEMSET("pool", rhs_w[:], 0.0, (rwb_,))
        MEMSET("pool", rhs_a[:], 0.0, (rwb_,))
        CP("dve", rhs_w[32:64, :], wl_f[32:64, :], (cb, rwb_), (rwb_,))
        CP("dve", rhs_a[0:32, :], al_f[:], (cb, rwb_), (rwb_,))
        TS("dve", gl_t[:], gl_f[:], 0.5, None, ALU.mult, None, (cb,), (cb2,))
        pass
        MEMSET("pool", onesc[:], 1.0, (cb2,))
        MEMSET("pool", nhalf[:], -0.5, (cb2,))
        eps64 = sb("eps64", [128, 1], F32)
        MEMSET("pool", eps64[:], 64 * RMS_EPS, (cb2,))
        ACT(sinkexp[:], sinkb[:], AF.Exp, (cb,), (cb2,))
        TS("dve", gk8[:], gkc[:], 8.0, None, ALU.mult, None, (cb,), (cb2,))
        w0b_ = P.buf("w0b")
        CP("dve", w0hl[0:1, :], w0f[:], (cb,), (w0b_,))
        CP("dve", w0tmp[:, 0:512], w0hl[0:1, :], (w0b_,), (w0b_,))
        TT_("dve", w0tmp[:, 512:1024], w0f[:], w0tmp[:, 0:512], ALU.subtract, (cb, w0b_), (w0b_,))
        w0lo = carve("w0lo", [1, 512], BF16)
        CP("dve", w0lo[:], w0tmp[:, 512:1024], (w0b_,), (w0b_,))
        DMA("sp", rhs_w[64:65, :], w0hl[0:1, :], (w0b_, rwb_), (cb2,), cb2)
        DMA("sp", rhs_w[65:66, :], w0lo[:], (w0b_, rwb_), (cb2,), cb2)
        a0b_ = P.buf("a0b")
        CP("dve", a0b[:], a0f[:], (cb,), (a0b_,))
        DMA("sp", rhs_a[64:65, :], a0b[:], (a0b_, rwb_), (cb2,), cb2)

        xb = [sbb("xb%d" % i, [128, 4, D], F32) for i in range(3)]
        hT, hTb = sbb("hT", [128, 8, 520], BF16)
        hTf, hTfb = sbb("hTf", [128, 8, 512], BF16)
        actT, actTb = sbb("actT", [128, 8, 512], BF16)
        sgt = [sbb("sgt%d" % i, [128, 512], F32) for i in range(2)]
        junk, junkb = sgt[0][0][:].bitcast(BF16), sgt[0][1]
        hb = [(sgt[1][0][:].bitcast(BF16), sgt[1][1])]
        ssq, ssqb = sbb("ssq", [128, 8], F32)
        rstd, rstdb = sbb("rstd", [128, 8], F32)
        ring = [sbb("ring%d" % i, [128, SLOT], BF16) for i in range(RING)]

        P.barrier()
        P.phase = "prep"
        aoff[0] = 0
        NSTF, NSTB = 4, 4
        stf = [carveb("stf%d" % i, [128, 1408], F32) for i in range(NSTF)]
        stb = [carveb("stb%d" % i, [128, 1408], BF16) for i in range(NSTB)]
        mub = carve("mub", [128, 1696], F32)
        omub = carve("omub", [128, 1696], F32)
        DMA("sp", mub[:], dr["mu"].partition_broadcast(128), (), (cb,), cb)
        TS("dve", omub[:], mub[:], -1.0, 1.0, ALU.mult, ALU.add, (cb,), (cb2,))
        pc = {"f": 0, "b": 0, "e": 0}

        def prep_piece(src, ncol, variants):
            sf, sfb = stf[pc["f"] % NSTF]
            pc["f"] += 1
            DMA("sp", sf[:, 0:ncol], src, (), (sfb,), sfb)
            for (rs, const, cs, stores) in variants:
                so, sob = stb[pc["b"] % NSTB]
                pc["b"] += 1
                if cs is not None:
                    TT_("dve", so[:, 0:ncol], sf[:, 0:ncol], cs, ALU.mult, (sfb, cb, cb2), (sob,))
                else:
                    eng = ("dve", "act")[pc["e"] % 2]
                    pc["e"] += 1
                    if eng == "act":
                        assert rs is None or const == 1.0
                        ACT(so[:, 0:ncol], sf[:, 0:ncol], AF.Copy, (sfb, cb), (sob,), scale=(const if rs is None else rs))
                    elif rs is None:
                        TS(eng, so[:, 0:ncol], sf[:, 0:ncol], const, None, ALU.mult, None, (sfb,), (sob,))
                    else:
                        TS(eng, so[:, 0:ncol], sf[:, 0:ncol], rs, const, ALU.mult, ALU.mult, (sfb, cb), (sob,))
                for (dst, src_ap) in stores(so):
                    DMA("sp", dst, src_ap, (sob,), (), sob)

        sbuf_ = {}

        def sbufof(name):
            if name not in sbuf_:
                sbuf_[name] = P.buf("scr_" + name)
            return sbuf_[name]

        def CAST(name, dst, src_):
            b_ = sbufof(name)
            ph = P.phase
            P.phase = "cast"
            P.add("pool", lambda e: e.dma_start(out=dst, in_=src_), (), (b_,), dma=True, track=b_)
            P.phase = ph

        pending = {"gen": None, "n": 0}
        CAST_NEED = {"gu0b": 16, "d0": 18, "q": 42, "kd": 42, "av": 42, "gate": 42, "br": 50, "wo": 58, "gu1": 90, "d1": 92}

        def drip(n):
            g = pending["gen"]
            if g is None:
                return
            for _ in range(n):
                try:
                    next(g)
                    pending["n"] += 1
                except StopIteration:
                    pending["gen"] = None
                    return

        def ensure_cast(sname):
            need_ = CAST_NEED.get(sname)
            if need_ is not None and pending["gen"] is not None and pending["n"] < need_:
                drip(need_ - pending["n"])

        def cast_ffn(fi, fgroups=((0, NF, ""),), with_d=True):
            gn, un, dn = (("w_g1", "w_u1", "w_d1"), ("w_g2", "w_u2", "w_d2"))[fi]
            guv = s_gu[fi].rearrange("f p (t k m) -> p f t k m", t=2, k=8)
            for (fa, fb, sfx) in fgroups:
                for t_, wn in enumerate((gn, un)):
                    for k in range(8):
                        CAST("gu%d%s" % (fi, sfx), guv[:, fa:fb, t_, k, :],
                             dr[wn][k * 128:(k + 1) * 128, fa * 128:fb * 128].rearrange("p (f m) -> p f m", m=128))
                        yield
            for half in range(2):
                if with_d:
                    CAST("d%d" % fi, s_d[fi][half].rearrange("f p n -> (f p) n"), dr[dn][:, half * 512:(half + 1) * 512])
                    yield

        def cast_mix():
            qv = s_q.rearrange("m p (k n) -> p m k n", k=8)
            kdv = s_kd.rearrange("g p (k n) -> p g k n", k=8)
            avv = s_av.rearrange("p (k n) -> p k n", k=8)
            gtv = s_gate.rearrange("m p (t k n) -> p t m k n", t=2, k=8)
            for k in range(8):
                rows = dr["w_in"][k * 128:(k + 1) * 128, :]
                CAST("q", qv[:, :, k, :], rows[:, 1696:2208].rearrange("p (m n) -> p m n", n=128))
                for g in range(2):
                    for hf in range(2):
                        CAST("kd", kdv[:, g, k, hf * 64:(hf + 1) * 64], rows[:, 2208 + g * 64:2272 + g * 64])
                CAST("av", avv[:, k, :], rows[:, 2336:2464])
                yield
                for t_ in range(2):
                    CAST("gate", gtv[:, t_, :, k, :], rows[:, 2464 + t_ * 1024:3488 + t_ * 1024].rearrange("p (m n) -> p m n", n=128))
                    yield
            brv = s_br.rearrange("m p (t c n) -> p t c m n", t=2, c=4)
            for t_, wn in enumerate(("w_br", "w_ba")):
                for c in range(4):
                    CAST("br", brv[:, t_, c, :, :], dr[wn][c * 128:(c + 1) * 128, :].rearrange("p (m n) -> p m n", n=128))
                    yield
            wov = s_wo.rearrange("h q p (k n) -> h q p k n", k=4)
            for m in range(8):
                for h in range(2):
                    CAST("wo", wov[h, m // 4, :, m % 4, :], dr["w_out"][m * 128:(m + 1) * 128, h * 512:(h + 1) * 512])
                yield

        if do_ffn1:
            for _ in cast_ffn(0, ((0, 8, "a"),), with_d=False):
                pass
        def staged_prep():
            rkvv = s_rkv.rearrange("c q p (k n) -> c q p k n", k=4)
            lorav = s_lora.rearrange("p (k m) -> p k m", m=64)
            xgv = s_xg.rearrange("p (k m) -> p k m", m=96)
            for k in range(8):
                def stores_a1(so, kc):
                    return [(rkvv[c, kc // 4, :, kc % 4, :], so[:, c * 512:(c + 1) * 512]) for c in range(2)]

                def stores_a2(so, kc):
                    return [(rkvv[2, kc // 4, :, kc % 4, :], so[:, 0:512]),
                            (lorav[:, kc, 0:32], so[:, 544:576]),
                            (lorav[:, kc, 32:64], so[:, 512:544]),
                            (xgv[:, kc, :], so[:, 576:672])]
                prep_piece(dr["w_in"][k * 128:(k + 1) * 128, 0:1024], 1024,
                           [(None, 1.0, omub[:, 0:1024], lambda so, k=k: stores_a1(so, k)),
                            (None, 1.0, mub[:, 0:1024], lambda so, k=k: stores_a1(so, 8 + k))])
                yield
                prep_piece(dr["w_in"][k * 128:(k + 1) * 128, 1024:1696], 672,
                           [(None, 1.0, omub[:, 1024:1696], lambda so, k=k: stores_a2(so, k)),
                            (None, 1.0, mub[:, 1024:1696], lambda so, k=k: stores_a2(so, 8 + k))])
                yield


        def cast2():
            if do_ffn1:
                yield from cast_ffn(0, ((8, NF, "b"),))
            if do_mix:
                yield from cast_mix()
            if do_ffn2:
                yield from cast_ffn(1)
        pending["gen"] = cast2()

        rec = []

        def wresolve(dsc):
            k = dsc[0]
            if k == "gu":
                return s_gu[dsc[1]][dsc[2]], 2048
            if k == "d":
                _, fi_, half, f0, nf = dsc
                return s_d[fi_][half, f0:f0 + nf].rearrange("f p n -> p f n"), nf * 512
            if k == "lora":
                return s_lora, 1024
            if k == "xg":
                return s_xg, 1536
            if k == "q":
                return s_q[dsc[1]], 1024
            if k == "kd":
                return s_kd[dsc[1]], 1024
            if k == "rkv":
                return s_rkv[dsc[1], dsc[2]], 2048
            if k == "av":
                return s_av, 1024
            if k == "gate":
                return s_gate[dsc[1]], 2048
            if k == "br":
                return s_br[dsc[1]], 1024
            if k == "wo":
                return s_wo[dsc[1], dsc[2]], 2048
            raise KeyError(dsc)

        sidx = {"get": 0, "issued": 0}
        staged_ready = {"v": not do_mix}

        def wget(dsc):
            i = sidx["get"]
            sidx["get"] += 1
            if order is None:
                rec.append(dsc)
                return ring[i % RING]
            assert order[i] == dsc, (i, order[i], dsc)
            while sidx["issued"] < min(len(order), i + RING):
                j = sidx["issued"]
                src_, n = wresolve(order[j])
                rt, rb_ = ring[j % RING]
                knd = order[j][0]
                if knd in ("lora", "xg", "rkv") and not staged_ready["v"]:
                    break
                sname = knd + str(order[j][1]) if knd in ("gu", "d") else knd
                if sname == "gu0":
                    sname = "gu0a" if order[j][2] < 8 else "gu0b"
                ensure_cast(sname)
                rd = (sbuf_[sname],) if sname in sbuf_ else ()
                if len(src_.shape) == 3:
                    DMA("sp", rt[:, 0:n].rearrange("p (f n) -> p f n", n=512), src_, rd, (rb_,), rb_)
                else:
                    DMA("sp", rt[:, 0:n], src_, rd, (rb_,), rb_)
                sidx["issued"] += 1
            return ring[i % RING]

        Hf, Hfb = sbb("Hf", [128, 256], F32)
        hcar, hcarb = sbb("hcar", [128, 8, 1], BF16)
        MEMSET("pool", Hf[:], 0.0, (Hfb,))
        MEMSET("pool", hcar[:], 0.0, (hcarb,))

        def rms_stats(xt, xtb):
            for s in range(4):
                ACT(junk[:], xt[:, s, :], AF.Square, (xtb,), (junkb, ssqb), accum=ssq[:, s:s + 1])
            TS("dve", ssq[:, 4:8], ssq[:, 0:4], 1.0 / D, RMS_EPS, ALU.mult, ALU.add, (ssqb,), (ssqb,))
            TT_("pool", rstd[:, 0:4], ssq[:, 4:8], nhalf[:, 0:4], ALU.pow, (ssqb, cb2), (rstdb,))
            drip(24)

        def norm_T(xt, xtb, dst, dstb, col0, gcol):
            rms_stats(xt, xtb)
            for s in range(4):
                h_, hb_ = hb[0]
                TS("dve", h_[:], xt[:, s, :], rstd[:, s:s + 1], None, ALU.mult, None, (xtb, rstdb), (hb_,))
                pt, ptb = PS()
                pv = pt[:].bitcast(BF16)
                for k in range(8):
                    TR(pv[:, k * 128:(k + 1) * 128], h_[:, k * 128:(k + 1) * 128], identb[:], (hb_, cb2), (ptb,))
                TT_("dve", dst[:, :, col0 + s * 128:col0 + (s + 1) * 128], pv.rearrange("p (k n) -> p k n", n=128),
                    gcol[:, 0:8].unsqueeze(2).to_broadcast([128, 8, 128]), ALU.mult, (ptb, cb), (dstb,))
                PSrel(ptb)
                yield

        GROUPS = ((0, 6), (6, 8), (14, 8))

        def ffn(fi_, xt, xtb):
            yield from norm_T(xt, xtb, hTf, hTfb, 0, (g1c, g2c)[fi_])
            for (g0, gn) in GROUPS:
                for fi in range(gn):
                    rt, rb_ = wget(("gu", fi_, g0 + fi))
                    gu = rt[:].rearrange("p (t k m) -> p t k m", t=2, k=8)
                    pg, pgb = PS()
                    pu, pub = PS()
                    for k in range(8):
                        MM(pg[:], gu[:, 0, k, :], hTf[:, k, :], k == 0, k == 7, (rb_, hTfb), (pgb,))
                    sg, sgb = sgt[fi % 2]
                    ACT(sg[:], pg[:], AF.Tanh, (pgb,), (sgb,), scale=0.5)
                    yield
                    for k in range(8):
                        MM(pu[:], gu[:, 1, k, :], hTf[:, k, :], k == 0, k == 7, (rb_, hTfb), (pub,))
                    STT(sg[:], sg[:], 1.0, pg[:], ALU.add, ALU.mult, (sgb, pgb), (sgb,))
                    TT_("dve", actT[:, fi, :], sg[:], pu[:], ALU.mult, (sgb, pub), (actTb,))
                    PSrel(pgb, pub)
                    yield
                for half in range(2):
                    pbs = [PS() for _ in range(4)]
                    for f0 in range(0, gn, 4):
                        nf = min(4, gn - f0)
                        rt, rb_ = wget(("d", fi_, half, g0 + f0, nf))
                        dv = rt[:].rearrange("p (f n) -> p f n", n=512)
                        for ff in range(nf):
                            f = f0 + ff
                            for s in range(4):
                                MM(pbs[s][0][:], actT[:, f, s * 128:(s + 1) * 128], dv[:, ff, :], f == 0, f == gn - 1,
                                   (actTb, rb_), (pbs[s][1],))
                            if ff % 2 == 1 or ff == nf - 1:
                                yield
                    for s in range(4):
                        xs = xt[:, s, half * 512:(half + 1) * 512]
                        STT(xs, pbs[s][0][:], 0.25, xs, ALU.mult, ALU.add, (pbs[s][1], xtb), (xtb,))
                        PSrel(pbs[s][1])
                    yield

        def tagged(gen, tag):
            while True:
                P.phase = tag
                try:
                    next(gen)
                except StopIteration:
                    return
                yield

        def run_all(gen):
            for _ in gen:
                pass

        def interleave(main_gen, side_gen, ratio=1):
            side_done = side_gen is None
            acc = 0.0
            for _ in main_gen:
                acc += ratio
                while acc >= 1.0:
                    acc -= 1.0
                    if not side_done:
                        try:
                            next(side_gen)
                        except StopIteration:
                            side_done = True
            if not side_done:
                run_all(side_gen)

        if do_mix:
            rkv = [sbb("rkv%d" % c, [128, 4, 512], BF16) for c in range(3)]
            lT, lTb = sbb("lT", [66, 512], BF16)
            gTt, gTb = sbb("gTt", [96, 512], BF16)
            qT, qTb = sbb("qT", [128, 4, 512], BF16)
            kTd, kTdb = sbb("kTd", [128, 2, 2, 640], BF16)
            vaug, vaugb = sbb("vaug", [128, 5, 2, VW], BF16)
            yrT, yrTb = sbb("yrT", [128, 4, 512], BF16)
            yaT, yaTb = sbb("yaT", [128, 4, 512], BF16)
            mgT, mgTb = actT[:, 0:8, :], actTb
            aoff[0] = 0
            tmpf = [carveb("tf%d" % i, [128, 512], F32) for i in range(9)]
            tmpf.append(tmpf[0])
            tmpb = [carveb("tb%d" % i, [128, 512], BF16) for i in range(6)]
            tmpb.append(tmpb[4])
            arT, arTb = sbb("arT", [128, 4, 2, 128], BF16)
            bkT, bkTb = sbb("bkT", [128, 4, 2, 128], BF16)
            arZ, arZb = sbb("arZ", [128, 4, 2, 2, 128], BF16)
            bZ, bZb = sbb("bZ", [128, 4, 2, 128], BF16)
            Hz, Hzb = sbb("Hz", [128, 4, 128], BF16)
            MEMSET("pool", arZ[:], 0.0, (arZb,))
            MEMSET("pool", bZ[:], 0.0, (bZb,))
            MEMSET("pool", Hz[:], 0.0, (Hzb,))
            MEMSET("pool", lT[:], 1.0, (lTb,))
            SCb = [carveb("SCb%d" % p_, [128, 2, 2, 128], BF16) for p_ in range(4)]
            SCk = [carveb("SCk%d" % p_, [128, 2, 2, 128], BF16) for p_ in range(4)]
            Pm = [[carveb("Pm%d_%d" % (a, g), [128, 4, 128], BF16) for g in range(2)] for a in range(2)]
            PTm = [[carveb("PTm%d_%d" % (a, g), [128, 4, 128], BF16) for g in range(2)] for a in range(2)]
            Qm = [[carveb("Qm%d_%d" % (a, g), [128, 4, 128], BF16) for g in range(2)] for a in range(2)]
            PTa = [sbb("PTa0", [128, 2, 512], BF16)] * 2
            yatt, yattb = sbb("yatt", [128, 512], BF16)
            gamC, gamCb = sbb("gamC", [128, 4], F32)
            sts = [sbb("sts%d" % i, [128, 16], F32) for i in range(7)]
            MEMSET("pool", vaug[:], 1.0, (vaugb,))
            MEMSET("pool", kTd[:], 0.0, (kTdb,))
            MEMSET("pool", yrT[:], 0.0, (yrTb,))
            MEMSET("pool", yaT[:], 0.0, (yaTb,))

            def v3(ap, d=64):
                return ap.rearrange("p (h d) -> p h d", d=d)

            def bc8(ap, n=8, d=64):
                return ap.unsqueeze(2).to_broadcast([128, n, d])

            def rhs_k(kc, lo=0, n=512):
                if kc < 8:
                    return hT[:, kc, 8 + lo:8 + lo + n]
                return hT[:, kc - 8, 7 + lo:7 + lo + n]

            def qk_norm(pq, pqb, gcol, dst, dstb, dsts=None):
                sq, sqb = tmpf[7]
                ACT(sq[:], pq[:], AF.Square, (pqb,), (sqb,))
                ps, psb = PS()
                MM(ps[:], blk[:], sq[:], True, True, (cb, sqb), (psb,))
                t_, tb_ = tmpf[8]
                ACT(t_[:], ps[:], AF.Ln, (psb, cb2), (tb_,), bias=eps64[:, 0:1])
                PSrel(psb)
                rs, rsb = tmpf[9]
                ACT(rs[:], t_[:], AF.Exp, (tb_,), (rsb,), scale=-0.5)
                if dsts is None:
                    dsts = [(slice(0, 128), dst)]
                for (rows, d_) in dsts:
                    STT(d_, pq[rows, :], gcol[rows, :], rs[rows, :], ALU.mult, ALU.mult, (pqb, rsb, cb, cb2), (dstb,))
                PSrel(pqb)

            def attn_block(b, gb):
                for g in range(2):
                    po, pob = PS()
                    if gb > 0:
                        pp, ppb = PS()
                    for hh in range(4):
                        h = 4 * g + hh
                        ch, hf = divmod(h, 2)
                        qs = qT[:, ch, b * 128:(b + 1) * 128]
                        MM(po[:, hh * 128:(hh + 1) * 128], kTd[:, g, hf, (1 + b) * 128:(2 + b) * 128], qs, True, True,
                           (kTdb, qTb), (pob,))
                        if gb > 0:
                            MM(pp[:, hh * 128:(hh + 1) * 128], kTd[:, g, hf, b * 128:(b + 1) * 128], qs, True, True,
                               (kTdb, qTb), (ppb,))
                    pt_, ptb_ = PTa[g]
                    ACT(pt_[:, 1, :], po[:], AF.Exp, (pob,), (ptb_,))
                    PSrel(pob)
                    TT_("dve", v3(pt_[:, 1, :], 128), v3(pt_[:, 1, :], 128), amo[:].unsqueeze(1).to_broadcast([128, 4, 128]),
                        ALU.mult, (ptb_, cb2), (ptb_,))
                    if gb > 0:
                        ACT(pt_[:, 0, :], pp[:], AF.Exp, (ppb,), (ptb_,))
                        PSrel(ppb)
                        TT_("dve", v3(pt_[:, 0, :], 128), v3(pt_[:, 0, :], 128),
                            amp[:].unsqueeze(1).to_broadcast([128, 4, 128]), ALU.mult, (ptb_, cb2), (ptb_,))
                    yield
                    ppv, ppvb = PS()
                    for hh in range(4):
                        o_ = ppv[:, hh * VW:(hh + 1) * VW]
                        if gb > 0:
                            MM(o_, pt_[:, 0, hh * 128:(hh + 1) * 128], vaug[:, b, g, :], True, False, (ptb_, vaugb), (ppvb,))
                        MM(o_, pt_[:, 1, hh * 128:(hh + 1) * 128], vaug[:, 1 + b, g, :], gb == 0, True, (ptb_, vaugb), (ppvb,))
                    pv3 = ppv[:, 0:4 * VW].rearrange("p (h d) -> p h d", d=VW)
                    sg_, sgb_ = sts[6]
                    TT_("dve", sg_[:, 0:4].unsqueeze(2), pv3[:, :, 64:65], sinkexp[:, 4 * g:4 * g + 4].unsqueeze(2), ALU.add,
                        (ppvb, cb2), (sgb_,))
                    P.add("dve", lambda e, sg_=sg_: e.reciprocal(out=sg_[:, 4:8], in_=sg_[:, 0:4]), (sgb_,), (sgb_,))
                    TT_("dve", v3(yatt[:, g * 256:(g + 1) * 256]), pv3[:, :, 0:64],
                        sg_[:, 4:8].unsqueeze(2).to_broadcast([128, 4, 64]), ALU.mult, (ppvb, sgb_), (yattb,))
                    PSrel(ppvb)
                    yield
                pt, ptb = PS()
                pv = pt[:].bitcast(BF16)
                for c in range(4):
                    TR(pv[:, c * 128:(c + 1) * 128], yatt[:, c * 128:(c + 1) * 128], identb[:], (yattb, cb2), (ptb,))
                CP("act", yaT[:, :, b * 128:(b + 1) * 128], pv[:, 0:512].rearrange("p (c n) -> p c n", n=128), (ptb,), (yaTb,))
                PSrel(ptb)
                yield

            def rwkv_block(b):
                t0 = b * 128
                r_b, k0_b, v_b = rkv[0][0][:, b, :], rkv[1][0][:, b, :], rkv[2][0][:, b, :]
                rB, kB, vB = rkv[0][1], rkv[1][1], rkv[2][1]
                (f_tw, f_twb), (f_al, f_alb), (f_gam, f_gamb), (f_ig, f_igb), (f_gx, f_gxb), (f_kk, f_kkb), \
                    (f_km, f_kmb), (f_s1, f_s1b), (f_s2, f_s2b), (f_yc, f_ycb) = tmpf
                (b_kt, b_ktb), (b_bt, b_btb), (b_at, b_atb), (b_rt, b_rtb), (b_X, b_Xb), (b_U, b_Ub), (b_y, b_yb) = tmpb
                (st_a, st_ab), (st_b, st_bb), (st_c, st_cb), (st_d, st_db), (st_e, st_eb), (st_f, st_fb), _ = sts
                p1, p1b = PS()
                MM(p1[:], lT[0:66, t0:t0 + 128], rhs_w[0:66, :], True, True, (lTb, cb2), (p1b,))
                ACT(f_tw[:], p1[:], AF.Tanh, (p1b,), (f_twb,), scale=0.5)
                PSrel(p1b)
                p2, p2b = PS()
                MM(p2[:], lT[0:66, t0:t0 + 128], rhs_a[0:66, :], True, True, (lTb, cb2), (p2b,))
                ACT(f_s1[:], p2[:], AF.Tanh, (p2b,), (f_s1b,), scale=0.5)
                PSrel(p2b)
                yield
                TS("dve", f_al[:], f_s1[:], 0.5, 0.5, ALU.mult, ALU.add, (f_s1b,), (f_alb,))
                p3, p3b = PS()
                MM(p3[:], tri_i[:], f_tw[:], True, True, (cb, f_twb), (p3b,))
                p4, p4b = PS()
                MM(p4[:], tri_s[:], f_tw[:], True, True, (cb, f_twb), (p4b,))
                ACT(f_gam[:], p3[:], AF.Exp, (p3b, cb), (f_gamb,), scale=-0.5 * C0, bias=tbias[:, 0:1])
                ACT(f_ig[:], p3[:], AF.Exp, (p3b, cb), (f_igb,), scale=0.5 * C0, bias=tbias[:, 1:2])
                ACT(f_gx[:], p4[:], AF.Exp, (p4b, cb), (f_gxb,), scale=-0.5 * C0, bias=tbias[:, 2:3])
                PSrel(p3b, p4b)
                p5, p5b = PS()
                for p_ in range(4):
                    MM(p5[:, p_:p_ + 1], f_tw[:, p_ * 128:(p_ + 1) * 128], onesc[:, 0:1], True, True, (f_twb, cb2), (p5b,))
                ACT(gamC[:, 0:4], p5[:, 0:4], AF.Exp, (p5b, cb), (gamCb,), scale=-0.5 * C0, bias=tbias[:, 3:4])
                PSrel(p5b)
                yield
                TT_("dve", f_kk[:], k0_b, kkb[:], ALU.mult, (kB, cb), (f_kkb,))
                ACT(f_s1[:], f_kk[:], AF.Square, (f_kkb,), (f_s1b,))
                RED(st_a[:, 0:8], v3(f_s1[:]), (f_s1b,), (st_ab,))
                TS("dve", st_a[:, 8:16], st_a[:, 0:8], 1e-24, None, ALU.max, None, (st_ab,), (st_ab,))
                TT_("pool", st_b[:, 0:8], st_a[:, 8:16], nhalf[:, 0:8], ALU.pow, (st_ab, cb2), (st_bb,))
                drip(4)
                TT_("dve", v3(f_kk[:]), v3(f_kk[:]), bc8(st_b[:, 0:8]), ALU.mult, (f_kkb, st_bb), (f_kkb,))
                STT(f_s2[:], f_al[:], -1.0, kab[:], ALU.add, ALU.mult, (f_alb, cb), (f_s2b,))
                STT(f_km[:], f_s2[:], 1.0, k0_b, ALU.add, ALU.mult, (f_s2b, kB), (f_kmb,))
                TT_("dve", b_kt[:], f_km[:], f_ig[:], ALU.mult, (f_kmb, f_igb), (b_ktb,))
                TT_("dve", f_s2[:], f_kk[:], f_al[:], ALU.mult, (f_kkb, f_alb), (f_s2b,))
                TT_("dve", b_bt[:], f_s2[:], f_ig[:], ALU.mult, (f_s2b, f_igb), (b_btb,))
                STT(b_at[:], f_kk[:], -1.0, f_gx[:], ALU.mult, ALU.mult, (f_kkb, f_gxb), (b_atb,))
                TT_("dve", b_rt[:], r_b, f_gam[:], ALU.mult, (rB, f_gamb), (b_rtb,))
                TT_("dve", f_s1[:], r_b, f_km[:], ALU.mult, (rB, f_kmb), (f_s1b,))
                TT_("dve", f_s1[:], f_s1[:], rkb[:], ALU.mult, (f_s1b, cb), (f_s1b,))
                RED(st_c[:, 0:8], v3(f_s1[:]), (f_s1b,), (st_cb,))
                yield
                for (xa, xab, xr, xrb, dst, dstb) in ((b_at, b_atb, b_rt, b_rtb, arT, arTb), (b_bt, b_btb, b_kt, b_ktb, bkT, bkTb)):
                    pt, ptb = PS()
                    pv = pt[:].bitcast(BF16)
                    for p_ in range(4):
                        TR(pv[:, p_ * 128:(p_ + 1) * 128], xa[:, p_ * 128:(p_ + 1) * 128], identb[:], (xab, cb2), (ptb,))
                    for p_ in range(4):
                        TR(pv[:, 512 + p_ * 128:512 + (p_ + 1) * 128], xr[:, p_ * 128:(p_ + 1) * 128], identb[:], (xrb, cb2), (ptb,))
                    pv4 = pv.rearrange("d (a p t) -> d a p t", a=2, p=4)
                    CP("act", dst[:].rearrange("d p a t -> d a p t"), pv4, (ptb,), (dstb,))
                    for q_ in range(2):
                        rows = slice(q_ * 64, q_ * 64 + 64)
                        if dst is arT:
                            CP("dve", arZ[rows, :, q_, :, :].rearrange("d p a t -> d a p t"), pv4[rows], (ptb,), (arZb,))
                        else:
                            CP("dve", bZ[rows, :, q_, :], pv4[rows, 0, :, :], (ptb,), (bZb,))
                    PSrel(ptb)
                yield
                for p_ in range(4):
                    rhs = arZ[:, p_, :, :, :].rearrange("k q a t -> k (q a t)")
                    for (x_, SCx) in ((0, SCb), (1, SCk)):
                        psc, pscb = PS()
                        MM(psc[:], bkT[:, p_, x_, :], rhs, True, True, (bkTb, arZb), (pscb,))
                        TT_("dve", SCx[p_][0][:].rearrange("s q a t -> s (q a t)"), psc[:], mask4[:],
                            ALU.mult, (pscb, cb2), (SCx[p_][1],))
                        PSrel(pscb)
                    if p_ % 2 == 1:
                        yield
                for hg in range(2):
                    pn, pnb = PS()
                    for pp_ in range(2):
                        p_ = hg * 2 + pp_
                        MM(pn[:, pp_ * 256:(pp_ + 1) * 256], arT[:, p_, 0, :], bZ[:, p_, :, :].rearrange("k q s -> k (q s)"),
                           True, True, (arTb, bZb), (pnb,))
                    TT_("dve", PTm[0][hg][0][:], v3(pn[:], 128), maskl[:].unsqueeze(1).to_broadcast([128, 4, 128]), ALU.mult,
                        (pnb, cb2), (PTm[0][hg][1],))
                    PSrel(pnb)
                    for pp_ in range(2):
                        p_ = hg * 2 + pp_
                        TT_("dve", Qm[0][hg][0][:, 2 * pp_:2 * pp_ + 2, :], SCb[p_][0][:, :, 0, :],
                            identb[:].unsqueeze(1).to_broadcast([128, 2, 128]), ALU.add, (SCb[p_][1], cb2), (Qm[0][hg][1],))
                for l in range(7):
                    for hg in range(2):
                        def Pl(hh):
                            if l == 0:
                                h = hg * 4 + hh
                                return SCb[h // 2][0][:, h % 2, 0, :], SCb[h // 2][1]
                            return Pm[l % 2][hg][0][:, hh, :], Pm[l % 2][hg][1]
                        PTl, PTlb = PTm[l % 2][hg]
                        if l <= 4:
                            pP, pPb = PS()
                            for hh in range(4):
                                ap, bf = Pl(hh)
                                MM(pP[:, hh * 128:(hh + 1) * 128], PTl[:, hh, :], ap, True, True, (PTlb, bf), (pPb,))
                        if l <= 5:
                            pT_, pTb_ = PS()
                            for hh in range(4):
                                ap, bf = Pl(hh)
                                MM(pT_[:, hh * 128:(hh + 1) * 128], ap, PTl[:, hh, :], True, True, (bf, PTlb), (pTb_,))
                        if l >= 1:
                            pQ, pQb = PS()
                            Qp, Qpb = Qm[(l - 1) % 2][hg]
                            for hh in range(4):
                                MM(pQ[:, hh * 128:(hh + 1) * 128], PTl[:, hh, :], Qp[:, hh, :], True, True, (PTlb, Qpb), (pQb,))
                        if l <= 4:
                            CP("act", Pm[(l + 1) % 2][hg][0][:], v3(pP[:], 128), (pPb,), (Pm[(l + 1) % 2][hg][1],))
                            PSrel(pPb)
                        if l <= 5:
                            CP("act", PTm[(l + 1) % 2][hg][0][:], v3(pT_[:], 128), (pTb_,), (PTm[(l + 1) % 2][hg][1],))
                            PSrel(pTb_)
                        if l >= 1:
                            TT_("dve", Qm[l % 2][hg][0][:], v3(pQ[:], 128), Qp[:], ALU.add, (pQb, Qpb), (Qm[l % 2][hg][1],))
                            PSrel(pQb)
                        yield
                def hd(h):
                    return h // 2, h % 2, slice(h * 64, (h + 1) * 64)
                pX, pXb = PS()
                for p_ in range(4):
                    MM(pX[:, p_ * 128:(p_ + 1) * 128], arT[:, p_, 0, :], Hz[:, p_, :], True, False, (arTb, Hzb), (pXb,))
                    for h in (2 * p_, 2 * p_ + 1):
                        _, q_, cs = hd(h)
                        MM(pX[:, cs], SCk[p_][0][:, q_, 0, :], v_b[:, cs], False, h % 2 == 1, (SCk[p_][1], vB), (pXb,))
                CP("act", b_X[:], pX[:], (pXb,), (b_Xb,))
                PSrel(pXb)
                yield
                pU, pUb = PS()
                for h in range(8):
                    p_, q_, cs = hd(h)
                    MM(pU[:, cs], Qm[0][h // 4][0][:, h % 4, :], b_X[:, cs], True, True, (Qm[0][h // 4][1], b_Xb), (pUb,))
                CP("dve", b_U[:], pU[:], (pUb,), (b_Ub,))
                PSrel(pUb)
                yield
                pY, pYb = PS()
                for p_ in range(4):
                    MM(pY[:, p_ * 128:(p_ + 1) * 128], arT[:, p_, 1, :], Hz[:, p_, :], True, False, (arTb, Hzb), (pYb,))
                    for h in (2 * p_, 2 * p_ + 1):
                        _, q_, cs = hd(h)
                        MM(pY[:, cs], SCb[p_][0][:, q_, 1, :], b_U[:, cs], False, False, (SCb[p_][1], b_Ub), (pYb,))
                        MM(pY[:, cs], SCk[p_][0][:, q_, 1, :], v_b[:, cs], False, h % 2 == 1, (SCk[p_][1], vB), (pYb,))
                pD, pDb = PS()
                for p_ in range(4):
                    ps_ = slice(p_ * 128, (p_ + 1) * 128)
                    MM(pD[:, ps_], b_bt[:, ps_], b_U[:, ps_], True, False, (b_btb, b_Ub), (pDb,))
                    MM(pD[:, ps_], b_kt[:, ps_], v_b[:, ps_], False, True, (b_ktb, vB), (pDb,))
                for q_ in range(2):
                    rows = slice(q_ * 64, q_ * 64 + 64)
                    TT_("dve", v3(Hf[rows, :]), v3(pD[rows, :], 128)[:, :, q_ * 64:(q_ + 1) * 64], v3(Hf[rows, :]), ALU.add,
                        (pDb, Hfb), (Hfb,))
                TT_("dve", v3(Hf[:]), v3(Hf[:]), gamC[:, 0:4].unsqueeze(2).to_broadcast([128, 4, 64]), ALU.mult,
                    (Hfb, gamCb), (Hfb,))
                PSrel(pDb)
                for q_ in range(2):
                    rows = slice(q_ * 64, q_ * 64 + 64)
                    CP("act", Hz[rows, :, q_ * 64:(q_ + 1) * 64], v3(Hf[rows, :]), (Hfb,), (Hzb,))
                yield
                RED(st_d[:, 0:8], v3(pY[:]), (pYb,), (st_db,))
                TS("dve", st_d[:, 8:16], st_d[:, 0:8], -1.0 / 64, None, ALU.mult, None, (st_db,), (st_db,))
                TT_("dve", v3(f_yc[:]), v3(pY[:]), bc8(st_d[:, 8:16]), ALU.add, (pYb, st_db), (f_ycb,))
                PSrel(pYb)
                ACT(f_s1[:], f_yc[:], AF.Square, (f_ycb,), (f_s1b,))
                RED(st_e[:, 0:8], v3(f_s1[:]), (f_s1b,), (st_eb,))
                TS("dve", st_e[:, 8:16], st_e[:, 0:8], 1.0 / 64, GN_EPS, ALU.mult, ALU.add, (st_eb,), (st_eb,))
                TT_("pool", st_f[:, 0:8], st_e[:, 8:16], nhalf[:, 0:8], ALU.pow, (st_eb, cb2), (st_fb,))
                drip(4)
                TT_("dve", v3(f_yc[:]), v3(f_yc[:]), bc8(st_f[:, 0:8]), ALU.mult, (f_ycb, st_fb), (f_ycb,))
                TT_("dve", f_yc[:], f_yc[:], lnwb[:], ALU.mult, (f_ycb, cb), (f_ycb,))
                TT_("dve", f_yc[:], f_yc[:], lnbb[:], ALU.add, (f_ycb, cb), (f_ycb,))
                TT_("dve", v3(f_s1[:]), v3(v_b), bc8(st_c[:, 0:8]), ALU.mult, (vB, st_cb), (f_s1b,))
                TT_("dve", f_yc[:], f_yc[:], f_s1[:], ALU.add, (f_ycb, f_s1b), (f_ycb,))
                pG, pGb = PS()
                MM(pG[:], gTt[0:96, t0:t0 + 128], gl_t[0:96, :], True, True, (gTb, cb2), (pGb,))
                TT_("dve", b_y[:], f_yc[:], pG[:], ALU.mult, (f_ycb, pGb), (b_yb,))
                PSrel(pGb)
                yield
                pt, ptb = PS()
                pv = pt[:].bitcast(BF16)
                for c in range(4):
                    TR(pv[:, c * 128:(c + 1) * 128], b_y[:, c * 128:(c + 1) * 128], identb[:], (b_yb, cb2), (ptb,))
                CP("act", yrT[:, :, t0:t0 + 128], pv[:, 0:512].rearrange("p (c n) -> p c n", n=128), (ptb,), (yrTb,))
                PSrel(ptb)
                yield

            def mixer_pre(ti, xt, xtb):
                run_all(norm_T(xt, xtb, hT, hTb, 8, gmc))
                CP("dve", hT[:, :, 7:8], hcar[:], (hcarb,), (hTb,))
                CP("dve", hcar[:], hT[:, :, 519:520], (hTb,), (hcarb,))
                rt, rb_ = wget(("lora",))
                w = rt[:, 0:1024].rearrange("p (k m) -> p k m", m=64)
                pa, pab = PS()
                for kc in range(16):
                    MM(pa[0:64, :], w[:, kc, :], rhs_k(kc), kc == 0, kc == 15, (rb_, hTb), (pab,))
                ACT(lT[0:32, :], pa[0:32, :], AF.Copy, (pab,), (lTb,))
                ACT(lT[32:64, :], pa[32:64, :], AF.Tanh, (pab,), (lTb,))
                PSrel(pab)
                rt, rb_ = wget(("xg",))
                w = rt[:, 0:1536].rearrange("p (k m) -> p k m", m=96)
                pg, pgb = PS()
                for kc in range(16):
                    MM(pg[0:96, :], w[:, kc, :], rhs_k(kc), kc == 0, kc == 15, (rb_, hTb), (pgb,))
                ACT(tmpf[0][0][0:96, :], pg[0:96, :], AF.Tanh, (pgb,), (tmpf[0][1],), scale=0.5)
                PSrel(pgb)
                TS("dve", gTt[:], tmpf[0][0][0:96, :], 1.0, None, ALU.add, None, (tmpf[0][1],), (gTb,))
                for m in range(4):
                    rt, rb_ = wget(("q", m))
                    w = rt[:, 0:1024].rearrange("p (k m) -> p k m", m=128)
                    pq, pqb = PS()
                    for k in range(8):
                        MM(pq[:], w[:, k, :], rhs_k(k), k == 0, k == 7, (rb_, hTb), (pqb,))
                    qk_norm(pq, pqb, gqc[:, 0:1], qT[:, m, :], qTb)
                CP("pool", kTd[:, :, :, 0:128], kTd[:, :, :, 512:640], (kTdb,), (kTdb,))
                for g in range(2):
                    rt, rb_ = wget(("kd", g))
                    w = rt[:, 0:1024].rearrange("p (k m) -> p k m", m=128)
                    pq, pqb = PS()
                    for k in range(8):
                        MM(pq[:], w[:, k, :], rhs_k(k), k == 0, k == 7, (rb_, hTb), (pqb,))
                    qk_norm(pq, pqb, gk8[:, 0:1], None, kTdb,
                            dsts=[(slice(0, 64), kTd[0:64, g, 0, 128:640]), (slice(64, 128), kTd[64:128, g, 1, 128:640])])
                for c in range(3):
                    pbs = [PS() for _ in range(4)]
                    for q in range(4):
                        rt, rb_ = wget(("rkv", c, q))
                        w = rt[:].rearrange("p (k n) -> p k n", n=512)
                        for kk in range(4):
                            kc = q * 4 + kk
                            for b in range(4):
                                MM(pbs[b][0][:], rhs_k(kc, b * 128, 128), w[:, kk, :], kc == 0, kc == 15, (hTb, rb_), (pbs[b][1],))
                    for b in range(4):
                        CP("act" if b % 2 else "dve", rkv[c][0][:, b, :], pbs[b][0][:], (pbs[b][1],), (rkv[c][1],))
                        PSrel(pbs[b][1])
                rt, rb_ = wget(("av",))
                w = rt[:, 0:1024].rearrange("p (k n) -> p k n", n=128)
                pv_, pvb_ = PS()
                for b in range(4):
                    for k in range(8):
                        MM(pv_[:, b * 128:(b + 1) * 128], rhs_k(k, b * 128, 128), w[:, k, :], k == 0, k == 7, (hTb, rb_), (pvb_,))
                CP("pool", vaug[:, 0, :, :], vaug[:, 4, :, :], (vaugb,), (vaugb,))
                CP("act", vaug[:, 1:5, :, 0:64], pv_[:].rearrange("p (b g d) -> p b g d", b=4, g=2), (pvb_,), (vaugb,))
                PSrel(pvb_)

            def mixer_blocks(ti):
                for b in range(4):
                    if "attn" not in skip:
                        yield from tagged(attn_block(b, ti * 4 + b), "attn")
                    if "rwkv" not in skip:
                        yield from tagged(rwkv_block(b), "rwkv")

            def mixer_post(ti, xt, xtb):
                for m in range(8):
                    rt, rb_ = wget(("gate", m))
                    gw = rt[:].rearrange("p (t k n) -> p t k n", t=2, k=8)
                    pgr, pgrb = PS()
                    pga, pgab = PS()
                    for k in range(8):
                        MM(pgr[:], gw[:, 0, k, :], rhs_k(k), k == 0, k == 7, (rb_, hTb), (pgrb,))
                    for k in range(8):
                        MM(pga[:], gw[:, 1, k, :], rhs_k(k), k == 0, k == 7, (rb_, hTb), (pgab,))
                    rt2, rb2 = wget(("br", m))
                    bw = rt2[:, 0:1024].rearrange("p (t c n) -> p t c n", t=2, c=4)
                    pbr, pbrb = PS()
                    pba, pbab = PS()
                    for c in range(4):
                        MM(pbr[:], bw[:, 0, c, :], yrT[:, c, :], c == 0, c == 3, (rb2, yrTb), (pbrb,))
                    for c in range(4):
                        MM(pba[:], bw[:, 1, c, :], yaT[:, c, :], c == 0, c == 3, (rb2, yaTb), (pbab,))
                    (fa, fab), (fb, fbb), (fc, fcb), (fd, fdb) = tmpf[0:4]
                    ACT(fa[:], pgr[:], AF.Tanh, (pgrb,), (fab,), scale=0.5)
                    ACT(fb[:], pga[:], AF.Tanh, (pgab,), (fbb,), scale=0.5)
                    STT(fc[:], fa[:], 1.0, pbr[:], ALU.add, ALU.mult, (fab, pbrb), (fcb,))
                    STT(fd[:], fb[:], 1.0, pba[:], ALU.add, ALU.mult, (fbb, pbab), (fdb,))
                    PSrel(pgrb, pgab, pbrb, pbab)
                    TT_("dve", mgT[:, m, :], fc[:], fd[:], ALU.add, (fcb, fdb), (mgTb,))
                for h in range(2):
                    pbs = [PS() for _ in range(4)]
                    for q in range(2):
                        rt, rb_ = wget(("wo", h, q))
                        w = rt[:].rearrange("p (k n) -> p k n", n=512)
                        for mm in range(4):
                            m = q * 4 + mm
                            for s in range(4):
                                MM(pbs[s][0][:], mgT[:, m, s * 128:(s + 1) * 128], w[:, mm, :], m == 0, m == 7, (mgTb, rb_), (pbs[s][1],))
                    for s in range(4):
                        xs = xt[:, s, h * 512:(h + 1) * 512]
                        STT(xs, pbs[s][0][:], 0.5, xs, ALU.mult, ALU.add, (pbs[s][1], xtb), (xtb,))
                        PSrel(pbs[s][1])

        xv = dr["x"].rearrange("(t s p) d -> t p s d", s=4, p=128)
        yv = y_out.rearrange("(t s p) d -> t p s d", s=4, p=128)

        def finish(ti):
            xt, xtb = xb[ti % 3]
            rms_stats(xt, xtb)
            for s in range(4):
                STT(xt[:, s, :], xt[:, s, :], rstd[:, s:s + 1], fgain[:], ALU.mult, ALU.mult, (xtb, rstdb, cb), (xtb,))
            DMA("act", yv[ti], xt[:], (xtb,), (), xtb)
            yield

        def side_stream(ti):
            if ti >= 1:
                if do_ffn2:
                    yield from tagged(ffn(1, xb[(ti - 1) % 3][0], xb[(ti - 1) % 3][1]), "ffn2")
                yield from tagged(finish(ti - 1), "fin")
            if ti + 1 < NT and do_ffn1:
                yield from tagged(ffn(0, xb[(ti + 1) % 3][0], xb[(ti + 1) % 3][1]), "ffn1")

        DMA("sp", xb[0][0][:], xv[0], (), (xb[0][1],), xb[0][1])
        interleave(tagged(ffn(0, xb[0][0], xb[0][1]), "ffn1") if do_ffn1 else iter(()),
                   tagged(staged_prep(), "prep") if do_mix else None, 0.3)
        P.barrier()
        staged_ready["v"] = True
        for ti in range(NT):
            xt, xtb = xb[ti % 3]
            if ti + 1 < NT:
                xn, xnb = xb[(ti + 1) % 3]
                DMA("sp", xn[:], xv[ti + 1], (), (xnb,), xnb)
            side = side_stream(ti)
            if do_mix:
                P.phase = "mix"
                mixer_pre(ti, xt, xtb)
                if overlap:
                    interleave(mixer_blocks(ti), side, ratio)
                else:
                    run_all(mixer_blocks(ti))
                    run_all(side)
                P.phase = "branch"
                mixer_post(ti, xt, xtb)
            else:
                run_all(side)
        if do_ffn2:
            run_all(tagged(ffn(1, xb[(NT - 1) % 3][0], xb[(NT - 1) % 3][1]), "ffn2"))
        run_all(tagged(finish(NT - 1), "fin"))

        if order is None:
            return None, None, rec
        P.resolve()
        P.prepare(nc, st)
        with nc.Block() as block:
            P.emit(block)
    return nc, P, None


def build(T=4096, **kw):
    _, _, rec = _build(T, order=None, **kw)
    nc, P, _ = _build(T, order=rec, **kw)
    return nc, P


def host_consts():
    s = np.arange(128)[:, None]
    t = np.arange(128)[None, :]
    tri_i = (s <= t).astype(np.float32)
    tri_s = (s < t).astype(np.float32)
    c = {
        "c_ident": np.eye(128, dtype=np.float32),
        "c_tri_i": tri_i,
        "c_tri_s": tri_s,
        "c_blk": ((s // 64) == (t // 64)).astype(np.float32),
        "c_mask4": np.concatenate([tri_s, tri_i, tri_s, tri_i], axis=1),
        "c_maskl": (s > t).astype(np.float32),
        "c_amo": tri_i.copy(),
        "c_amp": (s > t).astype(np.float32),
    }
    tt = np.arange(128, dtype=np.float64)
    c["c_tb"] = np.stack([-0.5 * C0 * (tt + 1), 0.5 * C0 * (tt + 1), -0.5 * C0 * tt,
                          np.full(128, -0.5 * C0 * 128)], axis=1).astype(np.float32)
    return c


def make_in_map(inp, xs):
    f = lambda a: np.ascontiguousarray(np.asarray(a, dtype=np.float32))
    col = lambda g: f(np.asarray(g).reshape(8, 128).T)
    m = {
        "x": f(xs),
        "w_g1": f(inp["ffn1_w_gate"][0]), "w_u1": f(inp["ffn1_w_up"][0]), "w_d1": f(inp["ffn1_w_down"][0]),
        "w_g2": f(inp["ffn2_w_gate"][0]), "w_u2": f(inp["ffn2_w_up"][0]), "w_d2": f(inp["ffn2_w_down"][0]),
        "w_in": f(inp["w_in"][0]),
        "w_br": f(inp["w_branch_rwkv"][0]), "w_ba": f(inp["w_branch_attn"][0]), "w_out": f(inp["w_out"][0]),
        "g1c": col(inp["ffn1_norm"][0]), "gmc": col(inp["mix_norm"][0]), "g2c": col(inp["ffn2_norm"][0]),
        "fgain": f(np.asarray(inp["final_norm"][0]).reshape(1, D)),
        "mu": f(np.asarray(inp["rwkv_mu"][0]).reshape(1, 1696)),
        "p_kk": f(np.asarray(inp["rwkv_k_k"][0]).reshape(1, 512)),
        "p_ka": f(np.asarray(inp["rwkv_k_a"][0]).reshape(1, 512)),
        "p_rk": f(np.asarray(inp["rwkv_r_k"][0]).reshape(1, 512)),
        "p_lnw": f(np.asarray(inp["rwkv_ln_w"][0]).reshape(1, 512)),
        "p_lnb": f(np.asarray(inp["rwkv_ln_b"][0]).reshape(1, 512)),
        "p_w0": f(np.asarray(inp["rwkv_w0"][0]).reshape(1, 512)),
        "p_a0": f(np.asarray(inp["rwkv_a0"][0]).reshape(1, 512)),
        "p_wl": f(inp["rwkv_w_lora_up"][0]), "p_al": f(inp["rwkv_a_lora_up"][0]), "p_gl": f(inp["rwkv_g_lora_up"][0]),
        "gqc": f(np.tile(np.asarray(inp["attn_q_norm"][0]).reshape(64), 2).reshape(128, 1)),
        "gkc": f(np.tile(np.asarray(inp["attn_k_norm"][0]).reshape(64), 2).reshape(128, 1)),
        "sinks": f(np.asarray(inp["attn_sinks"][0]).reshape(1, 8)),
    }
    m.update(host_consts())
    return m


_CACHE = {}


def kernel(**inputs):
    x = np.asarray(inputs["x"], dtype=np.float32)
    B, T, _ = x.shape
    key = (T,)
    if key not in _CACHE:
        _CACHE[key] = build(T)[0]
    nc = _CACHE[key]
    in_maps = [make_in_map(inputs, x[b]) for b in range(B)]
    res = run_bass_kernel_spmd(nc, in_maps, core_ids=list(range(B)))
    return np.stack([np.asarray(r["y"], dtype=np.float32) for r in res.results], axis=0)
```

```python
import math
from contextlib import ExitStack

import numpy as np
import concourse.bass as bass
import concourse.mybir as mybir
from concourse.bass_utils import run_bass_kernel_spmd

F32 = mybir.dt.float32
BF16 = mybir.dt.bfloat16
AF = mybir.ActivationFunctionType
ALU = mybir.AluOpType
AX = mybir.AxisListType

D = 1024
DFF = 2816
NF = 22
TT = 512
RW = 512
C0 = math.exp(-0.5)
RMS_EPS = 1e-6
GN_EPS = 64e-5
RING = 4
SLOT = 2048
VW = 80


class Buf:
    __slots__ = ("name", "last_w", "readers", "sem", "dma_cnt")

    def __init__(self, name):
        self.name = name
        self.last_w = None
        self.readers = []
        self.sem = None
        self.dma_cnt = 0


class Op:
    __slots__ = ("eng", "fn", "reads", "writes", "dma", "track", "deps", "signal", "tok", "barrier", "phase")

    def __init__(self, eng, fn, reads, writes, dma, track):
        self.eng = eng
        self.fn = fn
        self.reads = reads
        self.writes = writes
        self.dma = dma
        self.track = track
        self.deps = []
        self.signal = False
        self.tok = None
        self.barrier = False


ENGS = ("pe", "act", "dve", "pool", "sp")


class Prog:
    def __init__(self):
        self.ops = []
        self.bufs = []
        self.phase = "setup"

    def buf(self, name):
        b = Buf(name)
        self.bufs.append(b)
        return b

    def add(self, eng, fn, reads=(), writes=(), dma=False, track=None):
        op = Op(eng, fn, tuple(reads), tuple(writes), dma, track)
        op.phase = self.phase
        self.ops.append(op)
        return op

    def barrier(self):
        for e in ENGS:
            op = Op(e, None, (), (), False, None)
            op.barrier = True
            op.phase = self.phase
            self.ops.append(op)

    def resolve(self):
        ops = self.ops
        last_on_eng = {e: None for e in ENGS}
        dma_ops = []
        for i, op in enumerate(ops):
            if op.barrier:
                deps = set()
                for e in ENGS:
                    if e != "sp" and last_on_eng[e] is not None and e != op.eng:
                        deps.add(last_on_eng[e])
                deps.update(dma_ops)
                op.deps = sorted(deps)
                for j in op.deps:
                    ops[j].signal = True
                continue
            deps = set()

            def need(j, kind):
                p = ops[j]
                if p.dma or op.dma:
                    if p.dma and op.dma and kind == "waw" and p.track is op.track:
                        return
                    deps.add(j)
                    return
                if p.eng == op.eng:
                    if op.eng == "pe":
                        return
                deps.add(j)

            for b in op.reads:
                if b.last_w is not None:
                    need(b.last_w, "raw")
            for b in op.writes:
                if b.last_w is not None:
                    need(b.last_w, "waw")
                lastr = {}
                for j in b.readers:
                    p = ops[j]
                    if p.dma:
                        need(j, "war")
                    else:
                        lastr[p.eng] = j
                for j in lastr.values():
                    need(j, "war")
            for b in op.reads:
                b.readers.append(i)
            for b in op.writes:
                b.last_w = i
                b.readers = []
            op.deps = sorted(deps)
            for j in op.deps:
                ops[j].signal = True
            if op.dma:
                dma_ops.append(i)
            else:
                last_on_eng[op.eng] = i

    def prepare(self, nc, stack):
        ops = self.ops
        esem = {}
        for e in ("pe", "act", "dve", "pool"):
            esem[e] = stack.enter_context(nc.semaphore("es_" + e))
        for b in self.bufs:
            b.dma_cnt = 0
        tracked = []
        for op in ops:
            if op.dma:
                t = op.track
                if t.sem is None:
                    t.sem = stack.enter_context(nc.semaphore("ds_" + t.name))
                    tracked.append(t)
        cnt = {e: 0 for e in ENGS}
        for op in ops:
            if op.barrier:
                continue
            if op.dma:
                op.track.dma_cnt += 1
                op.tok = (op.track.sem, 16 * op.track.dma_cnt)
            elif op.signal:
                cnt[op.eng] += 1
                op.tok = (esem[op.eng], cnt[op.eng])
        know = {e: {} for e in ENGS}
        clocks = {}
        nw = 0
        for i, op in enumerate(ops):
            kn = know[op.eng]
            w = {}
            for j in op.deps:
                s, v = ops[j].tok
                if kn.get(s, 0) < v and w.get(s, (0, None))[0] < v:
                    w[s] = (v, j)
            waits = []
            for s, (v, j) in w.items():
                if kn.get(s, 0) >= v:
                    continue
                waits.append((s, v))
                for s2, v2 in clocks[j].items():
                    if kn.get(s2, 0) < v2:
                        kn[s2] = v2
                kn[s] = max(kn.get(s, 0), v)
            op.deps = waits
            nw += len(waits)
            if op.tok is not None:
                c = dict(kn)
                if not op.dma:
                    c[op.tok[0]] = op.tok[1]
                else:
                    c[op.tok[0]] = max(c.get(op.tok[0], 0), op.tok[1])
                clocks[i] = c
        self.nwaits = nw
        per = {e: [] for e in ENGS}
        for op in ops:
            per[op.eng].append(op)
        self.stats = {e: len(per[e]) for e in ENGS}
        self._emit_state = (ops, esem, tracked, per)

    def emit(self, block):
        ops, esem, tracked, per = self._emit_state

        def run(name, e):
            seen = {}
            for op in per[name]:
                for s, v in op.deps:
                    if seen.get(s, 0) < v:
                        e.wait_ge(s, v)
                        seen[s] = v
                if op.barrier:
                    continue
                ins = op.fn(e)
                if op.dma:
                    ins.then_inc(op.track.sem, 16)
                elif op.signal:
                    ins.then_inc(esem[name], 1)
            if name == "sp":
                for t in tracked:
                    if seen.get(t.sem, 0) < 16 * t.dma_cnt:
                        e.wait_ge(t.sem, 16 * t.dma_cnt)

        @block.sync
        def _(e):
            run("sp", e)

        @block.tensor
        def _(e):
            run("pe", e)

        @block.scalar
        def _(e):
            run("act", e)

        @block.vector
        def _(e):
            run("dve", e)

        @block.gpsimd
        def _(e):
            run("pool", e)


W_SPECS = [
    ("x", None, F32),
    ("w_g1", [D, DFF], F32), ("w_u1", [D, DFF], F32), ("w_d1", [DFF, D], F32),
    ("w_g2", [D, DFF], F32), ("w_u2", [D, DFF], F32), ("w_d2", [DFF, D], F32),
    ("w_in", [D, 4512], F32),
    ("w_br", [512, D], F32), ("w_ba", [512, D], F32), ("w_out", [D, D], F32),
    ("g1c", [128, 8], F32), ("gmc", [128, 8], F32), ("g2c", [128, 8], F32),
    ("fgain", [1, D], F32), ("mu", [1, 1696], F32),
    ("p_kk", [1, 512], F32), ("p_ka", [1, 512], F32), ("p_rk", [1, 512], F32),
    ("p_lnw", [1, 512], F32), ("p_lnb", [1, 512], F32),
    ("p_w0", [1, 512], F32), ("p_a0", [1, 512], F32),
    ("p_wl", [32, 512], F32), ("p_al", [32, 512], F32), ("p_gl", [96, 512], F32),
    ("gqc", [128, 1], F32), ("gkc", [128, 1], F32), ("sinks", [1, 8], F32),
    ("c_ident", [128, 128], F32), ("c_tri_i", [128, 128], F32), ("c_tri_s", [128, 128], F32),
    ("c_blk", [128, 128], F32), ("c_mask4", [128, 512], F32), ("c_maskl", [128, 128], F32),
    ("c_amo", [128, 128], F32), ("c_amp", [128, 128], F32), ("c_tb", [128, 4], F32),
]


def _build(T=4096, do_ffn1=True, do_mix=True, do_ffn2=True, dbg=None, skip=(), stage=9, order=None, overlap=True, ratio=1.0):
    NT = T // TT
    nc = bass.Bass("TRN2", target_bir_lowering=False)
    P = Prog()
    dr = {}
    for name, shp, dt in W_SPECS:
        if name == "x":
            shp = [T, D]
        dr[name] = nc.dram_tensor(name, shp, dt, kind="ExternalInput").ap()
    y_out = nc.dram_tensor("y", [T, D], F32, kind="ExternalOutput").ap()
    dbg_out = None
    if dbg:
        dbg_out = nc.dram_tensor("dbg", [128, dbg], F32, kind="ExternalOutput").ap()

    def scr(name, shape):
        return nc.dram_tensor(name, shape, BF16, kind="Internal").ap()

    s_gu = [scr("s_gu%d" % i, [NF, 128, 2048]) for i in range(2)]
    s_d = [scr("s_d%d" % i, [2, NF, 128, 512]) for i in range(2)]
    s_lora = scr("s_lora", [128, 16 * 64])
    s_xg = scr("s_xg", [128, 16 * 96])
    s_q = scr("s_q", [4, 128, 1024])
    s_kd = scr("s_kd", [2, 128, 1024])
    s_gate = scr("s_gate", [8, 128, 2048])
    s_rkv = scr("s_rkv", [3, 4, 128, 2048])
    s_av = scr("s_av", [128, 1024])
    s_br = scr("s_br", [8, 128, 1024])
    s_wo = scr("s_wo", [2, 2, 128, 2048])

    with ExitStack() as st:
        def sb(name, shape, dt=F32):
            return st.enter_context(nc.sbuf_tensor(name, shape, dt))

        def sbb(name, shape, dt=F32):
            return sb(name, shape, dt), P.buf(name)

        banks = []
        for i in range(8):
            t = st.enter_context(nc.psum_tensor("pb%d" % i, [128, 512], F32))
            banks.append((t, P.buf("pb%d" % i)))
        bank_free = list(range(8))
        bank_of = {b_: i_ for i_, (_, b_) in enumerate(banks)}

        def PS():
            assert bank_free, "out of PSUM banks (too many concurrently open)"
            return banks[bank_free.pop(0)]

        def PSrel(*bufs_):
            for b_ in bufs_:
                assert bank_of[b_] not in bank_free
                bank_free.append(bank_of[b_])

        def MM(out, lhsT, rhs, start, stop, reads, writes):
            P.add("pe", lambda e: e.matmul(out, lhsT=lhsT, rhs=rhs, start=start, stop=stop), reads, writes)

        def TR(out, in_, ident, reads, writes):
            P.add("pe", lambda e: e.transpose(out=out, in_=in_, identity=ident), reads, writes)

        def ACT(out, in_, func, reads, writes, scale=None, bias=None, accum=None):
            kw = {}
            if scale is not None:
                kw["scale"] = scale
            if bias is not None:
                kw["bias"] = bias
            if accum is not None:
                kw["accum_out"] = accum
            P.add("act", lambda e: e.activation(out=out, in_=in_, func=func, **kw), reads, writes)

        def TT_(eng, out, in0, in1, op, reads, writes):
            P.add(eng, lambda e: e.tensor_tensor(out=out, in0=in0, in1=in1, op=op), reads, writes)

        def TS(eng, out, in0, s1, s2, op0, op1, reads, writes):
            if op1 is None:
                P.add(eng, lambda e: e.tensor_scalar(out=out, in0=in0, scalar1=s1, scalar2=None, op0=op0), reads, writes)
            else:
                P.add(eng, lambda e: e.tensor_scalar(out=out, in0=in0, scalar1=s1, scalar2=s2, op0=op0, op1=op1), reads, writes)

        def STT(out, in0, scalar, in1, op0, op1, reads, writes):
            P.add("dve", lambda e: e.scalar_tensor_tensor(out=out, in0=in0, scalar=scalar, in1=in1, op0=op0, op1=op1), reads, writes)

        def CP(eng, out, in_, reads, writes):
            if eng == "act":
                ACT(out, in_, AF.Copy, reads, writes)
            else:
                P.add(eng, lambda e: e.tensor_copy(out=out, in_=in_), reads, writes)

        def RED(out, in_, reads, writes):
            P.add("dve", lambda e: e.tensor_reduce(out=out, in_=in_, axis=AX.X, op=ALU.add), reads, writes)

        def DMA(eng, out, in_, reads, writes, track):
            P.add(eng, lambda e: e.dma_start(out=out, in_=in_), reads, writes, dma=True, track=track)

        def MEMSET(eng, ap, val, writes):
            P.add(eng, lambda e: e.memset(ap, val), (), writes)

        ARENA = 23808
        arena = sb("arena", [128, ARENA], BF16)
        aoff = [0]

        def carve(name, shape, dt=F32):
            n = 1
            for s_ in shape[1:]:
                n *= s_
            ne = n * (2 if dt == F32 else 1)
            assert aoff[0] + ne <= ARENA, (name, aoff[0], ne)
            ap = arena[0:shape[0], aoff[0]:aoff[0] + ne]
            aoff[0] += ne
            if dt == F32:
                ap = ap.bitcast(F32)
            if len(shape) == 3:
                ap = ap.rearrange("p (a b) -> p a b", b=shape[2])
            elif len(shape) == 4:
                ap = ap.rearrange("p (a b c) -> p a b c", b=shape[2], c=shape[3])
            return ap

        def carveb(name, shape, dt=F32):
            return carve(name, shape, dt), P.buf(name)

        cb = P.buf("consts")

        def cload(name, shape, src=None, bcast=False, temp=False):
            t = carve("c_" + name, shape, F32) if temp else sb("c_" + name, shape, F32)
            s = dr[name] if src is None else src
            if bcast:
                DMA("sp", t[:], s.partition_broadcast(shape[0]), (), (cb,), cb)
            else:
                DMA("sp", t[:], s[:, :], (), (cb,), cb)
            return t

        identf = cload("c_ident", [128, 128], temp=True)
        tri_i = cload("c_tri_i", [128, 128])
        tri_s = cload("c_tri_s", [128, 128])
        blk = cload("c_blk", [128, 128])
        mask4f = cload("c_mask4", [128, 512], temp=True)
        masklf = cload("c_maskl", [128, 128], temp=True)
        amof = cload("c_amo", [128, 128], temp=True)
        ampf = cload("c_amp", [128, 128], temp=True)
        tbias = cload("c_tb", [128, 4])
        g1c = cload("g1c", [128, 8])
        gmc = cload("gmc", [128, 8])
        g2c = cload("g2c", [128, 8])
        gqc = cload("gqc", [128, 1])
        gkc = cload("gkc", [128, 1], temp=True)
        fgain = cload("fgain", [128, D], bcast=True)
        kkb = cload("p_kk", [128, 512], bcast=True)
        kab = cload("p_ka", [128, 512], bcast=True)
        rkb = cload("p_rk", [128, 512], bcast=True)
        lnwb = cload("p_lnw", [128, 512], bcast=True)
        lnbb = cload("p_lnb", [128, 512], bcast=True)
        sinkb = cload("sinks", [128, 8], bcast=True, temp=True)
        w0f = cload("p_w0", [1, 512], temp=True)
        a0f = cload("p_a0", [1, 512], temp=True)
        wl_f = carve("wl_f", [64, 512], F32)
        DMA("sp", wl_f[32:64, :], dr["p_wl"][:, :], (), (cb,), cb)
        al_f = cload("p_al", [32, 512], temp=True)
        gl_f = cload("p_gl", [96, 512], temp=True)

        cb2 = P.buf("consts2")
        identb = sb("identb", [128, 128], BF16)
        mask4 = sb("mask4", [128, 512], BF16)
        maskl = sb("maskl", [128, 128], BF16)
        amo = sb("amo", [128, 128], BF16)
        amp = sb("amp", [128, 128], BF16)
        rhs_w = sb("rhs_w", [66, 512], BF16)
        rhs_a = sb("rhs_a", [66, 512], BF16)
        gl_t = sb("gl_t", [96, 512], BF16)
        w0hl = carve("w0hl", [2, 512], BF16)
        a0b = carve("a0b", [1, 512], BF16)
        onesc = sb("onesc", [128, 1], F32)
        nhalf = sb("nhalf", [128, 8], F32)
        sinkexp = sb("sinkexp", [128, 8], F32)
        gk8 = sb("gk8", [128, 1], F32)
        w0tmp = carve("w0tmp", [1, 1024], F32)
        CP("dve", identb[:], identf[:], (cb,), (cb2,))
        CP("dve", mask4[:], mask4f[:], (cb,), (cb2,))
        CP("dve", maskl[:], masklf[:], (cb,), (cb2,))
        CP("dve", amo[:], amof[:], (cb,), (cb2,))
        CP("dve", amp[:], ampf[:], (cb,), (cb2,))
        rwb_ = P.buf("rhs_wa")
        MEMSET("pool", rhs_w[:], 0.0, (rwb_,))
        MEMSET("pool", rhs_a[:], 0.0, (rwb_,))
        CP("dve", rhs_w[32:64, :], wl_f[32:64, :], (cb, rwb_), (rwb_,))
        CP("dve", rhs_a[0:32, :], al_f[:], (cb, rwb_), (rwb_,))
        TS("dve", gl_t[:], gl_f[:], 0.5, None, ALU.mult, None, (cb,), (cb2,))
        pass
        MEMSET("pool", onesc[:], 1.0, (cb2,))
        MEMSET("pool", nhalf[:], -0.5, (cb2,))
        eps64 = sb("eps64", [128, 1], F32)
        MEMSET("pool", eps64[:], 64 * RMS_EPS, (cb2,))
        ACT(sinkexp[:], sinkb[:], AF.Exp, (cb,), (cb2,))
        TS("dve", gk8[:], gkc[:], 8.0, None, ALU.mult, None, (cb,), (cb2,))
        w0b_ = P.buf("w0b")
        CP("dve", w0hl[0:1, :], w0f[:], (cb,), (w0b_,))
        CP("dve", w0tmp[:, 0:512], w0hl[0:1, :], (w0b_,), (w0b_,))
        TT_("dve", w0tmp[:, 512:1024], w0f[:], w0tmp[:, 0:512], ALU.subtract, (cb, w0b_), (w0b_,))
        w0lo = carve("w0lo", [1, 512], BF16)
        CP("dve", w0lo[:], w0tmp[:, 512:1024], (w0b_,), (w0b_,))
        DMA("sp", rhs_w[64:65, :], w0hl[0:1, :], (w0b_, rwb_), (cb2,), cb2)
        DMA("sp", rhs_w[65:66, :], w0lo[:], (w0b_, rwb_), (cb2,), cb2)
        a0b_ = P.buf("a0b")
        CP("dve", a0b[:], a0f[:], (cb,), (a0b_,))
        DMA("sp", rhs_a[64:65, :], a0b[:], (a0b_, rwb_), (cb2,), cb2)

        xb = [sbb("xb%d" % i, [128, 4, D], F32) for i in range(3)]
        hT, hTb = sbb("hT", [128, 8, 520], BF16)
        hTf, hTfb = sbb("hTf", [128, 8, 512], BF16)
        actT, actTb = sbb("actT", [128, 8, 512], BF16)
        sgt = [sbb("sgt%d" % i, [128, 512], F32) for i in range(2)]
        junk, junkb = sgt[0][0][:].bitcast(BF16), sgt[0][1]
        hb = [(sgt[1][0][:].bitcast(BF16), sgt[1][1])]
        ssq, ssqb = sbb("ssq", [128, 8], F32)
        rstd, rstdb = sbb("rstd", [128, 8], F32)
        ring = [sbb("ring%d" % i, [128, SLOT], BF16) for i in range(RING)]

        P.barrier()
        P.phase = "prep"
        aoff[0] = 0
        NSTF, NSTB = 4, 4
        stf = [carveb("stf%d" % i, [128, 1408], F32) for i in range(NSTF)]
        stb = [carveb("stb%d" % i, [128, 1408], BF16) for i in range(NSTB)]
        mub = carve("mub", [128, 1696], F32)
        omub = carve("omub", [128, 1696], F32)
        DMA("sp", mub[:], dr["mu"].partition_broadcast(128), (), (cb,), cb)
        TS("dve", omub[:], mub[:], -1.0, 1.0, ALU.mult, ALU.add, (cb,), (cb2,))
        pc = {"f": 0, "b": 0, "e": 0}

        def prep_piece(src, ncol, variants):
            sf, sfb = stf[pc["f"] % NSTF]
            pc["f"] += 1
            DMA("sp", sf[:, 0:ncol], src, (), (sfb,), sfb)
            for (rs, const, cs, stores) in variants:
                so, sob = stb[pc["b"] % NSTB]
                pc["b"] += 1
                if cs is not None:
                    TT_("dve", so[:, 0:ncol], sf[:, 0:ncol], cs, ALU.mult, (sfb, cb, cb2), (sob,))
                else:
                    eng = ("dve", "act")[pc["e"] % 2]
                    pc["e"] += 1
                    if eng == "act":
                        assert rs is None or const == 1.0
                        ACT(so[:, 0:ncol], sf[:, 0:ncol], AF.Copy, (sfb, cb), (sob,), scale=(const if rs is None else rs))
                    elif rs is None:
                        TS(eng, so[:, 0:ncol], sf[:, 0:ncol], const, None, ALU.mult, None, (sfb,), (sob,))
                    else:
                        TS(eng, so[:, 0:ncol], sf[:, 0:ncol], rs, const, ALU.mult, ALU.mult, (sfb, cb), (sob,))
                for (dst, src_ap) in stores(so):
                    DMA("sp", dst, src_ap, (sob,), (), sob)

        sbuf_ = {}

        def sbufof(name):
            if name not in sbuf_:
                sbuf_[name] = P.buf("scr_" + name)
            return sbuf_[name]

        def CAST(name, dst, src_):
            b_ = sbufof(name)
            ph = P.phase
            P.phase = "cast"
            P.add("pool", lambda e: e.dma_start(out=dst, in_=src_), (), (b_,), dma=True, track=b_)
            P.phase = ph

        pending = {"gen": None, "n": 0}
        CAST_NEED = {"gu0b": 16, "d0": 18, "q": 42, "kd": 42, "av": 42, "gate": 42, "br": 50, "wo": 58, "gu1": 90, "d1": 92}

        def drip(n):
            g = pending["gen"]
            if g is None:
                return
            for _ in range(n):
                try:
                    next(g)
                    pending["n"] += 1
                except StopIteration:
                    pending["gen"] = None
                    return

        def ensure_cast(sname):
            need_ = CAST_NEED.get(sname)
            if need_ is not None and pending["gen"] is not None and pending["n"] < need_:
                drip(need_ - pending["n"])

        def cast_ffn(fi, fgroups=((0, NF, ""),), with_d=True):
            gn, un, dn = (("w_g1", "w_u1", "w_d1"), ("w_g2", "w_u2", "w_d2"))[fi]
            guv = s_gu[fi].rearrange("f p (t k m) -> p f t k m", t=2, k=8)
            for (fa, fb, sfx) in fgroups:
                for t_, wn in enumerate((gn, un)):
                    for k in range(8):
                        CAST("gu%d%s" % (fi, sfx), guv[:, fa:fb, t_, k, :],
                             dr[wn][k * 128:(k + 1) * 128, fa * 128:fb * 128].rearrange("p (f m) -> p f m", m=128))
                        yield
            for half in range(2):
                if with_d:
                    CAST("d%d" % fi, s_d[fi][half].rearrange("f p n -> (f p) n"), dr[dn][:, half * 512:(half + 1) * 512])
                    yield

        def cast_mix():
            qv = s_q.rearrange("m p (k n) -> p m k n", k=8)
            kdv = s_kd.rearrange("g p (k n) -> p g k n", k=8)
            avv = s_av.rearrange("p (k n) -> p k n", k=8)
            gtv = s_gate.rearrange("m p (t k n) -> p t m k n", t=2, k=8)
            for k in range(8):
                rows = dr["w_in"][k * 128:(k + 1) * 128, :]
                CAST("q", qv[:, :, k, :], rows[:, 1696:2208].rearrange("p (m n) -> p m n", n=128))
                for g in range(2):
                    for hf in range(2):
                        CAST("kd", kdv[:, g, k, hf * 64:(hf + 1) * 64], rows[:, 2208 + g * 64:2272 + g * 64])
                CAST("av", avv[:, k, :], rows[:, 2336:2464])
                yield
                for t_ in range(2):
                    CAST("gate", gtv[:, t_, :, k, :], rows[:, 2464 + t_ * 1024:3488 + t_ * 1024].rearrange("p (m n) -> p m n", n=128))
                    yield
            brv = s_br.rearrange("m p (t c n) -> p t c m n", t=2, c=4)
            for t_, wn in enumerate(("w_br", "w_ba")):
                for c in range(4):
                    CAST("br", brv[:, t_, c, :, :], dr[wn][c * 128:(c + 1) * 128, :].rearrange("p (m n) -> p m n", n=128))
                    yield
            wov = s_wo.rearrange("h q p (k n) -> h q p k n", k=4)
            for m in range(8):
                for h in range(2):
                    CAST("wo", wov[h, m // 4, :, m % 4, :], dr["w_out"][m * 128:(m + 1) * 128, h * 512:(h + 1) * 512])
                yield

        if do_ffn1:
            for _ in cast_ffn(0, ((0, 8, "a"),), with_d=False):
                pass
        def staged_prep():
            rkvv = s_rkv.rearrange("c q p (k n) -> c q p k n", k=4)
            lorav = s_lora.rearrange("p (k m) -> p k m", m=64)
            xgv = s_xg.rearrange("p (k m) -> p k m", m=96)
            for k in range(8):
                def stores_a1(so, kc):
                    return [(rkvv[c, kc // 4, :, kc % 4, :], so[:, c * 512:(c + 1) * 512]) for c in range(2)]

                def stores_a2(so, kc):
                    return [(rkvv[2, kc // 4, :, kc % 4, :], so[:, 0:512]),
                            (lorav[:, kc, 0:32], so[:, 544:576]),
                            (lorav[:, kc, 32:64], so[:, 512:544]),
                            (xgv[:, kc, :], so[:, 576:672])]
                prep_piece(dr["w_in"][k * 128:(k + 1) * 128, 0:1024], 1024,
                           [(None, 1.0, omub[:, 0:1024], lambda so, k=k: stores_a1(so, k)),
                            (None, 1.0, mub[:, 0:1024], lambda so, k=k: stores_a1(so, 8 + k))])
                yield
                prep_piece(dr["w_in"][k * 128:(k + 1) * 128, 1024:1696], 672,
                           [(None, 1.0, omub[:, 1024:1696], lambda so, k=k: stores_a2(so, k)),
                            (None, 1.0, mub[:, 1024:1696], lambda so, k=k: stores_a2(so, 8 + k))])
                yield


        def cast2():
            if do_ffn1:
                yield from cast_ffn(0, ((8, NF, "b"),))
            if do_mix:
                yield from cast_mix()
            if do_ffn2:
                yield from cast_ffn(1)
        pending["gen"] = cast2()

        rec = []

        def wresolve(dsc):
            k = dsc[0]
            if k == "gu":
                return s_gu[dsc[1]][dsc[2]], 2048
            if k == "d":
                _, fi_, half, f0, nf = dsc
                return s_d[fi_][half, f0:f0 + nf].rearrange("f p n -> p f n"), nf * 512
            if k == "lora":
                return s_lora, 1024
            if k == "xg":
                return s_xg, 1536
            if k == "q":
                return s_q[dsc[1]], 1024
            if k == "kd":
                return s_kd[dsc[1]], 1024
            if k == "rkv":
                return s_rkv[dsc[1], dsc[2]], 2048
            if k == "av":
                return s_av, 1024
            if k == "gate":
                return s_gate[dsc[1]], 2048
            if k == "br":
                return s_br[dsc[1]], 1024
            if k == "wo":
                return s_wo[dsc[1], dsc[2]], 2048
            raise KeyError(dsc)

        sidx = {"get": 0, "issued": 0}
        staged_ready = {"v": not do_mix}

        def wget(dsc):
            i = sidx["get"]
            sidx["get"] += 1
            if order is None:
                rec.append(dsc)
                return ring[i % RING]
            assert order[i] == dsc, (i, order[i], dsc)
            while sidx["issued"] < min(len(order), i + RING):
                j = sidx["issued"]
                src_, n = wresolve(order[j])
                rt, rb_ = ring[j % RING]
                knd = order[j][0]
                if knd in ("lora", "xg", "rkv") and not staged_ready["v"]:
                    break
                sname = knd + str(order[j][1]) if knd in ("gu", "d") else knd
                if sname == "gu0":
                    sname = "gu0a" if order[j][2] < 8 else "gu0b"
                ensure_cast(sname)
                rd = (sbuf_[sname],) if sname in sbuf_ else ()
                if len(src_.shape) == 3:
                    DMA("sp", rt[:, 0:n].rearrange("p (f n) -> p f n", n=512), src_, rd, (rb_,), rb_)
                else:
                    DMA("sp", rt[:, 0:n], src_, rd, (rb_,), rb_)
                sidx["issued"] += 1
            return ring[i % RING]

        Hf, _hfb = sbb("Hf", [128, 256], F32)
        Hfb = (_hfb, P.buf("Hf_hi"))
        hcar, hcarb = sbb("hcar", [128, 8, 1], BF16)
        MEMSET("pool", Hf[:], 0.0, Hfb)
        MEMSET("pool", hcar[:], 0.0, (hcarb,))

        def rms_stats(xt, xtb):
            for s in range(4):
                ACT(junk[:], xt[:, s, :], AF.Square, (xtb,), (junkb, ssqb), accum=ssq[:, s:s + 1])
            TS("dve", ssq[:, 4:8], ssq[:, 0:4], 1.0 / D, RMS_EPS, ALU.mult, ALU.add, (ssqb,), (ssqb,))
            TT_("pool", rstd[:, 0:4], ssq[:, 4:8], nhalf[:, 0:4], ALU.pow, (ssqb, cb2), (rstdb,))
            drip(24)

        def norm_T(xt, xtb, dst, dstb, col0, gcol):
            rms_stats(xt, xtb)
            for s in range(4):
                h_, hb_ = hb[0]
                TS("dve", h_[:], xt[:, s, :], rstd[:, s:s + 1], None, ALU.mult, None, (xtb, rstdb), (hb_,))
                pt, ptb = PS()
                pv = pt[:].bitcast(BF16)
                for k in range(8):
                    TR(pv[:, k * 128:(k + 1) * 128], h_[:, k * 128:(k + 1) * 128], identb[:], (hb_, cb2), (ptb,))
                TT_("dve", dst[:, :, col0 + s * 128:col0 + (s + 1) * 128], pv.rearrange("p (k n) -> p k n", n=128),
                    gcol[:, 0:8].unsqueeze(2).to_broadcast([128, 8, 128]), ALU.mult, (ptb, cb), (dstb,))
                PSrel(ptb)
                yield

        GROUPS = ((0, 8), (8, 8), (16, 6))

        def ffn(fi_, xt, xtb):
            yield from norm_T(xt, xtb, hTf, hTfb, 0, (g1c, g2c)[fi_])
            for (g0, gn) in GROUPS:
                for fi in range(gn):
                    rt, rb_ = wget(("gu", fi_, g0 + fi))
                    gu = rt[:].rearrange("p (t k m) -> p t k m", t=2, k=8)
                    pg, pgb = PS()
                    pu, pub = PS()
                    for k in range(8):
                        MM(pg[:], gu[:, 0, k, :], hTf[:, k, :], k == 0, k == 7, (rb_, hTfb), (pgb,))
                    sg, sgb = sgt[fi % 2]
                    ACT(sg[:], pg[:], AF.Tanh, (pgb,), (sgb,), scale=0.5)
                    yield
                    for k in range(8):
                        MM(pu[:], gu[:, 1, k, :], hTf[:, k, :], k == 0, k == 7, (rb_, hTfb), (pub,))
                    STT(sg[:], sg[:], 1.0, pg[:], ALU.add, ALU.mult, (sgb, pgb), (sgb,))
                    TT_("dve", actT[:, fi, :], sg[:], pu[:], ALU.mult, (sgb, pub), (actTb,))
                    PSrel(pgb, pub)
                    yield
                for half in range(2):
                    pbs = [PS() for _ in range(4)]
                    for f0 in range(0, gn, 4):
                        nf = min(4, gn - f0)
                        rt, rb_ = wget(("d", fi_, half, g0 + f0, nf))
                        dv = rt[:].rearrange("p (f n) -> p f n", n=512)
                        for ff in range(nf):
                            f = f0 + ff
                            for s in range(4):
                                MM(pbs[s][0][:], actT[:, f, s * 128:(s + 1) * 128], dv[:, ff, :], f == 0, f == gn - 1,
                                   (actTb, rb_), (pbs[s][1],))
                            if ff % 2 == 1 or ff == nf - 1:
                                yield
                    for s in range(4):
                        xs = xt[:, s, half * 512:(half + 1) * 512]
                        STT(xs, pbs[s][0][:], 0.25, xs, ALU.mult, ALU.add, (pbs[s][1], xtb), (xtb,))
                        PSrel(pbs[s][1])
                    yield

        def tagged(gen, tag):
            while True:
                P.phase = tag
                try:
                    next(gen)
                except StopIteration:
                    return
                yield

        def run_all(gen):
            for _ in gen:
                pass

        def interleave(main_gen, side_gen, ratio=1):
            side_done = side_gen is None
            acc = 0.0
            for _ in main_gen:
                acc += ratio
                while acc >= 1.0:
                    acc -= 1.0
                    if not side_done:
                        try:
                            next(side_gen)
                        except StopIteration:
                            side_done = True
            if not side_done:
                run_all(side_gen)

        if do_mix:
            rkv = [sbb("rkv%d" % c, [128, 4, 512], BF16) for c in range(3)]
            lT, lTb = sbb("lT", [66, 512], BF16)
            gTt, gTb = sbb("gTt", [96, 512], BF16)
            qT, qTb = sbb("qT", [128, 4, 512], BF16)
            kTd, kTdb = sbb("kTd", [128, 2, 2, 640], BF16)
            vaug, vaugb = sbb("vaug", [128, 5, 2, VW], BF16)
            yrT, yrTb = sbb("yrT", [128, 4, 512], BF16)
            yaT, yaTb = sbb("yaT", [128, 4, 512], BF16)
            mgT, mgTb = actT[:, 0:8, :], actTb
            aoff[0] = 0
            tmpf = [carveb("tf%d" % i, [128, 512], F32) for i in range(9)]
            tmpf.append(tmpf[0])
            tmpb = [carveb("tb%d" % i, [128, 512], BF16) for i in range(6)]
            tmpb.append(tmpb[4])
            arT, arTb = sbb("arT", [128, 4, 2, 128], BF16)
            bkT, bkTb = sbb("bkT", [128, 4, 2, 128], BF16)
            arZ, _b0 = sbb("arZ", [128, 4, 2, 2, 128], BF16)
            bZ, _b1 = sbb("bZ", [128, 4, 2, 128], BF16)
            Hz, _b2 = sbb("Hz", [128, 4, 128], BF16)
            arZb, bZb, Hzb = (_b0, P.buf("arZ_hi")), (_b1, P.buf("bZ_hi")), (_b2, P.buf("Hz_hi"))
            MEMSET("pool", arZ[:], 0.0, arZb)
            MEMSET("pool", bZ[:], 0.0, bZb)
            MEMSET("pool", Hz[:], 0.0, Hzb)
            MEMSET("pool", lT[:], 1.0, (lTb,))
            SCb = [carveb("SCb%d" % p_, [128, 2, 2, 128], BF16) for p_ in range(4)]
            SCk = [carveb("SCk%d" % p_, [128, 2, 2, 128], BF16) for p_ in range(4)]
            Pm = [[carveb("Pm%d_%d" % (a, g), [128, 4, 128], BF16) for g in range(2)] for a in range(2)]
            PTm = [[carveb("PTm%d_%d" % (a, g), [128, 4, 128], BF16) for g in range(2)] for a in range(2)]
            Qm = [[carveb("Qm%d_%d" % (a, g), [128, 4, 128], BF16) for g in range(2)] for a in range(2)]
            PTa = [sbb("PTa0", [128, 2, 512], BF16)] * 2
            yatt, yattb = sbb("yatt", [128, 512], BF16)
            gamC, gamCb = sbb("gamC", [128, 4], F32)
            sts = [sbb("sts%d" % i, [128, 16], F32) for i in range(7)]
            MEMSET("pool", vaug[:], 1.0, (vaugb,))
            MEMSET("pool", kTd[:], 0.0, (kTdb,))
            MEMSET("pool", yrT[:], 0.0, (yrTb,))
            MEMSET("pool", yaT[:], 0.0, (yaTb,))

            def v3(ap, d=64):
                return ap.rearrange("p (h d) -> p h d", d=d)

            def bc8(ap, n=8, d=64):
                return ap.unsqueeze(2).to_broadcast([128, n, d])

            def rhs_k(kc, lo=0, n=512):
                if kc < 8:
                    return hT[:, kc, 8 + lo:8 + lo + n]
                return hT[:, kc - 8, 7 + lo:7 + lo + n]

            def qk_norm(pq, pqb, gcol, dst, dstb, dsts=None):
                sq, sqb = tmpf[7]
                ACT(sq[:], pq[:], AF.Square, (pqb,), (sqb,))
                ps, psb = PS()
                MM(ps[:], blk[:], sq[:], True, True, (cb, sqb), (psb,))
                t_, tb_ = tmpf[8]
                ACT(t_[:], ps[:], AF.Ln, (psb, cb2), (tb_,), bias=eps64[:, 0:1])
                PSrel(psb)
                rs, rsb = tmpf[9]
                ACT(rs[:], t_[:], AF.Exp, (tb_,), (rsb,), scale=-0.5)
                if dsts is None:
                    dsts = [(slice(0, 128), dst)]
                for (rows, d_) in dsts:
                    STT(d_, pq[rows, :], gcol[rows, :], rs[rows, :], ALU.mult, ALU.mult, (pqb, rsb, cb, cb2), (dstb,))
                PSrel(pqb)

            def attn_block(b, gb):
                for g in range(2):
                    po, pob = PS()
                    if gb > 0:
                        pp, ppb = PS()
                    for hh in range(4):
                        h = 4 * g + hh
                        ch, hf = divmod(h, 2)
                        qs = qT[:, ch, b * 128:(b + 1) * 128]
                        MM(po[:, hh * 128:(hh + 1) * 128], kTd[:, g, hf, (1 + b) * 128:(2 + b) * 128], qs, True, True,
                           (kTdb, qTb), (pob,))
                        if gb > 0:
                            MM(pp[:, hh * 128:(hh + 1) * 128], kTd[:, g, hf, b * 128:(b + 1) * 128], qs, True, True,
                               (kTdb, qTb), (ppb,))
                    pt_, ptb_ = PTa[g]
                    ACT(pt_[:, 1, :], po[:], AF.Exp, (pob,), (ptb_,))
                    PSrel(pob)
                    TT_("dve", v3(pt_[:, 1, :], 128), v3(pt_[:, 1, :], 128), amo[:].unsqueeze(1).to_broadcast([128, 4, 128]),
                        ALU.mult, (ptb_, cb2), (ptb_,))
                    if gb > 0:
                        ACT(pt_[:, 0, :], pp[:], AF.Exp, (ppb,), (ptb_,))
                        PSrel(ppb)
                        TT_("dve", v3(pt_[:, 0, :], 128), v3(pt_[:, 0, :], 128),
                            amp[:].unsqueeze(1).to_broadcast([128, 4, 128]), ALU.mult, (ptb_, cb2), (ptb_,))
                    yield
                    ppv, ppvb = PS()
                    for hh in range(4):
                        o_ = ppv[:, hh * VW:(hh + 1) * VW]
                        if gb > 0:
                            MM(o_, pt_[:, 0, hh * 128:(hh + 1) * 128], vaug[:, b, g, :], True, False, (ptb_, vaugb), (ppvb,))
                        MM(o_, pt_[:, 1, hh * 128:(hh + 1) * 128], vaug[:, 1 + b, g, :], gb == 0, True, (ptb_, vaugb), (ppvb,))
                    pv3 = ppv[:, 0:4 * VW].rearrange("p (h d) -> p h d", d=VW)
                    sg_, sgb_ = sts[6]
                    TT_("dve", sg_[:, 0:4].unsqueeze(2), pv3[:, :, 64:65], sinkexp[:, 4 * g:4 * g + 4].unsqueeze(2), ALU.add,
                        (ppvb, cb2), (sgb_,))
                    P.add("dve", lambda e, sg_=sg_: e.reciprocal(out=sg_[:, 4:8], in_=sg_[:, 0:4]), (sgb_,), (sgb_,))
                    TT_("dve", v3(yatt[:, g * 256:(g + 1) * 256]), pv3[:, :, 0:64],
                        sg_[:, 4:8].unsqueeze(2).to_broadcast([128, 4, 64]), ALU.mult, (ppvb, sgb_), (yattb,))
                    PSrel(ppvb)
                    yield
                pt, ptb = PS()
                pv = pt[:].bitcast(BF16)
                for c in range(4):
                    TR(pv[:, c * 128:(c + 1) * 128], yatt[:, c * 128:(c + 1) * 128], identb[:], (yattb, cb2), (ptb,))
                CP("act", yaT[:, :, b * 128:(b + 1) * 128], pv[:, 0:512].rearrange("p (c n) -> p c n", n=128), (ptb,), (yaTb,))
                PSrel(ptb)
                yield

            def rwkv_block(b):
                t0 = b * 128
                r_b, k0_b, v_b = rkv[0][0][:, b, :], rkv[1][0][:, b, :], rkv[2][0][:, b, :]
                rB, kB, vB = rkv[0][1], rkv[1][1], rkv[2][1]
                (f_tw, f_twb), (f_al, f_alb), (f_gam, f_gamb), (f_ig, f_igb), (f_gx, f_gxb), (f_kk, f_kkb), \
                    (f_km, f_kmb), (f_s1, f_s1b), (f_s2, f_s2b), (f_yc, f_ycb) = tmpf
                (b_kt, b_ktb), (b_bt, b_btb), (b_at, b_atb), (b_rt, b_rtb), (b_X, b_Xb), (b_U, b_Ub), (b_y, b_yb) = tmpb
                (st_a, st_ab), (st_b, st_bb), (st_c, st_cb), (st_d, st_db), (st_e, st_eb), (st_f, st_fb), _ = sts
                p1, p1b = PS()
                MM(p1[:], lT[0:66, t0:t0 + 128], rhs_w[0:66, :], True, True, (lTb, cb2), (p1b,))
                ACT(f_tw[:], p1[:], AF.Tanh, (p1b,), (f_twb,), scale=0.5)
                PSrel(p1b)
                p2, p2b = PS()
                MM(p2[:], lT[0:66, t0:t0 + 128], rhs_a[0:66, :], True, True, (lTb, cb2), (p2b,))
                ACT(f_s1[:], p2[:], AF.Tanh, (p2b,), (f_s1b,), scale=0.5)
                PSrel(p2b)
                yield
                TS("dve", f_al[:], f_s1[:], 0.5, 0.5, ALU.mult, ALU.add, (f_s1b,), (f_alb,))
                p3, p3b = PS()
                MM(p3[:], tri_i[:], f_tw[:], True, True, (cb, f_twb), (p3b,))
                p4, p4b = PS()
                MM(p4[:], tri_s[:], f_tw[:], True, True, (cb, f_twb), (p4b,))
                ACT(f_gam[:], p3[:], AF.Exp, (p3b, cb), (f_gamb,), scale=-0.5 * C0, bias=tbias[:, 0:1])
                ACT(f_ig[:], p3[:], AF.Exp, (p3b, cb), (f_igb,), scale=0.5 * C0, bias=tbias[:, 1:2])
                ACT(f_gx[:], p4[:], AF.Exp, (p4b, cb), (f_gxb,), scale=-0.5 * C0, bias=tbias[:, 2:3])
                PSrel(p3b, p4b)
                p5, p5b = PS()
                for p_ in range(4):
                    MM(p5[:, p_:p_ + 1], f_tw[:, p_ * 128:(p_ + 1) * 128], onesc[:, 0:1], True, True, (f_twb, cb2), (p5b,))
                ACT(gamC[:, 0:4], p5[:, 0:4], AF.Exp, (p5b, cb), (gamCb,), scale=-0.5 * C0, bias=tbias[:, 3:4])
                PSrel(p5b)
                yield
                TT_("dve", f_kk[:], k0_b, kkb[:], ALU.mult, (kB, cb), (f_kkb,))
                ACT(f_s1[:], f_kk[:], AF.Square, (f_kkb,), (f_s1b,))
                RED(st_a[:, 0:8], v3(f_s1[:]), (f_s1b,), (st_ab,))
                TS("dve", st_a[:, 8:16], st_a[:, 0:8], 1e-24, None, ALU.max, None, (st_ab,), (st_ab,))
                TT_("pool", st_b[:, 0:8], st_a[:, 8:16], nhalf[:, 0:8], ALU.pow, (st_ab, cb2), (st_bb,))
                drip(4)
                TT_("dve", v3(f_kk[:]), v3(f_kk[:]), bc8(st_b[:, 0:8]), ALU.mult, (f_kkb, st_bb), (f_kkb,))
                STT(f_s2[:], f_al[:], -1.0, kab[:], ALU.add, ALU.mult, (f_alb, cb), (f_s2b,))
                STT(f_km[:], f_s2[:], 1.0, k0_b, ALU.add, ALU.mult, (f_s2b, kB), (f_kmb,))
                TT_("dve", b_kt[:], f_km[:], f_ig[:], ALU.mult, (f_kmb, f_igb), (b_ktb,))
                TT_("dve", f_s2[:], f_kk[:], f_al[:], ALU.mult, (f_kkb, f_alb), (f_s2b,))
                TT_("dve", b_bt[:], f_s2[:], f_ig[:], ALU.mult, (f_s2b, f_igb), (b_btb,))
                STT(b_at[:], f_kk[:], -1.0, f_gx[:], ALU.mult, ALU.mult, (f_kkb, f_gxb), (b_atb,))
                TT_("dve", b_rt[:], r_b, f_gam[:], ALU.mult, (rB, f_gamb), (b_rtb,))
                TT_("dve", f_s1[:], r_b, f_km[:], ALU.mult, (rB, f_kmb), (f_s1b,))
                TT_("dve", f_s1[:], f_s1[:], rkb[:], ALU.mult, (f_s1b, cb), (f_s1b,))
                RED(st_c[:, 0:8], v3(f_s1[:]), (f_s1b,), (st_cb,))
                yield
                for (xa, xab, xr, xrb, dst, dstb) in ((b_at, b_atb, b_rt, b_rtb, arT, arTb), (b_bt, b_btb, b_kt, b_ktb, bkT, bkTb)):
                    pt, ptb = PS()
                    pv = pt[:].bitcast(BF16)
                    for p_ in range(4):
                        TR(pv[:, p_ * 128:(p_ + 1) * 128], xa[:, p_ * 128:(p_ + 1) * 128], identb[:], (xab, cb2), (ptb,))
                    for p_ in range(4):
                        TR(pv[:, 512 + p_ * 128:512 + (p_ + 1) * 128], xr[:, p_ * 128:(p_ + 1) * 128], identb[:], (xrb, cb2), (ptb,))
                    pv4 = pv.rearrange("d (a p t) -> d a p t", a=2, p=4)
                    CP("act", dst[:].rearrange("d p a t -> d a p t"), pv4, (ptb,), (dstb,))
                    for q_ in range(2):
                        rows = slice(q_ * 64, q_ * 64 + 64)
                        if dst is arT:
                            CP("dve", arZ[rows, :, q_, :, :].rearrange("d p a t -> d a p t"), pv4[rows], (ptb,), (arZb[q_],))
                        else:
                            CP("dve", bZ[rows, :, q_, :], pv4[rows, 0, :, :], (ptb,), (bZb[q_],))
                    PSrel(ptb)
                yield
                for p_ in range(4):
                    rhs = arZ[:, p_, :, :, :].rearrange("k q a t -> k (q a t)")
                    for (x_, SCx) in ((0, SCb), (1, SCk)):
                        psc, pscb = PS()
                        MM(psc[:], bkT[:, p_, x_, :], rhs, True, True, (bkTb,) + arZb, (pscb,))
                        TT_("dve", SCx[p_][0][:].rearrange("s q a t -> s (q a t)"), psc[:], mask4[:],
                            ALU.mult, (pscb, cb2), (SCx[p_][1],))
                        PSrel(pscb)
                    if p_ % 2 == 1:
                        yield
                for hg in range(2):
                    pn, pnb = PS()
                    for pp_ in range(2):
                        p_ = hg * 2 + pp_
                        MM(pn[:, pp_ * 256:(pp_ + 1) * 256], arT[:, p_, 0, :], bZ[:, p_, :, :].rearrange("k q s -> k (q s)"),
                           True, True, (arTb,) + bZb, (pnb,))
                    TT_("dve", PTm[0][hg][0][:], v3(pn[:], 128), maskl[:].unsqueeze(1).to_broadcast([128, 4, 128]), ALU.mult,
                        (pnb, cb2), (PTm[0][hg][1],))
                    PSrel(pnb)
                    for pp_ in range(2):
                        p_ = hg * 2 + pp_
                        TT_("dve", Qm[0][hg][0][:, 2 * pp_:2 * pp_ + 2, :], SCb[p_][0][:, :, 0, :],
                            identb[:].unsqueeze(1).to_broadcast([128, 2, 128]), ALU.add, (SCb[p_][1], cb2), (Qm[0][hg][1],))
                for l in range(7):
                    for hg in range(2):
                        def Pl(hh):
                            if l == 0:
                                h = hg * 4 + hh
                                return SCb[h // 2][0][:, h % 2, 0, :], SCb[h // 2][1]
                            return Pm[l % 2][hg][0][:, hh, :], Pm[l % 2][hg][1]
                        PTl, PTlb = PTm[l % 2][hg]
                        if l <= 4:
                            pP, pPb = PS()
                            for hh in range(4):
                                ap, bf = Pl(hh)
                                MM(pP[:, hh * 128:(hh + 1) * 128], PTl[:, hh, :], ap, True, True, (PTlb, bf), (pPb,))
                        if l <= 5:
                            pT_, pTb_ = PS()
                            for hh in range(4):
                                ap, bf = Pl(hh)
                                MM(pT_[:, hh * 128:(hh + 1) * 128], ap, PTl[:, hh, :], True, True, (bf, PTlb), (pTb_,))
                        if l >= 1:
                            pQ, pQb = PS()
                            Qp, Qpb = Qm[(l - 1) % 2][hg]
                            for hh in range(4):
                                MM(pQ[:, hh * 128:(hh + 1) * 128], PTl[:, hh, :], Qp[:, hh, :], True, True, (PTlb, Qpb), (pQb,))
                        if l <= 4:
                            CP("act", Pm[(l + 1) % 2][hg][0][:], v3(pP[:], 128), (pPb,), (Pm[(l + 1) % 2][hg][1],))
                            PSrel(pPb)
                        if l <= 5:
                            CP("act", PTm[(l + 1) % 2][hg][0][:], v3(pT_[:], 128), (pTb_,), (PTm[(l + 1) % 2][hg][1],))
                            PSrel(pTb_)
                        if l >= 1:
                            TT_("dve", Qm[l % 2][hg][0][:], v3(pQ[:], 128), Qp[:], ALU.add, (pQb, Qpb), (Qm[l % 2][hg][1],))
                            PSrel(pQb)
                        yield
                def hd(h):
                    return h // 2, h % 2, slice(h * 64, (h + 1) * 64)
                pX, pXb = PS()
                for p_ in range(4):
                    MM(pX[:, p_ * 128:(p_ + 1) * 128], arT[:, p_, 0, :], Hz[:, p_, :], True, False, (arTb,) + Hzb, (pXb,))
                    for h in (2 * p_, 2 * p_ + 1):
                        _, q_, cs = hd(h)
                        MM(pX[:, cs], SCk[p_][0][:, q_, 0, :], v_b[:, cs], False, h % 2 == 1, (SCk[p_][1], vB), (pXb,))
                CP("act", b_X[:], pX[:], (pXb,), (b_Xb,))
                PSrel(pXb)
                yield
                pU, pUb = PS()
                for h in range(8):
                    p_, q_, cs = hd(h)
                    MM(pU[:, cs], Qm[0][h // 4][0][:, h % 4, :], b_X[:, cs], True, True, (Qm[0][h // 4][1], b_Xb), (pUb,))
                CP("dve", b_U[:], pU[:], (pUb,), (b_Ub,))
                PSrel(pUb)
                yield
                pY, pYb = PS()
                for p_ in range(4):
                    MM(pY[:, p_ * 128:(p_ + 1) * 128], arT[:, p_, 1, :], Hz[:, p_, :], True, False, (arTb,) + Hzb, (pYb,))
                    for h in (2 * p_, 2 * p_ + 1):
                        _, q_, cs = hd(h)
                        MM(pY[:, cs], SCb[p_][0][:, q_, 1, :], b_U[:, cs], False, False, (SCb[p_][1], b_Ub), (pYb,))
                        MM(pY[:, cs], SCk[p_][0][:, q_, 1, :], v_b[:, cs], False, h % 2 == 1, (SCk[p_][1], vB), (pYb,))
                pD, pDb = PS()
                for p_ in range(4):
                    ps_ = slice(p_ * 128, (p_ + 1) * 128)
                    MM(pD[:, ps_], b_bt[:, ps_], b_U[:, ps_], True, False, (b_btb, b_Ub), (pDb,))
                    MM(pD[:, ps_], b_kt[:, ps_], v_b[:, ps_], False, True, (b_ktb, vB), (pDb,))
                for q_ in range(2):
                    rows = slice(q_ * 64, q_ * 64 + 64)
                    TT_("dve", v3(Hf[rows, :]), v3(pD[rows, :], 128)[:, :, q_ * 64:(q_ + 1) * 64], v3(Hf[rows, :]), ALU.add,
                        (pDb, Hfb[q_]), (Hfb[q_],))
                TT_("dve", v3(Hf[:]), v3(Hf[:]), gamC[:, 0:4].unsqueeze(2).to_broadcast([128, 4, 64]), ALU.mult,
                    Hfb + (gamCb,), Hfb)
                PSrel(pDb)
                for q_ in range(2):
                    rows = slice(q_ * 64, q_ * 64 + 64)
                    CP("act", Hz[rows, :, q_ * 64:(q_ + 1) * 64], v3(Hf[rows, :]), (Hfb[q_],), (Hzb[q_],))
                yield
                RED(st_d[:, 0:8], v3(pY[:]), (pYb,), (st_db,))
                TS("dve", st_d[:, 8:16], st_d[:, 0:8], -1.0 / 64, None, ALU.mult, None, (st_db,), (st_db,))
                TT_("dve", v3(f_yc[:]), v3(pY[:]), bc8(st_d[:, 8:16]), ALU.add, (pYb, st_db), (f_ycb,))
                PSrel(pYb)
                ACT(f_s1[:], f_yc[:], AF.Square, (f_ycb,), (f_s1b,))
                RED(st_e[:, 0:8], v3(f_s1[:]), (f_s1b,), (st_eb,))
                TS("dve", st_e[:, 8:16], st_e[:, 0:8], 1.0 / 64, GN_EPS, ALU.mult, ALU.add, (st_eb,), (st_eb,))
                TT_("pool", st_f[:, 0:8], st_e[:, 8:16], nhalf[:, 0:8], ALU.pow, (st_eb, cb2), (st_fb,))
                drip(4)
                TT_("dve", v3(f_yc[:]), v3(f_yc[:]), bc8(st_f[:, 0:8]), ALU.mult, (f_ycb, st_fb), (f_ycb,))
                TT_("dve", f_yc[:], f_yc[:], lnwb[:], ALU.mult, (f_ycb, cb), (f_ycb,))
                TT_("dve", f_yc[:], f_yc[:], lnbb[:], ALU.add, (f_ycb, cb), (f_ycb,))
                TT_("dve", v3(f_s1[:]), v3(v_b), bc8(st_c[:, 0:8]), ALU.mult, (vB, st_cb), (f_s1b,))
                TT_("dve", f_yc[:], f_yc[:], f_s1[:], ALU.add, (f_ycb, f_s1b), (f_ycb,))
                pG, pGb = PS()
                MM(pG[:], gTt[0:96, t0:t0 + 128], gl_t[0:96, :], True, True, (gTb, cb2), (pGb,))
                TT_("dve", b_y[:], f_yc[:], pG[:], ALU.mult, (f_ycb, pGb), (b_yb,))
                PSrel(pGb)
                yield
                pt, ptb = PS()
                pv = pt[:].bitcast(BF16)
                for c in range(4):
                    TR(pv[:, c * 128:(c + 1) * 128], b_y[:, c * 128:(c + 1) * 128], identb[:], (b_yb, cb2), (ptb,))
                CP("act", yrT[:, :, t0:t0 + 128], pv[:, 0:512].rearrange("p (c n) -> p c n", n=128), (ptb,), (yrTb,))
                PSrel(ptb)
                yield

            def mixer_pre(ti, xt, xtb):
                run_all(norm_T(xt, xtb, hT, hTb, 8, gmc))
                CP("dve", hT[:, :, 7:8], hcar[:], (hcarb,), (hTb,))
                CP("dve", hcar[:], hT[:, :, 519:520], (hTb,), (hcarb,))
                rt, rb_ = wget(("lora",))
                w = rt[:, 0:1024].rearrange("p (k m) -> p k m", m=64)
                pa, pab = PS()
                for kc in range(16):
                    MM(pa[0:64, :], w[:, kc, :], rhs_k(kc), kc == 0, kc == 15, (rb_, hTb), (pab,))
                ACT(lT[0:32, :], pa[0:32, :], AF.Copy, (pab,), (lTb,))
                ACT(lT[32:64, :], pa[32:64, :], AF.Tanh, (pab,), (lTb,))
                PSrel(pab)
                rt, rb_ = wget(("xg",))
                w = rt[:, 0:1536].rearrange("p (k m) -> p k m", m=96)
                pg, pgb = PS()
                for kc in range(16):
                    MM(pg[0:96, :], w[:, kc, :], rhs_k(kc), kc == 0, kc == 15, (rb_, hTb), (pgb,))
                ACT(tmpf[0][0][0:96, :], pg[0:96, :], AF.Tanh, (pgb,), (tmpf[0][1],), scale=0.5)
                PSrel(pgb)
                TS("dve", gTt[:], tmpf[0][0][0:96, :], 1.0, None, ALU.add, None, (tmpf[0][1],), (gTb,))
                for m in range(4):
                    rt, rb_ = wget(("q", m))
                    w = rt[:, 0:1024].rearrange("p (k m) -> p k m", m=128)
                    pq, pqb = PS()
                    for k in range(8):
                        MM(pq[:], w[:, k, :], rhs_k(k), k == 0, k == 7, (rb_, hTb), (pqb,))
                    qk_norm(pq, pqb, gqc[:, 0:1], qT[:, m, :], qTb)
                CP("pool", kTd[:, :, :, 0:128], kTd[:, :, :, 512:640], (kTdb,), (kTdb,))
                for g in range(2):
                    rt, rb_ = wget(("kd", g))
                    w = rt[:, 0:1024].rearrange("p (k m) -> p k m", m=128)
                    pq, pqb = PS()
                    for k in range(8):
                        MM(pq[:], w[:, k, :], rhs_k(k), k == 0, k == 7, (rb_, hTb), (pqb,))
                    qk_norm(pq, pqb, gk8[:, 0:1], None, kTdb,
                            dsts=[(slice(0, 64), kTd[0:64, g, 0, 128:640]), (slice(64, 128), kTd[64:128, g, 1, 128:640])])
                for c in range(3):
                    pbs = [PS() for _ in range(4)]
                    for q in range(4):
                        rt, rb_ = wget(("rkv", c, q))
                        w = rt[:].rearrange("p (k n) -> p k n", n=512)
                        for kk in range(4):
                            kc = q * 4 + kk
                            for b in range(4):
                                MM(pbs[b][0][:], rhs_k(kc, b * 128, 128), w[:, kk, :], kc == 0, kc == 15, (hTb, rb_), (pbs[b][1],))
                    for b in range(4):
                        CP("act" if b % 2 else "dve", rkv[c][0][:, b, :], pbs[b][0][:], (pbs[b][1],), (rkv[c][1],))
                        PSrel(pbs[b][1])
                rt, rb_ = wget(("av",))
                w = rt[:, 0:1024].rearrange("p (k n) -> p k n", n=128)
                pv_, pvb_ = PS()
                for b in range(4):
                    for k in range(8):
                        MM(pv_[:, b * 128:(b + 1) * 128], rhs_k(k, b * 128, 128), w[:, k, :], k == 0, k == 7, (hTb, rb_), (pvb_,))
                CP("pool", vaug[:, 0, :, :], vaug[:, 4, :, :], (vaugb,), (vaugb,))
                CP("act", vaug[:, 1:5, :, 0:64], pv_[:].rearrange("p (b g d) -> p b g d", b=4, g=2), (pvb_,), (vaugb,))
                PSrel(pvb_)

            def mixer_blocks(ti):
                for b in range(4):
                    if "attn" not in skip:
                        yield from tagged(attn_block(b, ti * 4 + b), "attn")
                    if "rwkv" not in skip:
                        yield from tagged(rwkv_block(b), "rwkv")

            def mixer_post(ti, xt, xtb):
                for m in range(8):
                    rt, rb_ = wget(("gate", m))
                    gw = rt[:].rearrange("p (t k n) -> p t k n", t=2, k=8)
                    pgr, pgrb = PS()
                    pga, pgab = PS()
                    for k in range(8):
                        MM(pgr[:], gw[:, 0, k, :], rhs_k(k), k == 0, k == 7, (rb_, hTb), (pgrb,))
                    for k in range(8):
                        MM(pga[:], gw[:, 1, k, :], rhs_k(k), k == 0, k == 7, (rb_, hTb), (pgab,))
                    rt2, rb2 = wget(("br", m))
                    bw = rt2[:, 0:1024].rearrange("p (t c n) -> p t c n", t=2, c=4)
                    pbr, pbrb = PS()
                    pba, pbab = PS()
                    for c in range(4):
                        MM(pbr[:], bw[:, 0, c, :], yrT[:, c, :], c == 0, c == 3, (rb2, yrTb), (pbrb,))
                    for c in range(4):
                        MM(pba[:], bw[:, 1, c, :], yaT[:, c, :], c == 0, c == 3, (rb2, yaTb), (pbab,))
                    (fa, fab), (fb, fbb), (fc, fcb), (fd, fdb) = tmpf[0:4]
                    ACT(fa[:], pgr[:], AF.Tanh, (pgrb,), (fab,), scale=0.5)
                    ACT(fb[:], pga[:], AF.Tanh, (pgab,), (fbb,), scale=0.5)
                    STT(fc[:], fa[:], 1.0, pbr[:], ALU.add, ALU.mult, (fab, pbrb), (fcb,))
                    STT(fd[:], fb[:], 1.0, pba[:], ALU.add, ALU.mult, (fbb, pbab), (fdb,))
                    PSrel(pgrb, pgab, pbrb, pbab)
                    TT_("dve", mgT[:, m, :], fc[:], fd[:], ALU.add, (fcb, fdb), (mgTb,))
                for h in range(2):
                    pbs = [PS() for _ in range(4)]
                    for q in range(2):
                        rt, rb_ = wget(("wo", h, q))
                        w = rt[:].rearrange("p (k n) -> p k n", n=512)
                        for mm in range(4):
                            m = q * 4 + mm
                            for s in range(4):
                                MM(pbs[s][0][:], mgT[:, m, s * 128:(s + 1) * 128], w[:, mm, :], m == 0, m == 7, (mgTb, rb_), (pbs[s][1],))
                    for s in range(4):
                        xs = xt[:, s, h * 512:(h + 1) * 512]
                        STT(xs, pbs[s][0][:], 0.5, xs, ALU.mult, ALU.add, (pbs[s][1], xtb), (xtb,))
                        PSrel(pbs[s][1])

        xv = dr["x"].rearrange("(t s p) d -> t p s d", s=4, p=128)
        yv = y_out.rearrange("(t s p) d -> t p s d", s=4, p=128)

        def finish(ti):
            xt, xtb = xb[ti % 3]
            rms_stats(xt, xtb)
            for s in range(4):
                STT(xt[:, s, :], xt[:, s, :], rstd[:, s:s + 1], fgain[:], ALU.mult, ALU.mult, (xtb, rstdb, cb), (xtb,))
            DMA("act", yv[ti], xt[:], (xtb,), (), xtb)
            yield

        def side_stream(ti):
            if ti >= 1:
                if do_ffn2:
                    yield from tagged(ffn(1, xb[(ti - 1) % 3][0], xb[(ti - 1) % 3][1]), "ffn2")
                yield from tagged(finish(ti - 1), "fin")
            if ti + 1 < NT and do_ffn1:
                yield from tagged(ffn(0, xb[(ti + 1) % 3][0], xb[(ti + 1) % 3][1]), "ffn1")

        DMA("sp", xb[0][0][:], xv[0], (), (xb[0][1],), xb[0][1])
        interleave(tagged(ffn(0, xb[0][0], xb[0][1]), "ffn1") if do_ffn1 else iter(()),
                   tagged(staged_prep(), "prep") if do_mix else None, 0.3)
        P.barrier()
        staged_ready["v"] = True
        for ti in range(NT):
            xt, xtb = xb[ti % 3]
            if ti + 1 < NT:
                xn, xnb = xb[(ti + 1) % 3]
                DMA("sp", xn[:], xv[ti + 1], (), (xnb,), xnb)
            side = side_stream(ti)
            if do_mix:
                P.phase = "mix"
                mixer_pre(ti, xt, xtb)
                if overlap:
                    interleave(mixer_blocks(ti), side, ratio)
                else:
                    run_all(mixer_blocks(ti))
                    run_all(side)
                P.phase = "branch"
                mixer_post(ti, xt, xtb)
            else:
                run_all(side)
        if do_ffn2:
            run_all(tagged(ffn(1, xb[(NT - 1) % 3][0], xb[(NT - 1) % 3][1]), "ffn2"))
        run_all(tagged(finish(NT - 1), "fin"))

        if order is None:
            return None, None, rec
        P.resolve()
        P.prepare(nc, st)
        with nc.Block() as block:
            P.emit(block)
    return nc, P, None


def build(T=4096, **kw):
    _, _, rec = _build(T, order=None, **kw)
    nc, P, _ = _build(T, order=rec, **kw)
    return nc, P


def host_consts():
    s = np.arange(128)[:, None]
    t = np.arange(128)[None, :]
    tri_i = (s <= t).astype(np.float32)
    tri_s = (s < t).astype(np.float32)
    c = {
        "c_ident": np.eye(128, dtype=np.float32),
        "c_tri_i": tri_i,
        "c_tri_s": tri_s,
        "c_blk": ((s // 64) == (t // 64)).astype(np.float32),
        "c_mask4": np.concatenate([tri_s, tri_i, tri_s, tri_i], axis=1),
        "c_maskl": (s > t).astype(np.float32),
        "c_amo": tri_i.copy(),
        "c_amp": (s > t).astype(np.float32),
    }
    tt = np.arange(128, dtype=np.float64)
    c["c_tb"] = np.stack([-0.5 * C0 * (tt + 1), 0.5 * C0 * (tt + 1), -0.5 * C0 * tt,
                          np.full(128, -0.5 * C0 * 128)], axis=1).astype(np.float32)
    return c


def make_in_map(inp, xs):
    f = lambda a: np.ascontiguousarray(np.asarray(a, dtype=np.float32))
    col = lambda g: f(np.asarray(g).reshape(8, 128).T)
    m = {
        "x": f(xs),
        "w_g1": f(inp["ffn1_w_gate"][0]), "w_u1": f(inp["ffn1_w_up"][0]), "w_d1": f(inp["ffn1_w_down"][0]),
        "w_g2": f(inp["ffn2_w_gate"][0]), "w_u2": f(inp["ffn2_w_up"][0]), "w_d2": f(inp["ffn2_w_down"][0]),
        "w_in": f(inp["w_in"][0]),
        "w_br": f(inp["w_branch_rwkv"][0]), "w_ba": f(inp["w_branch_attn"][0]), "w_out": f(inp["w_out"][0]),
        "g1c": col(inp["ffn1_norm"][0]), "gmc": col(inp["mix_norm"][0]), "g2c": col(inp["ffn2_norm"][0]),
        "fgain": f(np.asarray(inp["final_norm"][0]).reshape(1, D)),
        "mu": f(np.asarray(inp["rwkv_mu"][0]).reshape(1, 1696)),
        "p_kk": f(np.asarray(inp["rwkv_k_k"][0]).reshape(1, 512)),
        "p_ka": f(np.asarray(inp["rwkv_k_a"][0]).reshape(1, 512)),
        "p_rk": f(np.asarray(inp["rwkv_r_k"][0]).reshape(1, 512)),
        "p_lnw": f(np.asarray(inp["rwkv_ln_w"][0]).reshape(1, 512)),
        "p_lnb": f(np.asarray(inp["rwkv_ln_b"][0]).reshape(1, 512)),
        "p_w0": f(np.asarray(inp["rwkv_w0"][0]).reshape(1, 512)),
        "p_a0": f(np.asarray(inp["rwkv_a0"][0]).reshape(1, 512)),
        "p_wl": f(inp["rwkv_w_lora_up"][0]), "p_al": f(inp["rwkv_a_lora_up"][0]), "p_gl": f(inp["rwkv_g_lora_up"][0]),
        "gqc": f(np.tile(np.asarray(inp["attn_q_norm"][0]).reshape(64), 2).reshape(128, 1)),
        "gkc": f(np.tile(np.asarray(inp["attn_k_norm"][0]).reshape(64), 2).reshape(128, 1)),
        "sinks": f(np.asarray(inp["attn_sinks"][0]).reshape(1, 8)),
    }
    m.update(host_consts())
    return m


_CACHE = {}


def kernel(**inputs):
    x = np.asarray(inputs["x"], dtype=np.float32)
    B, T, _ = x.shape
    key = (T,)
    if key not in _CACHE:
        _CACHE[key] = build(T)[0]
    nc = _CACHE[key]
    in_maps = [make_in_map(inputs, x[b]) for b in range(B)]
    res = run_bass_kernel_spmd(nc, in_maps, core_ids=list(range(B)))
    return np.stack([np.asarray(r["y"], dtype=np.float32) for r in res.results], axis=0)
```
